# Optimizing a Trainium2 kernel written in Bass

```python
import jax, jax.numpy as jnp
from jax import lax
import numpy as np

D_MODEL = 1024
BATCH = 4
SEQ = 8192
DEPTH = 2

N_MIXERS = 2
N_A = (DEPTH + 1) // 2
N_B = DEPTH // 2
N_META = 16
EXPAND = 2
E_CONV = EXPAND * D_MODEL
CONV_WIDTH = 3
E_MLSTM = EXPAND * D_MODEL
N_HEADS = 4
DV = E_MLSTM // N_HEADS
DK = DV // 2
QK = N_HEADS * DK
CHUNK = 64
RMS_EPS = 1e-6

kernel_name = "hybrid_shortconv_mlstm_interleaved"


def _rmsnorm(x, w):
    xf = x.astype(jnp.float32)
    y = xf * lax.rsqrt(jnp.mean(xf * xf, axis=-1, keepdims=True) + RMS_EPS)
    return (y * w.astype(jnp.float32)).astype(x.dtype)


def _short_conv_mixer(u, w_in, w_conv, w_out):
    proj = u @ w_in
    b_gate, c_gate, xin, z = jnp.split(proj, 4, axis=-1)
    y = lax.conv_general_dilated(
        c_gate * xin, w_conv[:, None, :].astype(u.dtype),
        window_strides=(1,), padding=[(CONV_WIDTH - 1, 0)],
        dimension_numbers=("NWC", "WIO", "NWC"), feature_group_count=E_CONV)
    return (jax.nn.silu(z) * b_gate * y) @ w_out


def _mlstm_chunk(carry, inp):
    c_state, n_state, m_state = carry
    q, k, v, logi, logf = inp
    L = q.shape[-2]
    b = jnp.cumsum(logf, axis=-1)
    causal = jnp.tril(jnp.ones((L, L), dtype=bool))
    log_d = jnp.where(causal, b[..., :, None] - b[..., None, :] + logi[..., None, :], -jnp.inf)
    log_inter = b + m_state[..., None]
    m_row = jnp.maximum(log_inter, jnp.max(log_d, axis=-1))
    d = jnp.exp(log_d - m_row[..., None])
    inter = jnp.exp(log_inter - m_row)
    s = jnp.einsum("bhtd,bhsd->bhts", q, k) * d
    num = jnp.einsum("bhts,bhsv->bhtv", s, v) + inter[..., None] * jnp.einsum("bhtd,bhdv->bhtv", q, c_state)
    den = jnp.sum(s, axis=-1) + inter * jnp.einsum("bhtd,bhd->bht", q, n_state)
    h = num / jnp.maximum(jnp.abs(den), jnp.exp(-m_row))[..., None]
    log_w = b[..., -1:] - b + logi
    m_new = jnp.maximum(b[..., -1] + m_state, jnp.max(log_w, axis=-1))
    decay = jnp.exp(b[..., -1] + m_state - m_new)
    w = jnp.exp(log_w - m_new[..., None])
    c_new = decay[..., None, None] * c_state + jnp.einsum("bhs,bhsd,bhsv->bhdv", w, k, v)
    n_new = decay[..., None] * n_state + jnp.einsum("bhs,bhsd->bhd", w, k)
    return (c_new, n_new, m_new), h


def _mlstm_mixer(u, w_in, gate_b, head_norm_w, w_out):
    bsz, T, _ = u.shape
    proj = u @ w_in
    q, k, v, o, z, g = jnp.split(
        proj, [QK, 2 * QK, 2 * QK + E_MLSTM, 2 * QK + 2 * E_MLSTM, 2 * QK + 3 * E_MLSTM], axis=-1)
    g = g.astype(jnp.float32) + gate_b.astype(jnp.float32)
    logi = jnp.transpose(g[..., :N_HEADS], (0, 2, 1))
    logf = jnp.transpose(jax.nn.log_sigmoid(g[..., N_HEADS:]), (0, 2, 1))

    def heads(a, dh):
        return a.astype(jnp.float32).reshape(bsz, T, N_HEADS, dh).transpose(0, 2, 1, 3)

    q = heads(q, DK) * (DK ** -0.5)
    k = heads(k, DK)
    v = heads(v, DV)
    seqs = (q, k, v, logi, logf)

    def chunks(a):
        real = a[:, :, N_META:]
        nc = real.shape[2] // CHUNK
        return jnp.moveaxis(real.reshape(bsz, N_HEADS, nc, CHUNK, *real.shape[3:]), 2, 0)

    carry0 = (jnp.zeros((bsz, N_HEADS, DK, DV), jnp.float32),
              jnp.zeros((bsz, N_HEADS, DK), jnp.float32),
              jnp.zeros((bsz, N_HEADS), jnp.float32))
    carry, h_meta = _mlstm_chunk(carry0, tuple(a[:, :, :N_META] for a in seqs))
    _, h_real = lax.scan(_mlstm_chunk, carry, tuple(chunks(a) for a in seqs))
    h_real = jnp.moveaxis(h_real, 0, 2).reshape(bsz, N_HEADS, T - N_META, DV)
    h = jnp.concatenate([h_meta, h_real], axis=2).transpose(0, 2, 1, 3)
    h = h * lax.rsqrt(jnp.mean(h * h, axis=-1, keepdims=True) + RMS_EPS)
    h = h * head_norm_w.astype(jnp.float32).reshape(N_HEADS, DV)
    h = h.reshape(bsz, T, E_MLSTM).astype(u.dtype) * jax.nn.sigmoid(o) * jax.nn.silu(z)
    return h @ w_out


def setup_inputs(seed: int = 0) -> dict:
    key = jax.random.key(seed)
    ks = jax.random.split(key, 12)
    f32 = jnp.float32
    x = jax.random.normal(ks[0], (BATCH, SEQ, D_MODEL), f32)
    meta_tokens = jax.random.normal(ks[1], (N_META, D_MODEL), f32)
    norm_w = 1.0 + 0.05 * jax.random.normal(ks[2], (DEPTH, D_MODEL), f32)
    conv_in_w = jax.random.normal(ks[3], (N_A, D_MODEL, 4 * E_CONV), f32) * D_MODEL ** -0.5
    conv_w = jax.random.normal(ks[4], (N_A, CONV_WIDTH, E_CONV), f32) * CONV_WIDTH ** -0.5
    conv_out_w = jax.random.normal(ks[5], (N_A, E_CONV, D_MODEL), f32) * E_CONV ** -0.5
    n_in = 2 * QK + 3 * E_MLSTM + 2 * N_HEADS
    mlstm_in_w = jax.random.normal(ks[6], (N_B, D_MODEL, n_in), f32) * D_MODEL ** -0.5
    i_bias = 0.1 * jax.random.normal(ks[7], (N_B, N_HEADS), f32)
    f_bias = jnp.linspace(3.0, 6.0, N_HEADS, dtype=f32)[None, :] + 0.1 * jax.random.normal(ks[8], (N_B, N_HEADS), f32)
    mlstm_gate_b = jnp.concatenate([i_bias, f_bias], axis=-1)
    mlstm_head_norm_w = 1.0 + 0.05 * jax.random.normal(ks[9], (N_B, E_MLSTM), f32)
    mlstm_out_w = jax.random.normal(ks[10], (N_B, E_MLSTM, D_MODEL), f32) * E_MLSTM ** -0.5
    final_norm_w = 1.0 + 0.05 * jax.random.normal(ks[11], (D_MODEL,), f32)
    return {"x": x, "meta_tokens": meta_tokens, "norm_w": norm_w, "conv_in_w": conv_in_w,
            "conv_w": conv_w, "conv_out_w": conv_out_w, "mlstm_in_w": mlstm_in_w,
            "mlstm_gate_b": mlstm_gate_b, "mlstm_head_norm_w": mlstm_head_norm_w,
            "mlstm_out_w": mlstm_out_w, "final_norm_w": final_norm_w}


def reference(x, meta_tokens, norm_w, conv_in_w, conv_w, conv_out_w, mlstm_in_w,
              mlstm_gate_b, mlstm_head_norm_w, mlstm_out_w, final_norm_w):
    bsz = x.shape[0]
    meta = jnp.broadcast_to(meta_tokens.astype(x.dtype)[None], (bsz, N_META, D_MODEL))
    h = jnp.concatenate([meta, x], axis=1)
    for i in range(DEPTH):
        u = _rmsnorm(h, norm_w[i])
        j = i // N_MIXERS
        if i % N_MIXERS == 0:
            h = h + _short_conv_mixer(u, conv_in_w[j], conv_w[j], conv_out_w[j])
        else:
            h = h + _mlstm_mixer(u, mlstm_in_w[j], mlstm_gate_b[j], mlstm_head_norm_w[j], mlstm_out_w[j])
    h = _rmsnorm(h, final_norm_w)
    return h[:, N_META:]
```

```python
import numpy as np
from contextlib import ExitStack
import concourse.bass as bass
import concourse.mybir as mybir
from concourse.bass_utils import run_bass_kernel_spmd

F32 = mybir.dt.float32
BF16 = mybir.dt.bfloat16
AF = mybir.ActivationFunctionType
ALU = mybir.AluOpType

D = 1024
E = 2048
NPRE = 16
TT = 512
H = 4
DK = 256
DV = 512
EPS = 1e-6
NRING = 3
DBG = 99
CASTBAR = False
GSUB = 99
RSLOT = 4096

P_ID, P_TRI, P_ONES = 0, 128, 256
P_NW = 384
P_CW = 400
P_GB = 448
P_FL = 456
P_PM = 458
P_FNW = 464
P_HNW = 464 + 1024
PCOLS = 464 + 1024 + 2048

ENGS = ("pe", "act", "dve", "pool", "sp")


class Buf:
    __slots__ = ("name", "w", "r", "excl")

    def __init__(self, name):
        self.name = name
        self.w = None
        self.r = []
        self.excl = name.startswith("ps")


class Op:
    __slots__ = ("eng", "fn", "deps", "raw", "pos", "is_dma", "key", "val", "inc", "signal", "count", "waits")


class Sched:
    def __init__(self):
        self.ops = {e: [] for e in ENGS}
        self.dma_cnt = {}

    def _rec(self, o, reads, writes):
        deps = []
        for b in reads:
            if b.w is not None:
                deps.append(b.w)
        o.raw = set(id(d) for d in deps)
        for b in reads:
            if b.excl:
                deps.extend(b.r)
        for b in writes:
            if b.w is not None:
                deps.append(b.w)
            deps.extend(b.r)
        o.deps = deps
        o.signal = False
        o.count = 0
        o.pos = len(self.ops[o.eng])
        self.ops[o.eng].append(o)
        for b in reads:
            if b.excl:
                b.w = o
                b.r = []
            else:
                b.r.append(o)
        for b in writes:
            b.w = o
            b.r = []
        return o

    def op(self, eng, fn, reads=(), writes=()):
        o = Op()
        o.eng, o.fn, o.is_dma, o.key, o.val = eng, fn, False, None, 0
        return self._rec(o, reads, writes)

    def dma(self, q, fn, key, reads=(), writes=(), inc=16):
        o = Op()
        o.eng, o.fn, o.is_dma, o.key = q, fn, True, key
        o.inc = inc
        o.val = self.dma_cnt.get(key, 0) + inc
        self.dma_cnt[key] = o.val
        return self._rec(o, reads, writes)

    def plan(self):
        for e in ENGS:
            seen_pos = {p: -1 for p in ENGS}
            seen_dma = {}
            for o in self.ops[e]:
                need_c, need_d = {}, {}
                for d in o.deps:
                    if d.is_dma:
                        if d.val > seen_dma.get(d.key, 0):
                            need_d[d.key] = max(need_d.get(d.key, 0), d.val)
                    else:
                        if d.eng == e and not o.is_dma:
                            if e == "pe" or id(d) not in o.raw:
                                continue
                        if d.pos > seen_pos[d.eng]:
                            if d.eng not in need_c or need_c[d.eng].pos < d.pos:
                                need_c[d.eng] = d
                waits = []
                for k, v in need_d.items():
                    seen_dma[k] = v
                    waits.append(("d", k, v))
                for pe, d in need_c.items():
                    seen_pos[pe] = d.pos
                    d.signal = True
                    waits.append(("c", d, None))
                o.waits = waits
        for e in ENGS:
            c = 0
            for o in self.ops[e]:
                if o.signal and not o.is_dma:
                    c += 1
                o.count = c

    def emit(self, block, eng_sems, dma_sems):
        self.plan()

        def run(e, eng):
            for o in self.ops[e]:
                for w in o.waits:
                    if w[0] == "d":
                        eng.wait_ge(dma_sems[w[1]], w[2])
                    else:
                        eng.wait_ge(eng_sems[w[1].eng], w[1].count)
                if o.fn is None:
                    continue
                inst = o.fn(eng)
                if o.is_dma:
                    inst.then_inc(dma_sems[o.key], o.inc)
                elif o.signal:
                    inst.then_inc(eng_sems[e], 1)

        @block.tensor
        def _(eng):
            run("pe", eng)

        @block.scalar
        def _(eng):
            run("act", eng)

        @block.vector
        def _(eng):
            run("dve", eng)

        @block.gpsimd
        def _(eng):
            run("pool", eng)

        @block.sync
        def _(eng):
            run("sp", eng)


def _wblk(h, s):
    if s in (1, 2):
        return h * 2 + (s - 1)
    return 2 * H + h * 3 + {0: 0, 3: 1, 4: 2}[s]


def _layout_weights(conv_in_w, conv_out_w, mlstm_in_w, mlstm_out_w):
    ci = conv_in_w[0].reshape(8, 128, 4, 16, 128)
    wcin = np.ascontiguousarray(ci.transpose(3, 1, 0, 2, 4)).reshape(16 * 128, RSLOT)
    wcout = np.ascontiguousarray(conv_out_w[0].reshape(16, 128, D).transpose(1, 0, 2)).reshape(128, 16 * D)
    wm = mlstm_in_w[0]
    wmin = np.zeros((H, 5, 128, RSLOT), np.float32)
    for h in range(H):
        q = wm[:, h * DK:(h + 1) * DK].reshape(8, 128, 2, 128)
        k = wm[:, D + h * DK:D + (h + 1) * DK].reshape(8, 128, 2, 128)
        qk = np.concatenate([q, k], axis=2)
        wmin[h, 0] = qk.transpose(1, 0, 2, 3).reshape(128, RSLOT)
        kt = wm[:, D + h * DK:D + (h + 1) * DK].reshape(8, 128, DK).transpose(1, 0, 2)
        wmin[h, 1, :, :8 * DK] = kt.reshape(128, 8 * DK)
        for s, base in ((2, 2 * D), (3, 2 * D + E), (4, 2 * D + 2 * E)):
            blk = wm[:, base + h * DV:base + (h + 1) * DV].reshape(8, 128, DV).transpose(1, 0, 2)
            wmin[h, s] = blk.reshape(128, RSLOT)
    wperm = np.zeros((H * 5, 128, RSLOT), np.float32)
    for h in range(H):
        for sl in range(5):
            wperm[_wblk(h, sl)] = wmin[h, sl]
    wmin = wperm.reshape(H * 5 * 128, RSLOT)
    wg = np.ascontiguousarray(wm[:, 2 * D + 3 * E:].reshape(8, 128, 8).transpose(1, 0, 2)).reshape(128, 64)
    wmout = np.ascontiguousarray(mlstm_out_w[0].reshape(16, 128, D).transpose(1, 0, 2)).reshape(128, 16 * D)
    return wcin, wcout, wmin, wg, wmout


def _layout_params(norm_w, conv_w, gate_b, head_norm_w, final_norm_w, pre_flag, state_flag):
    p = np.zeros((128, PCOLS), np.float32)
    p[:, P_ID:P_ID + 128] = np.eye(128, dtype=np.float32)
    p[:, P_TRI:P_TRI + 128] = np.triu(np.ones((128, 128), np.float32))
    p[:, P_ONES:P_ONES + 128] = 1.0
    for l in range(2):
        p[:, P_NW + 8 * l:P_NW + 8 * l + 8] = norm_w[l].reshape(8, 128).T
    for k in range(3):
        p[:, P_CW + 16 * k:P_CW + 16 * k + 16] = conv_w[0, k].reshape(16, 128).T
    p[:, P_GB:P_GB + 8] = gate_b[0][None, :]
    p[:, P_FL] = pre_flag
    p[:, P_FL + 1] = state_flag
    p[:NPRE, P_PM] = pre_flag
    p[:, P_FNW:P_FNW + D] = final_norm_w[None, :]
    p[:, P_HNW:P_HNW + E] = head_norm_w[0][None, :]
    return p


def _tiles(n_main):
    tiles = []
    for i in range(n_main):
        if i == 0:
            tiles.append(dict(t0=0, W=NPRE + TT, segs=[(0, NPRE), (NPRE, TT)],
                              subs=[(0, NPRE)] + [(NPRE + 128 * j, 128) for j in range(4)]))
        else:
            tiles.append(dict(t0=NPRE + TT * i, W=TT, segs=[(0, TT)],
                              subs=[(128 * j, 128) for j in range(4)]))
    return tiles


def build_program(mode, n_main=8, n_cores=8):
    do1 = mode in ("p1", "fused")
    do2 = mode in ("p2", "fused")
    ntok = NPRE + TT * n_main
    tiles = _tiles(n_main)
    nc = bass.Bass("TRN2", target_bir_lowering=False)
    S = Sched()
    es = ExitStack()
    bufs = {}

    def B(name):
        if name not in bufs:
            bufs[name] = Buf(name)
        return bufs[name]

    def dram(name, shape, dt, kind):
        return nc.dram_tensor(name, shape, dt, kind=kind).ap()

    def sb(name, shape, dt):
        return es.enter_context(nc.sbuf_tensor(name, shape, dt))

    params_d = dram("params", [128, PCOLS], F32, "ExternalInput")
    if do1:
        xin_d = dram("xin", [ntok, D], F32, "ExternalInput")
        wcin_d = dram("wcin", [16 * 128, RSLOT], F32, "ExternalInput")
        wcout_d = dram("wcout", [128, 16 * D], F32, "ExternalInput")
        wcin_b = dram("wcin_b", [16 * 128, RSLOT], BF16, "Internal")
        wcout_b = dram("wcout_b", [128, 16 * D], BF16, "Internal")
    wmin_d = dram("wmin", [H * 5 * 128, RSLOT], F32, "ExternalInput")
    wg_d = dram("wg", [128, 64], F32, "ExternalInput")
    wmin_b = dram("wmin_b", [H * 5 * 128, RSLOT], BF16, "Internal")
    if do2:
        wmout_d = dram("wmout", [128, 16 * D], F32, "ExternalInput")
        wmout_b = dram("wmout_b", [128, 16 * D], BF16, "Internal")
        out_d = dram("out", [ntok - NPRE, D], F32, "ExternalOutput")
    NST = H * 2 * DV + 8
    if mode == "p1":
        h1_d = dram("h1", [ntok, D], F32, "ExternalOutput")
        st_d = dram("st_out", [128, NST], F32, "ExternalOutput")
    elif mode == "p2":
        h1_d = dram("h1", [ntok, D], F32, "ExternalInput")
        st_d = dram("st_in", [128, NST], F32, "ExternalInput")
    else:
        h1_d = dram("h1", [ntok, D], F32, "Internal")
        kv_d = dram("kv_scr", [H, ntok, DK + DV], BF16, "Internal")
        NT_ = H * 2 * DV
        st_locT = nc.dram_tensor("st_locT", [128, NT_], F32).ap()
        st_locN = nc.dram_tensor("st_locN", [128, 8], F32).ap()
        st_allT = nc.dram_tensor("st_allT", [256, NT_], F32).ap()
        st_allN = nc.dram_tensor("st_allN", [256, 8], F32).ap()

    prm = sb("prm", [128, PCOLS], F32)
    idb = sb("idb", [128, 128], BF16)
    xt = [sb(f"xt{j}", [128, D], F32) for j in range(5)]
    junk = sb("junk", [128, D], BF16)
    xs0 = sb("xs0", [128, D], BF16)
    xs = [xs0, xs0]
    stat = sb("stat", [128, 64], F32)
    uT = sb("uT", [128, 8, NPRE + TT], BF16)
    u1T = sb("u1T", [128, 8, NPRE + TT], BF16)
    gT = sb("gT", [128, 16, NPRE + TT], BF16)
    ring = [sb(f"ring{k}", [128, RSLOT], BF16) for k in range(NRING)]
    wout = sb("wout", [128, 16, D], BF16)
    wgs = sb("wgs", [128, 8, 8], BF16)
    tA = sb("tA", [128, TT], F32)
    tB = sb("tB", [128, TT], F32)
    if do1:
        cx = sb("cx", [128, NPRE + TT + 2], F32)
        bg = sb("bg", [128, NPRE + TT], F32)
        yv = sb("yv", [128, NPRE + TT], F32)
        hal = sb("hal", [128, 16, 2], F32)
    nset = 2 if do2 else 1
    kh2 = [[sb(f"kh{st}_{j}", [128, DK], BF16) for j in range(5)] for st in range(nset)]
    vh2 = [[sb(f"vh{st}_{j}", [128, DV], BF16) for j in range(5)] for st in range(nset)]
    k_h, v_h = kh2[0], vh2[0]
    vp = [sb(f"vp{k}", [128, DV], BF16) for k in range(2)]
    gt = sb("gt", [128, 5, 8], F32)
    nlf = sb("nlf", [128, 5, 4], F32)
    gtmp = sb("gtmp", [128, 5, 4], F32)
    ee = sb("ee", [128, 5, 4], F32)
    eeb = sb("eeb", [128, 5, 4], BF16)
    av = sb("av", [128, 5, 4], F32)
    dec = sb("dec", [128, 11, 4], F32)
    Tst = sb("Tst", [128, H, 2, DV], F32)
    Tn = sb("Tn", [128, 8], F32)
    if do2:
        qT2 = [uT[:, 4 * st:4 * st + 2, :] for st in range(2)]
        kT2 = [uT[:, 4 * st + 2:4 * st + 4, :] for st in range(2)]
        p2buf = sb("p2buf", [128, 32 * DV], BF16)
        pv = lambda i: p2buf[:, i * DV:(i + 1) * DV]
        tw2 = [[pv(st * 5 + j) for j in range(5)] for st in range(2)]
        wgh2 = [[pv(10 + st * 5 + j) for j in range(5)] for st in range(2)]
        smT = [sb(f"smT{k}", [128, 128], BF16) for k in range(2)]
        hg = [pv(25 + k) for k in range(2)]
        stg = [p2buf[:, g * 8192:(g + 1) * 8192].bitcast(F32) for g in range(2)]
        Sbf = sb("Sbf", [128, 2, DV], BF16)
        nbf = sb("nbf", [128, 8], BF16)
        yout = p2buf[:, 28 * DV:32 * DV].bitcast(F32)
        sstat = sb("sstat", [128, 16], F32)
        gctr = [0]
        if do1:
            cxb, bgb = cx[:].bitcast(BF16), bg[:].bitcast(BF16)
            tG = [cxb[:, 0:DV], cxb[:, DV:2 * DV], bgb[:, 0:DV], bgb[:, DV:2 * DV]]
        else:
            tG = [sb(f"tG{k}", [128, DV], BF16) for k in range(4)]
        bst = sb("bst", [128, 64], F32)
        numS = [pv(20 + j) for j in range(5)]
    ps = [es.enter_context(nc.psum_tensor(f"ps{i}", [128, 512], F32)) for i in range(8)]
    eng_sems = {e: es.enter_context(nc.semaphore(f"s_{e}")) for e in ENGS}

    bank_ctr = [0]
    nbig = [6]

    def bank():
        i = bank_ctr[0] % nbig[0]
        bank_ctr[0] += 1
        return ps[i], B(f"ps{i}")

    def bank_fixed(i):
        return ps[i], B(f"ps{i}")

    small = ps[6]
    small2 = ps[7]

    ident = prm[:, P_ID:P_ID + 128]
    tri = prm[:, P_TRI:P_TRI + 128]
    ones = prm[:, P_ONES:P_ONES + 128]

    dma_keys = set()

    def dma(q, out, in_, key, reads, writes, **kw):
        dma_keys.add(key)
        return S.dma(q, lambda e: e.dma_start(out=out, in_=in_, **kw), key, reads=reads, writes=writes)

    dma("sp", prm[:], params_d, "prm", [], [B("prm")])
    S.op("dve", lambda e: e.tensor_copy(out=idb[:], in_=ident), [B("prm")], [B("idb")])
    dma("pool", wgs[:].rearrange("p c g -> p (c g)"), wg_d, "wgs", [], [B("wgs")])

    def cast_dram(src, dst, rows, name, r0=0, r1=None, step=512):
        s2 = src.rearrange("r (a k) -> (r a) k", k=2048)
        d2 = dst.rearrange("r (a k) -> (r a) k", k=2048)
        per = src.shape[1] // 2048
        chunks = []
        r1 = rows if r1 is None else r1
        for lo in range(r0, r1, step):
            hi = min(r1, lo + step)
            b = B(f"{name}_c{lo}")
            dma("pool", d2[lo * per:hi * per, :], s2[lo * per:hi * per, :], f"{name}_c{lo}", [], [b])
            chunks.append((lo, hi, b))
        return chunks

    def chunk_bufs(chunks, lo, hi):
        return [b for (a, z, b) in chunks if a < hi and z > lo]

    STAGED = mode == "fused"
    cin_chunks, cout_chunks, min_chunks = [], [], []
    if do1 and not STAGED:
        cin_chunks = cast_dram(wcin_d, wcin_b, 16 * 128, "wcin", 0, 256, step=128)
        cin_chunks += cast_dram(wcin_d, wcin_b, 16 * 128, "wcin", 256, 2048, step=256)
        cout_chunks = cast_dram(wcout_d, wcout_b, 128, "wcout")
    if not STAGED:
        min_chunks = cast_dram(wmin_d, wmin_b, H * 5 * 128, "wmin", 0, 2 * H * 128, step=256)
    def late_casts():
        min_chunks.extend(cast_dram(wmin_d, wmin_b, H * 5 * 128, "wmin", 2 * H * 128, H * 5 * 128, step=384))
        if do2:
            mout_chunks.extend(cast_dram(wmout_d, wmout_b, 128, "wmout"))
    mout_chunks = []
    if not STAGED:
        late_casts()

    plan = []
    for ti in range(n_main):
        if do1:
            for fc in range(16):
                plan.append(("cin", wcin_b, fc * 128, cin_chunks))
            for h in range(H):
                for s in (1, 2):
                    plan.append((f"m{s}", wmin_b, _wblk(h, s) * 128, min_chunks))
    n_p1 = len(plan)
    if do2:
        for ti in range(n_main):
            for h in range(H):
                for s in ((0, 3, 4) if mode == "fused" else range(5)):
                    plan.append((f"m{s}", wmin_b, _wblk(h, s) * 128, min_chunks))
    issued = [0]
    taken = [0]

    def staged_cast(dst_flat, g, dbuf):
        cuts = [(0, 1024, "pool"), (1024, 2560, "act"), (2560, 4096, "dve")]
        for a, b_, eng in cuts:
            if eng == "act":
                S.op("act", lambda e, a=a, b_=b_: e.activation(out=dst_flat[:, a:b_], in_=stg[g][:, a:b_],
                                                               func=AF.Copy), [B(f"stg{g}")], [dbuf])
            else:
                S.op(eng, lambda e, a=a, b_=b_: e.tensor_copy(out=dst_flat[:, a:b_], in_=stg[g][:, a:b_]),
                     [B(f"stg{g}")], [dbuf])

    n_stage = (16 + 2 * H) if (STAGED and do1) else 0
    wbmap = {}
    stgc = [0]

    def ring_issue(upto):
        while issued[0] < min(upto, len(plan)):
            k = issued[0]
            name, src, lo, chunks = plan[k]
            slot = k % NRING
            if k < n_stage:
                g = stgc[0] % 2
                stgc[0] += 1
                src32 = wcin_d if name == "cin" else wmin_d
                dma("sp", stg[g], src32[lo:lo + 128, :], f"stg{g}", [], [B(f"stg{g}")])
                staged_cast(ring[slot][:], g, B(f"ring{slot}"))
                wb = B(f"wb_{name}_{lo}")
                wbmap[(name == "cin", lo)] = wb
                dma("pool", src[lo:lo + 128, :], ring[slot][:], f"wbk{slot}", [B(f"ring{slot}")], [wb])
            else:
                key = (name == "cin", lo)
                deps = [wbmap[key]] if key in wbmap else chunk_bufs(chunks, lo, lo + 128)
                dma("sp", ring[slot][:], src[lo:lo + 128, :], f"ring{slot}", deps, [B(f"ring{slot}")])
            issued[0] += 1

    def ring_next(name):
        k = taken[0]
        assert plan[k][0] == name, (plan[k][0], name)
        ring_issue(k + NRING)
        taken[0] += 1
        slot = k % NRING
        return ring[slot], B(f"ring{slot}")

    def norm_to_uT(tile, layer, dstT, dstname, src_loader, only=None):
        for j, (off, n) in enumerate(tile["subs"]):
            if only is not None and j not in only:
                continue
            if src_loader is not None:
                src_loader(j, off, n)
            ssj, rsj = B(f"ss{j}"), B(f"rs{j}")
            S.op("act", lambda e, j=j, n=n: (e.activation(out=junk[:n, :], in_=xt[j][:n, :], func=AF.Square,
                                                           accum_out=stat[:n, j:j + 1]), e.drain())[-1],
                 [B(f"xt{j}")], [B("junk"), ssj])
            S.op("act", lambda e, j=j, n=n: e.activation(out=stat[:n, 8 + j:9 + j], in_=stat[:n, j:j + 1],
                                                          func=AF.Sqrt, scale=1.0 / D, bias=EPS), [ssj], [rsj])
            S.op("dve", lambda e, j=j, n=n: e.reciprocal(out=stat[:n, 8 + j:9 + j], in_=stat[:n, 8 + j:9 + j]),
                 [rsj], [rsj])
            k = 0
            S.op("dve", lambda e, j=j, n=n, k=k: e.tensor_scalar(out=xs[k][:n, :], in0=xt[j][:n, :],
                                                                  scalar1=stat[:n, 8 + j:9 + j], scalar2=None,
                                                                  op0=ALU.mult),
                 [B(f"xt{j}"), rsj], [B(f"xs{k}")])
            pt, pb = bank()
            ptv = pt[:].bitcast(BF16).rearrange("p (c t) -> p c t", c=8)

            def tr(e, n=n, k=k, ptv=ptv):
                i = None
                for c in range(8):
                    i = e.transpose(out=ptv[:, c, :n], in_=xs[k][:n, c * 128:(c + 1) * 128], identity=idb[:n, :n])
                return i
            S.op("pe", tr, [B(f"xs{k}"), B("idb")], [pb])
            nwb = prm[:, P_NW + 8 * layer:P_NW + 8 * layer + 8].unsqueeze(2).to_broadcast([128, 8, n])
            S.op("act" if False else "dve",
                 lambda e, n=n, off=off, ptv=ptv, nwb=nwb: e.tensor_tensor(out=dstT[:, :, off:off + n],
                                                                           in0=ptv[:, :, :n], in1=nwb, op=ALU.mult),
                 [pb, B("prm")], [B(dstname)])

    def load_x(src_d, t0):
        def f(j, off, n):
            extra = [b for k, b in bufs.items() if "_c" in k] if CASTBAR else []
            dma("sp", xt[j][:n, :], src_d[t0 + off:t0 + off + n, :], f"xt{j}", extra, [B(f"xt{j}")])
        return f

    def load_h1(t0, ti):
        def f(j, off, n):
            dma("sp", xt[j][:n, :], h1_d[t0 + off:t0 + off + n, :], f"xt{j}", [B(f"h1d_{ti}_{j}")],
                [B(f"xt{j}")])
        return f

    def gates_stage(tile, first_tile, dbase):
        subs_ = tile["subs"]
        nc_ = len(subs_)
        sm = B("ps6")
        allj = lambda nm: [B(f"{nm}{j}") for j in range(nc_)]

        def gate_mm(e):
            i = None
            for j, (off, n) in enumerate(subs_):
                if first_tile and j == 0:
                    off, n = 0, 128
                for c in range(8):
                    i = e.matmul(small[:n, 8 * j:8 * j + 8], lhsT=u1T[:, c, off:off + n], rhs=wgs[:, c, :],
                                 start=(c == 0), stop=(c == 7))
            return i
        S.op("pe", gate_mm, [B("u1T"), B("wgs")], [sm])
        gb_b = prm[:, P_GB:P_GB + 8].unsqueeze(1).to_broadcast([128, nc_, 8])
        S.op("dve", lambda e: e.tensor_tensor(out=gt[:, 0:nc_, :],
                                              in0=small[:, 0:8 * nc_].rearrange("p (j g) -> p j g", g=8),
                                              in1=gb_b, op=ALU.add), [sm, B("prm")], allj("gt"))
        S.op("act", lambda e: e.activation(out=gtmp[:, 0:nc_, :], in_=gt[:, 0:nc_, 4:8], func=AF.Exp, scale=-1.0),
             allj("gt"), allj("gtmp"))
        S.op("act", lambda e: e.activation(out=nlf[:, 0:nc_, :], in_=gtmp[:, 0:nc_, :], func=AF.Ln, bias=1.0),
             allj("gtmp"), allj("nlf"))
        if first_tile:
            S.op("dve", lambda e: e.tensor_scalar(out=nlf[:, 0, :], in0=nlf[:, 0, :], scalar1=prm[:, P_PM:P_PM + 1],
                                                  scalar2=None, op0=ALU.mult), [B("nlf0"), B("prm")], [B("nlf0")])
        nlf_flat = nlf[:, 0:nc_, :].rearrange("p j g -> p (j g)")
        S.op("pe", lambda e: (e.matmul(small[:, 40:40 + 4 * nc_], lhsT=tri, rhs=nlf_flat, start=True, stop=True),
                              e.matmul(small[:, 64:64 + 4 * nc_], lhsT=ones, rhs=nlf_flat, start=True, stop=True))[-1],
             allj("nlf") + [B("prm")], [sm])
        cs = small[:, 40:40 + 4 * nc_].rearrange("p (j g) -> p j g", g=4)
        tot = small[:, 64:64 + 4 * nc_].rearrange("p (j g) -> p j g", g=4)
        S.op("act", lambda e: e.activation(out=av[:, 0:nc_, :], in_=cs, func=AF.Exp, scale=-1.0), [sm], allj("av"))
        S.op("dve", lambda e: e.tensor_tensor(out=gtmp[:, 0:nc_, :], in0=cs, in1=gt[:, 0:nc_, 0:4], op=ALU.add),
             [sm] + allj("gt"), allj("gtmp"))
        S.op("act", lambda e: e.activation(out=ee[:, 0:nc_, :], in_=gtmp[:, 0:nc_, :], func=AF.Exp),
             allj("gtmp"), allj("ee"))
        if first_tile:
            S.op("dve", lambda e: e.tensor_scalar(out=ee[:, 0, :], in0=ee[:, 0, :], scalar1=prm[:, P_PM:P_PM + 1],
                                                  scalar2=None, op0=ALU.mult), [B("ee0"), B("prm")], [B("ee0")])
        S.op("dve", lambda e: e.tensor_copy(out=eeb[:, 0:nc_, :], in_=ee[:, 0:nc_, :]), allj("ee"), allj("eeb"))
        S.op("act", lambda e: e.activation(out=dec[:, dbase:dbase + nc_, :], in_=tot, func=AF.Exp, scale=-1.0),
             [sm], [B(f"dec{dbase + j}") for j in range(nc_)])

    def proj_tok(tile, slot, sbuf, ncols, evac):
        sv = slot[:, 0:8 * ncols].rearrange("p (c n) -> p c n", c=8)
        for j, (off, n) in enumerate(tile["subs"]):
            pt, pb = bank()
            S.op("pe", lambda e, off=off, n=n, pt=pt, sv=sv: [e.matmul(pt[:n, :ncols], lhsT=u1T[:, c, off:off + n],
                                                                        rhs=sv[:, c, :], start=(c == 0), stop=(c == 7))
                                                               for c in range(8)][-1],
                 [B("u1T"), sbuf], [pb])
            evac(j, off, n, pt, pb)

    ONESI = 10
    prev_dec = {h: ONESI for h in range(H)}

    def state_update(j, n, h, k_src, v_src, st=0):
        pd = prev_dec[h]
        vk = (j + h) % 2
        S.op("act", lambda e, j=j, n=n, h=h, vk=vk: e.activation(out=vp[vk][:n, :], in_=v_src[j][:n, :], func=AF.Copy,
                                                                   scale=ee[:n, j, h:h + 1]),
             [B(f"vh{st}_{j}"), B(f"ee{j}")], [B(f"vp{vk}")])
        ups = []
        for half in range(2):
            pt, pb = bank()
            S.op("pe", lambda e, j=j, n=n, half=half, pt=pt, vk=vk: e.matmul(
                pt[:, :], lhsT=k_src[j][:n, half * 128:(half + 1) * 128], rhs=vp[vk][:n, :], start=True, stop=True),
                [B(f"kh{st}_{j}"), B(f"vp{vk}")], [pb])
            ups.append((pt, pb))
        sn = B("ps7")
        S.op("pe", lambda e, j=j, n=n, h=h: [e.matmul(small2[:, 8 + half:9 + half],
                                                      lhsT=k_src[j][:n, half * 128:(half + 1) * 128],
                                                      rhs=eeb[:n, j, h:h + 1], start=True, stop=True)
                                             for half in range(2)][-1],
             [B(f"kh{st}_{j}"), B(f"eeb{j}")], [sn])
        return ups, sn, pd

    def state_commit(j, h, ups, sn, pd, dbase):
        for half, (pt, pb) in enumerate(ups):
            S.op("dve", lambda e, h=h, half=half, pt=pt, pd=pd: e.scalar_tensor_tensor(
                out=Tst[:, h, half, :], in0=Tst[:, h, half, :], scalar=dec[:, pd, h:h + 1], in1=pt[:, :],
                op0=ALU.mult, op1=ALU.add), [B(f"T{h}"), pb, B(f"dec{pd}")], [B(f"T{h}")])
        S.op("dve", lambda e, h=h, pd=pd: e.scalar_tensor_tensor(
            out=Tn[:, 2 * h:2 * h + 2], in0=Tn[:, 2 * h:2 * h + 2], scalar=dec[:, pd, h:h + 1],
            in1=small2[:, 8:10], op0=ALU.mult, op1=ALU.add), [B(f"Tn{h}"), sn, B(f"dec{pd}")], [B(f"Tn{h}")])
        prev_dec[h] = dbase + j

    def copy_evac(dst_list, dname, ncols, eng="act"):
        def f(j, off, n, pt, pb):
            if eng == "act":
                S.op("act", lambda e: e.activation(out=dst_list[j][:n, :ncols], in_=pt[:n, :ncols], func=AF.Copy),
                     [pb], [B(f"{dname}{j}")])
            else:
                S.op("dve", lambda e: e.tensor_copy(out=dst_list[j][:n, :ncols], in_=pt[:n, :ncols]),
                     [pb], [B(f"{dname}{j}")])
        return f

    S.op("pool", lambda e: e.memset(dec[:, ONESI, :], 1.0), [], [B(f"dec{ONESI}")])
    if do1:
        S.op("pool", lambda e: e.memset(hal[:], 0.0), [], [B("hal")])
        S.op("pool", lambda e: e.memset(Tst[:].rearrange("p h a v -> p (h a v)"), 0.0), [],
             [B(f"T{h}") for h in range(H)])
        S.op("pool", lambda e: e.memset(Tn[:], 0.0), [], [B(f"Tn{h}") for h in range(H)])

    if do1:
        for ti, tile in enumerate(tiles):
            W_, t0 = tile["W"], tile["t0"]
            if STAGED and (ti == 1 or len(tiles) == 1):
                late_casts()
            if DBG >= 2 and ti == 0:
                norm_to_uT(tile, 0, uT, "uT", load_x(xin_d, t0))
            nbig[0] = 8
            for fc in range(16 if DBG >= 3 else 0):
                if ti == 0 and STAGED and fc == 8:
                    wflat = wout[:].rearrange("p c n -> p (c n)")
                    for q in range(4):
                        g = stgc[0] % 2
                        stgc[0] += 1
                        dma("sp", stg[g], wcout_d[:, q * 4096:(q + 1) * 4096], f"stg{g}", [], [B(f"stg{g}")])
                        staged_cast(wflat[:, q * 4096:(q + 1) * 4096], g, B("wout"))
                slot, sbuf = ring_next("cin")
                sv = slot[:].rearrange("p (c g j) -> p c g j", c=8, g=4)
                S.op("pool", lambda e, fc=fc: e.tensor_copy(out=cx[:, 0:2], in_=hal[:, fc, :]), [B("hal")], [B("cx")])
                for (off, n) in tile["segs"]:
                    pbk = {}
                    for g in (1, 2, 3, 0):
                        pt, pb = bank()
                        pbk[g] = (pt, pb)
                        S.op("pe", lambda e, off=off, n=n, g=g, pt=pt, sv=sv: [e.matmul(
                            pt[:, :n], lhsT=sv[:, c, g, :], rhs=uT[:, c, off:off + n], start=(c == 0), stop=(c == 7))
                            for c in range(8)][-1], [B("uT"), sbuf], [pb])
                    S.op("act", lambda e, n=n, p=pbk[2][0]: e.activation(out=tA[:, :n], in_=p[:, :n], func=AF.Copy),
                         [pbk[2][1]], [B("tA")])
                    S.op("dve", lambda e, off=off, n=n, p=pbk[1][0]: e.tensor_tensor(
                        out=cx[:, 2 + off:2 + off + n], in0=p[:, :n], in1=tA[:, :n], op=ALU.mult),
                        [pbk[1][1], B("tA")], [B("cx")])
                    S.op("act", lambda e, n=n, p=pbk[3][0]: e.activation(out=tB[:, :n], in_=p[:, :n], func=AF.Silu),
                         [pbk[3][1]], [B("tB")])
                    S.op("dve", lambda e, off=off, n=n, p=pbk[0][0]: e.tensor_tensor(
                        out=bg[:, off:off + n], in0=p[:, :n], in1=tB[:, :n], op=ALU.mult),
                        [pbk[0][1], B("tB")], [B("bg")])
                cw = lambda k, fc=fc: prm[:, P_CW + 16 * k + fc:P_CW + 16 * k + fc + 1]
                S.op("act", lambda e, W_=W_, cw=cw: e.activation(out=yv[:, :W_], in_=cx[:, 2:2 + W_], func=AF.Copy,
                                                                  scale=cw(2)), [B("cx"), B("prm")], [B("yv")])
                S.op("dve", lambda e, W_=W_, cw=cw: e.scalar_tensor_tensor(
                    out=yv[:, :W_], in0=cx[:, 1:1 + W_], scalar=cw(1), in1=yv[:, :W_], op0=ALU.mult, op1=ALU.add),
                    [B("cx"), B("yv"), B("prm")], [B("yv")])
                S.op("dve", lambda e, W_=W_, cw=cw: e.scalar_tensor_tensor(
                    out=yv[:, :W_], in0=cx[:, 0:W_], scalar=cw(0), in1=yv[:, :W_], op0=ALU.mult, op1=ALU.add),
                    [B("cx"), B("yv"), B("prm")], [B("yv")])
                S.op("dve", lambda e, W_=W_, fc=fc: e.tensor_tensor(out=gT[:, fc, :W_], in0=yv[:, :W_],
                                                                   in1=bg[:, :W_], op=ALU.mult),
                     [B("yv"), B("bg")], [B("gT")])
                S.op("pool", lambda e, W_=W_, fc=fc: e.tensor_copy(out=hal[:, fc, :], in_=cx[:, W_:W_ + 2]),
                     [B("cx")], [B("hal")])
            nbig[0] = 6
            if False:
                wflat = wout[:].rearrange("p c n -> p (c n)")
                for q in range(4):
                    g = stgc[0] % 2
                    stgc[0] += 1
                    dma("sp", stg[g], wcout_d[:, q * 4096:(q + 1) * 4096], f"stg{g}", [], [B(f"stg{g}")])
                    staged_cast(wflat[:, q * 4096:(q + 1) * 4096], g, B("wout"))
            elif ti == 0 and not STAGED:
                dma("sp", wout[:].rearrange("p c n -> p (c n)"), wcout_b, "wout", chunk_bufs(cout_chunks, 0, 128),
                    [B("wout")])
            for j, (off, n) in enumerate(tile["subs"] if DBG >= 4 else []):
                for half in range(2):
                    pt, pb = bank()
                    S.op("pe", lambda e, off=off, n=n, half=half, pt=pt: [e.matmul(
                        pt[:n, :], lhsT=gT[:, fc, off:off + n], rhs=wout[:, fc, half * 512:(half + 1) * 512],
                        start=(fc == 0), stop=(fc == 15)) for fc in range(16)][-1], [B("gT"), B("wout")], [pb])
                    S.op("dve", lambda e, j=j, n=n, half=half, pt=pt: e.tensor_tensor(
                        out=xt[j][:n, half * 512:(half + 1) * 512], in0=pt[:n, :],
                        in1=xt[j][:n, half * 512:(half + 1) * 512], op=ALU.add), [pb, B(f"xt{j}")], [B(f"xt{j}")])
                dma("sp", h1_d[t0 + off:t0 + off + n, :], xt[j][:n, :], f"h1s{j}", [B(f"xt{j}")],
                    [B(f"h1d_{ti}_{j}")])
                if DBG >= 5 and j >= 1:
                    norm_to_uT(tile, 1, u1T, "u1T", None, only=[j - 1])
            if DBG >= 5:
                norm_to_uT(tile, 1, u1T, "u1T", None, only=[len(tile["subs"]) - 1])
            if DBG >= 6:
                gates_stage(tile, ti == 0, 5 * (ti % 2))
            def p1_proj(h, st):
                slot, sbuf = ring_next("m1")
                proj_tok(tile, slot, sbuf, DK, copy_evac(kh2[st], f"kh{st}_", DK, "act"))
                slot, sbuf = ring_next("m2")
                proj_tok(tile, slot, sbuf, DV, copy_evac(vh2[st], f"vh{st}_", DV, "dve"))
                if mode == "fused":
                    for j, (off, n) in enumerate(tile["subs"]):
                        dma("sp", kv_d[h, t0 + off:t0 + off + n, 0:DK], kh2[st][j][:n, :], f"ks{st}_{j}",
                            [B(f"kh{st}_{j}")], [B(f"kvd_{h}_{ti}_{j}")])
                        dma("sp", kv_d[h, t0 + off:t0 + off + n, DK:DK + DV], vh2[st][j][:n, :], f"vs{st}_{j}",
                            [B(f"vh{st}_{j}")], [B(f"kvd_{h}_{ti}_{j}")])

            def p1_state(h, st):
                for j, (off, n) in enumerate(tile["subs"]):
                    ups, sn, pd = state_update(j, n, h, kh2[st], vh2[st], st)
                    state_commit(j, h, ups, sn, pd, 5 * (ti % 2))
            nset1 = len(kh2)
            if ti + 1 < len(tiles):
                nt = tiles[ti + 1]
                ld = load_x(xin_d, nt["t0"])
                for j, (off, n) in enumerate(nt["subs"]):
                    ld(j, off, n)
            p1_proj(0, 0)
            for h in range(H):
                if h + 1 < H and nset1 > 1:
                    p1_proj(h + 1, (h + 1) % 2)
                p1_state(h, h % nset1)
                if h + 1 < H and nset1 == 1:
                    p1_proj(h + 1, 0)
                if h == 2 and ti + 1 < len(tiles):
                    norm_to_uT(tiles[ti + 1], 0, uT, "uT", None)
        for h in range(H):
            pd = prev_dec[h]
            S.op("dve", lambda e, h=h, pd=pd: e.tensor_scalar(
                out=Tst[:, h, :, :].rearrange("p a v -> p (a v)"), in0=Tst[:, h, :, :].rearrange("p a v -> p (a v)"),
                scalar1=dec[:, pd, h:h + 1], scalar2=None, op0=ALU.mult), [B(f"T{h}"), B(f"dec{pd}")], [B(f"T{h}")])
            S.op("dve", lambda e, h=h, pd=pd: e.tensor_scalar(
                out=Tn[:, 2 * h:2 * h + 2], in0=Tn[:, 2 * h:2 * h + 2], scalar1=dec[:, pd, h:h + 1], scalar2=None,
                op0=ALU.mult), [B(f"Tn{h}"), B(f"dec{pd}")], [B(f"Tn{h}")])
            prev_dec[h] = ONESI
        Tall = [B(f"T{h}") for h in range(H)]
        Tnall = [B(f"Tn{h}") for h in range(H)]
        if mode == "p1":
            dT, dN = st_d[:, 0:H * 2 * DV], st_d[:, H * 2 * DV:NST]
        else:
            dT, dN = st_locT, st_locN
        dma("sp", dT, Tst[:].rearrange("p h a v -> p (h a v)"), "st_s", Tall, [B("st_d")])
        dma("sp", dN, Tn[:], "st_s", Tnall, [B("st_d")])

    if mode == "fused":
        dma_keys.add("ccT")
        dma_keys.add("ccN")
        groups = [[2 * i, 2 * i + 1] for i in range(n_cores // 2)]
        S.dma("pool", lambda e: e.collective_compute("AllGather", ALU.bypass, replica_groups=groups,
                                                     ins=[st_locT], outs=[st_allT]),
              "ccT", reads=[B("st_d")], writes=[B("st_all")], inc=1)
        S.dma("pool", lambda e: e.collective_compute("AllGather", ALU.bypass, replica_groups=groups,
                                                     ins=[st_locN], outs=[st_allN]),
              "ccN", reads=[B("st_d")], writes=[B("st_all")], inc=1)
        srcT, srcN = st_allT[0:128, :], st_allN[0:128, :]
        st_dep = [B("st_all")]
    elif mode == "p2":
        srcT, srcN = st_d[:, 0:H * 2 * DV], st_d[:, H * 2 * DV:NST]
        st_dep = []
    if do2:
        Tall = [B(f"T{h}") for h in range(H)]
        Tnall = [B(f"Tn{h}") for h in range(H)]
        dma("sp", Tst[:].rearrange("p h a v -> p (h a v)"), srcT, "st_l", st_dep, Tall)
        dma("sp", Tn[:], srcN, "st_l", st_dep, Tnall)
        for h in range(H):
            S.op("dve", lambda e, h=h: e.tensor_scalar(
                out=Tst[:, h, :, :].rearrange("p a v -> p (a v)"), in0=Tst[:, h, :, :].rearrange("p a v -> p (a v)"),
                scalar1=prm[:, P_FL + 1:P_FL + 2], scalar2=None, op0=ALU.mult), [B(f"T{h}"), B("prm")], [B(f"T{h}")])
            S.op("dve", lambda e, h=h: e.tensor_scalar(
                out=Tn[:, 2 * h:2 * h + 2], in0=Tn[:, 2 * h:2 * h + 2], scalar1=prm[:, P_FL + 1:P_FL + 2],
                scalar2=None, op0=ALU.mult), [B(f"Tn{h}"), B("prm")], [B(f"Tn{h}")])
        dma("sp", wout[:].rearrange("p c n -> p (c n)"), wmout_b, "wout", chunk_bufs(mout_chunks, 0, 128),
            [B("wout")])

    if do2:
        hnw = prm[:, P_HNW:P_HNW + E]
        fnw = prm[:, P_FNW:P_FNW + D]
        nbig[0] = 6
        for ti, tile in enumerate(tiles):
            W_, t0 = tile["W"], tile["t0"]
            norm_to_uT(tile, 1, u1T, "u1T", load_h1(t0, ti))
            gates_stage(tile, ti == 0, 5 * (ti % 2))
            def P_items(h, st):
                items = []
                box = {}

                def it_qk(blk, off, n):
                    def f():
                        if "qk" not in box:
                            box["qk"] = ring_next("m0")
                            if mode == "fused":
                                for j, (o2, n2) in enumerate(tile["subs"]):
                                    dma("sp", kh2[st][j][:n2, :], kv_d[h, t0 + o2:t0 + o2 + n2, 0:DK], f"kl{st}_{j}",
                                        [B(f"kvd_{h}_{ti}_{j}")], [B(f"kh{st}_{j}")])
                                    dma("sp", vh2[st][j][:n2, :], kv_d[h, t0 + o2:t0 + o2 + n2, DK:DK + DV],
                                        f"vl{st}_{j}", [B(f"kvd_{h}_{ti}_{j}")], [B(f"vh{st}_{j}")])
                        slot, sbuf = box["qk"]
                        sv = slot[:].rearrange("p (c g j) -> p c g j", c=8, g=4)
                        pt, pb = bank()
                        S.op("pe", lambda e: [e.matmul(pt[:, :n], lhsT=sv[:, c, blk, :], rhs=u1T[:, c, off:off + n],
                                                       start=(c == 0), stop=(c == 7)) for c in range(8)][-1],
                             [B("u1T"), sbuf], [pb])
                        if blk < 2:
                            S.op("act", lambda e: e.activation(out=qT2[st][:, blk, off:off + n], in_=pt[:, :n],
                                                               func=AF.Copy, scale=DK ** -0.5), [pb], [B(f"qT{st}")])
                        else:
                            S.op("dve", lambda e: e.tensor_copy(out=kT2[st][:, blk - 2, off:off + n], in_=pt[:, :n]),
                                 [pb], [B(f"kT{st}")])
                    return f
                for blk in range(4):
                    for (off, n) in tile["segs"]:
                        items.append(it_qk(blk, off, n))

                def it_tok(name, j, off, n):
                    def f():
                        if name not in box:
                            box[name] = ring_next(name)
                        slot, sbuf = box[name]
                        sv = slot[:].rearrange("p (c n) -> p c n", c=8)
                        pt, pb = bank()
                        S.op("pe", lambda e: [e.matmul(pt[:n, :], lhsT=u1T[:, c, off:off + n], rhs=sv[:, c, :],
                                                       start=(c == 0), stop=(c == 7)) for c in range(8)][-1],
                             [B("u1T"), sbuf], [pb])
                        if name == "m1":
                            S.op("act", lambda e: e.activation(out=kh2[st][j][:n, :], in_=pt[:n, :DK], func=AF.Copy),
                                 [pb], [B(f"kh{st}_{j}")])
                        elif name == "m2":
                            S.op("dve", lambda e: e.tensor_copy(out=vh2[st][j][:n, :], in_=pt[:n, :]),
                                 [pb], [B(f"vh{st}_{j}")])
                        elif name == "m3":
                            g = gctr[0] % 2
                            gctr[0] += 1
                            tg, tgb = tG[g], B(f"tG{g}")
                            S.op("act", lambda e: e.activation(out=tg[:n, :], in_=pt[:n, :], func=AF.Tanh, scale=0.5),
                                 [pb], [tgb])
                            S.op("dve", lambda e: e.scalar_tensor_tensor(
                                out=tw2[st][j][:n, :], in0=tg[:n, :], scalar=1.0, in1=hnw[:n, h * DV:(h + 1) * DV],
                                op0=ALU.add, op1=ALU.mult), [tgb, B("prm")], [B(f"tw{st}_{j}")])
                        else:
                            g = gctr[0] % 2
                            gctr[0] += 1
                            tg, tgb = tG[2 + g], B(f"tG{2 + g}")
                            S.op("act", lambda e: e.activation(out=tg[:n, :], in_=pt[:n, :], func=AF.Tanh, scale=0.5),
                                 [pb], [tgb])
                            S.op("dve", lambda e: e.scalar_tensor_tensor(
                                out=tg[:n, :], in0=tg[:n, :], scalar=1.0, in1=tw2[st][j][:n, :], op0=ALU.add,
                                op1=ALU.mult), [tgb, B(f"tw{st}_{j}")], [tgb])
                            S.op("dve", lambda e: e.scalar_tensor_tensor(
                                out=wgh2[st][j][:n, :], in0=pt[:n, :], scalar=0.25, in1=tg[:n, :], op0=ALU.mult,
                                op1=ALU.mult), [pb, tgb], [B(f"wgh{st}_{j}")])
                    return f
                for name in (("m3", "m4") if mode == "fused" else ("m1", "m2", "m3", "m4")):
                    for j, (off, n) in enumerate(tile["subs"]):
                        items.append(it_tok(name, j, off, n))
                return items

            def S_gen(h, st):
                qT, kT = qT2[st], kT2[st]
                ctx = {}

                def stage_a1(j, off, n):
                    vk = j % 2
                    S.op("act", lambda e: e.activation(out=vp[vk][:n, :], in_=vh2[st][j][:n, :], func=AF.Copy,
                                                       scale=ee[:n, j, h:h + 1]),
                         [B(f"vh{st}_{j}"), B(f"ee{j}")], [B(f"vp{vk}")])
                    pS, pSb = bank()
                    S.op("pe", lambda e: [e.matmul(
                        pS[:n, :n], lhsT=kT[:, a, off:off + n], rhs=qT[:, a, off:off + n], start=(a == 0),
                        stop=(a == 1)) for a in range(2)][-1], [B(f"kT{st}"), B(f"qT{st}")], [pSb])
                    mk = j % 2
                    S.op("dve", lambda e: e.tensor_tensor(
                        out=smT[mk][:n, :n], in0=pS[:n, :n], in1=tri[:n, :n], op=ALU.mult),
                        [pSb, B("prm")], [B(f"smT{mk}")])
                    yield

                def stage_a(j, off, n):
                    pd = prev_dec[h]
                    vk = mk = j % 2
                    kx = kh2[st]
                    ups = []
                    for half in range(2):
                        pt, pb = bank()
                        S.op("pe", lambda e, half=half, pt=pt: e.matmul(
                            pt[:, :], lhsT=kx[j][:n, half * 128:(half + 1) * 128], rhs=vp[vk][:n, :], start=True,
                            stop=True), [B(f"kh{st}_{j}"), B(f"vp{vk}")], [pb])
                        ups.append((pt, pb))
                    sn = B("ps7")
                    S.op("pe", lambda e: [e.matmul(small2[:, 8 + half:9 + half],
                                                   lhsT=kx[j][:n, half * 128:(half + 1) * 128],
                                                   rhs=eeb[:n, j, h:h + 1], start=True, stop=True)
                                          for half in range(2)][-1], [B(f"kh{st}_{j}"), B(f"eeb{j}")], [sn])
                    S.op("act", lambda e, pd=pd: e.activation(
                        out=Sbf[:].rearrange("p a v -> p (a v)"), in_=Tst[:, h, :, :].rearrange("p a v -> p (a v)"),
                        func=AF.Copy, scale=dec[:, pd, h:h + 1]), [B(f"T{h}"), B(f"dec{pd}")], [B("Sbf")])
                    S.op("dve", lambda e, pd=pd: e.tensor_scalar(
                        out=nbf[:, 2 * h:2 * h + 2], in0=Tn[:, 2 * h:2 * h + 2], scalar1=dec[:, pd, h:h + 1],
                        scalar2=None, op0=ALU.mult), [B(f"Tn{h}"), B(f"dec{pd}")], [B("nbf")])
                    yield
                    pN, pNb = bank()

                    def num_mm(e):
                        e.matmul(pN[:n, :], lhsT=smT[mk][:n, :n], rhs=vp[vk][:n, :], start=True, stop=False)
                        e.matmul(pN[:n, :], lhsT=qT[:, 0, off:off + n], rhs=Sbf[:, 0, :], start=False, stop=False)
                        return e.matmul(pN[:n, :], lhsT=qT[:, 1, off:off + n], rhs=Sbf[:, 1, :], start=False,
                                        stop=True)
                    S.op("pe", num_mm, [B(f"smT{mk}"), B(f"vp{vk}"), B(f"qT{st}"), B("Sbf")], [pNb])
                    sd = B("ps7")
                    dc = j % 2

                    def den_mm(e):
                        e.matmul(small2[:n, dc:dc + 1], lhsT=smT[mk][:n, :n], rhs=eeb[:n, j, h:h + 1], start=True,
                                 stop=False)
                        e.matmul(small2[:n, dc:dc + 1], lhsT=qT[:, 0, off:off + n], rhs=nbf[:, 2 * h:2 * h + 1],
                                 start=False, stop=False)
                        return e.matmul(small2[:n, dc:dc + 1], lhsT=qT[:, 1, off:off + n],
                                        rhs=nbf[:, 2 * h + 1:2 * h + 2], start=False, stop=True)
                    S.op("pe", den_mm, [B(f"smT{mk}"), B(f"eeb{j}"), B(f"qT{st}"), B("nbf")], [sd])
                    state_commit(j, h, ups, sn, pd, 5 * (ti % 2))
                    S.op("act", lambda e: e.activation(
                        out=bst[:n, j:j + 1], in_=small2[:n, dc:dc + 1], func=AF.Abs,
                        scale=av[:n, j, h:h + 1]), [sd, B(f"av{j}")], [B(f"bst_d{j}")])
                    S.op("act", lambda e: (e.activation(out=junk[:n, :DV], in_=pN[:n, :], func=AF.Square,
                                                        accum_out=bst[:n, 8 + j:9 + j]), e.drain())[-1],
                         [pNb], [B("junk"), B(f"bst_q{j}")])
                    S.op("act", lambda e: e.activation(out=numS[j][:n, :], in_=pN[:n, :], func=AF.Copy),
                         [pNb], [B(f"numS{j}")])
                    yield

                def stage_b_all():
                    subs_ = tile["subs"]
                    nc_ = len(subs_)
                    rd = [B(f"bst_d{j}") for j in range(nc_)] + [B(f"bst_q{j}") for j in range(nc_)]
                    bb = B("bst")
                    avh = av[:, 0:nc_, h]
                    S.op("dve", lambda e: e.tensor_scalar(out=bst[:, 16:16 + nc_], in0=bst[:, 0:nc_], scalar1=1.0,
                                                          scalar2=None, op0=ALU.max), rd, [bb])
                    S.op("dve", lambda e: e.reciprocal(out=bst[:, 16:16 + nc_], in_=bst[:, 16:16 + nc_]), [bb], [bb])
                    S.op("dve", lambda e: e.tensor_tensor(out=bst[:, 24:24 + nc_], in0=bst[:, 16:16 + nc_], in1=avh,
                                                          op=ALU.mult),
                         [bb] + [B(f"av{j}") for j in range(nc_)], [bb])
                    S.op("dve", lambda e: e.tensor_tensor(out=bst[:, 32:32 + nc_], in0=bst[:, 8:8 + nc_],
                                                          in1=bst[:, 24:24 + nc_], op=ALU.mult), [bb] + rd, [bb])
                    S.op("dve", lambda e: e.tensor_tensor(out=bst[:, 32:32 + nc_], in0=bst[:, 32:32 + nc_],
                                                          in1=bst[:, 24:24 + nc_], op=ALU.mult), [bb], [bb])
                    S.op("act", lambda e: e.activation(out=bst[:, 40:40 + nc_], in_=bst[:, 32:32 + nc_], func=AF.Sqrt,
                                                       scale=1.0 / DV, bias=EPS), [bb], [bb])
                    S.op("dve", lambda e: e.reciprocal(out=bst[:, 40:40 + nc_], in_=bst[:, 40:40 + nc_]), [bb], [bb])
                    S.op("dve", lambda e: e.tensor_tensor(out=bst[:, 48:48 + nc_], in0=bst[:, 40:40 + nc_],
                                                          in1=bst[:, 24:24 + nc_], op=ALU.mult), [bb], [bb])

                    def gate_stt(j, off, n):
                        hk = j % 2
                        S.op("dve", lambda e: e.scalar_tensor_tensor(
                            out=hg[hk][:n, :], in0=numS[j][:n, :], scalar=bst[:n, 48 + j:49 + j],
                            in1=wgh2[st][j][:n, :], op0=ALU.mult, op1=ALU.mult),
                            [B(f"numS{j}"), bb, B(f"wgh{st}_{j}")], [B(f"hg{hk}")])

                    def tr_evac(j, off, n):
                        hk = j % 2
                        pT, pTb = bank()
                        pTv = pT[:].bitcast(BF16).rearrange("p (c t) -> p c t", c=8)
                        S.op("pe", lambda e: [e.transpose(
                            out=pTv[:, c, :n], in_=hg[hk][:n, c * 128:(c + 1) * 128], identity=idb[:n, :n])
                            for c in range(4)][-1], [B(f"hg{hk}"), B("idb")], [pTb])
                        S.op("act", lambda e: e.activation(
                            out=gT[:, 4 * h:4 * h + 4, off:off + n], in_=pTv[:, 0:4, :n], func=AF.Copy),
                            [pTb], [B(f"gTh{h}_{j}")])
                    gate_stt(0, *subs_[0])
                    gate_stt(1, *subs_[1])
                    yield
                    for j in range(nc_):
                        tr_evac(j, *subs_[j])
                        if j + 2 < nc_:
                            gate_stt(j + 2, *subs_[j + 2])
                        yield

                subs = tile["subs"]
                yield from stage_a1(0, *subs[0])
                for j, (off, n) in enumerate(subs):
                    if j + 1 < len(subs):
                        yield from stage_a1(j + 1, *subs[j + 1])
                    yield from stage_a(j, off, n)
                yield from stage_b_all()

            for it in P_items(0, 0):
                it()
            for h in range(H):
                st = h % 2
                filler = P_items(h + 1, 1 - st) if h + 1 < H else []
                fi = 0
                nsub_ = len(tile["subs"])
                n_a = 3 * nsub_
                RES = min(5, len(filler))
                per = -(-(len(filler) - RES) // n_a) if filler else 0
                yi = 0
                for _ in S_gen(h, st):
                    yi += 1
                    if yi <= n_a:
                        lim, k = len(filler) - RES, per
                    elif yi == n_a + 1:
                        lim, k = len(filler), RES
                    else:
                        lim, k = len(filler), per
                    for _k in range(k):
                        if fi < lim:
                            filler[fi]()
                            fi += 1
                while fi < len(filler):
                    filler[fi]()
                    fi += 1
            opb = {}

            def op_first(j, off, n):
                for half in range(2):
                    pt, pb = bank()
                    opb[(j, half)] = (pt, pb)
                    S.op("pe", lambda e, half=half, pt=pt: [e.matmul(
                        pt[:n, :], lhsT=gT[:, ec, off:off + n], rhs=wout[:, ec, half * 512:(half + 1) * 512],
                        start=(ec == 0), stop=False) for ec in range(12)][-1],
                        [B(f"gTh{hh}_{j}") for hh in range(3)] + [B("wout")], [pb])

            def op_second(j, off, n):
                for half in range(2):
                    pt, pb = opb.pop((j, half))
                    S.op("pe", lambda e, half=half, pt=pt: [e.matmul(
                        pt[:n, :], lhsT=gT[:, ec, off:off + n], rhs=wout[:, ec, half * 512:(half + 1) * 512],
                        start=False, stop=(ec == 15)) for ec in range(12, 16)][-1],
                        [B(f"gTh3_{j}"), B("wout")], [pb])
                    S.op("dve", lambda e, half=half, pt=pt: e.tensor_tensor(
                        out=xt[j][:n, half * 512:(half + 1) * 512], in0=pt[:n, :],
                        in1=xt[j][:n, half * 512:(half + 1) * 512], op=ALU.add), [pb, B(f"xt{j}")], [B(f"xt{j}")])
                final_norm(j, off, n)

            def final_norm(j, off, n):
                if ti == 0 and j == 0:
                    return
                fs = B("fstat")
                S.op("act", lambda e, j=j, n=n: (e.activation(out=junk[:n, :], in_=xt[j][:n, :], func=AF.Square,
                                                               accum_out=sstat[:n, 8:9]), e.drain())[-1],
                     [B(f"xt{j}")], [B("junk"), fs])
                S.op("act", lambda e, n=n: e.activation(out=sstat[:n, 9:10], in_=sstat[:n, 8:9], func=AF.Sqrt,
                                                         scale=1.0 / D, bias=EPS), [fs], [fs])
                S.op("dve", lambda e, n=n: e.reciprocal(out=sstat[:n, 10:11], in_=sstat[:n, 9:10]), [fs], [fs])
                S.op("dve", lambda e, j=j, n=n: e.scalar_tensor_tensor(
                    out=yout[:n, :], in0=xt[j][:n, :], scalar=sstat[:n, 10:11], in1=fnw[:n, :], op0=ALU.mult,
                    op1=ALU.mult), [B(f"xt{j}"), fs, B("prm")], [B("yout")])
                o0 = t0 + off - NPRE
                dma("sp", out_d[o0:o0 + n, :], yout[:n, :], "yout", [B("yout")], [B(f"outd_{ti}_{j}")])
            psubs = tile["subs"]
            for j, (off, n) in enumerate(psubs):
                op_first(j, off, n)
                if j >= 1:
                    op_second(j - 1, *psubs[j - 1])
            op_second(len(psubs) - 1, *psubs[-1])

    fin = [b for k, b in bufs.items() if k.startswith("outd_") or k.startswith("h1d_") or k == "st_d"]
    S.op("sp", None, fin, [])

    dma_sems = {k: es.enter_context(nc.semaphore(f"d_{k}")) for k in sorted(dma_keys)}
    with nc.Block() as block:
        S.emit(block, eng_sems, dma_sems)
    es.close()
    return nc


_PROG = {}


def _prog(mode):
    if mode not in _PROG:
        _PROG[mode] = build_program(mode)
    return _PROG[mode]


def _core_inputs(x, meta_tokens):
    xs = []
    for c in range(8):
        b, half = divmod(c, 2)
        if half == 0:
            xs.append(np.ascontiguousarray(np.concatenate([meta_tokens, x[b, :4096]], axis=0)))
        else:
            xs.append(np.ascontiguousarray(x[b, 4096 - NPRE:]))
    return xs


def kernel(x, meta_tokens, norm_w, conv_in_w, conv_w, conv_out_w, mlstm_in_w, mlstm_gate_b,
           mlstm_head_norm_w, mlstm_out_w, final_norm_w):
    x = np.asarray(x, np.float32)
    f = lambda a: np.asarray(a, np.float32)
    wcin, wcout, wmin, wg, wmout = _layout_weights(f(conv_in_w), f(conv_out_w), f(mlstm_in_w), f(mlstm_out_w))
    xs = _core_inputs(x, f(meta_tokens))
    prm = [_layout_params(f(norm_w), f(conv_w), f(mlstm_gate_b), f(mlstm_head_norm_w), f(final_norm_w),
                          1.0 if c % 2 == 0 else 0.0, 0.0 if c % 2 == 0 else 1.0) for c in range(8)]
    cores = list(range(8))
    r2 = run_bass_kernel_spmd(_prog("fused"), [dict(params=prm[c], xin=xs[c], wcin=wcin, wcout=wcout, wmin=wmin,
                                                    wg=wg, wmout=wmout) for c in cores], core_ids=cores).results
    out = np.empty((4, 8192, D), np.float32)
    for c in cores:
        b, half = divmod(c, 2)
        out[b, half * 4096:(half + 1) * 4096] = r2[c]["out"]
    return out
```

```python
import numpy as np
from contextlib import ExitStack
import concourse.bass as bass
import concourse.mybir as mybir
from concourse.bass_utils import run_bass_kernel_spmd

F32 = mybir.dt.float32
BF16 = mybir.dt.bfloat16
AF = mybir.ActivationFunctionType
ALU = mybir.AluOpType

D = 1024
E = 2048
NPRE = 16
TT = 512
H = 4
DK = 256
DV = 512
EPS = 1e-6
NRING = 3
DBG = 99
CASTBAR = False
GSUB = 99
RSLOT = 4096

P_ID, P_TRI, P_ONES = 0, 128, 256
P_NW = 384
P_CW = 400
P_GB = 448
P_FL = 456
P_PM = 458
P_FNW = 464
P_HNW = 464 + 1024
PCOLS = 464 + 1024 + 2048

ENGS = ("pe", "act", "dve", "pool", "sp")


class Buf:
    __slots__ = ("name", "w", "r", "excl")

    def __init__(self, name):
        self.name = name
        self.w = None
        self.r = []
        self.excl = name.startswith("ps")


class Op:
    __slots__ = ("eng", "fn", "deps", "raw", "pos", "is_dma", "key", "val", "inc", "signal", "count", "waits")


class Sched:
    def __init__(self):
        self.ops = {e: [] for e in ENGS}
        self.dma_cnt = {}

    def _rec(self, o, reads, writes):
        deps = []
        for b in reads:
            if b.w is not None:
                deps.append(b.w)
        o.raw = set(id(d) for d in deps)
        for b in reads:
            if b.excl:
                deps.extend(b.r)
        for b in writes:
            if b.w is not None:
                deps.append(b.w)
            deps.extend(b.r)
        o.deps = deps
        o.signal = False
        o.count = 0
        o.pos = len(self.ops[o.eng])
        self.ops[o.eng].append(o)
        for b in reads:
            if b.excl:
                b.w = o
                b.r = []
            else:
                b.r.append(o)
        for b in writes:
            b.w = o
            b.r = []
        return o

    def op(self, eng, fn, reads=(), writes=()):
        o = Op()
        o.eng, o.fn, o.is_dma, o.key, o.val = eng, fn, False, None, 0
        return self._rec(o, reads, writes)

    def dma(self, q, fn, key, reads=(), writes=(), inc=16):
        o = Op()
        o.eng, o.fn, o.is_dma, o.key = q, fn, True, key
        o.inc = inc
        o.val = self.dma_cnt.get(key, 0) + inc
        self.dma_cnt[key] = o.val
        return self._rec(o, reads, writes)

    def plan(self):
        for e in ENGS:
            seen_pos = {p: -1 for p in ENGS}
            seen_dma = {}
            for o in self.ops[e]:
                need_c, need_d = {}, {}
                for d in o.deps:
                    if d.is_dma:
                        if d.val > seen_dma.get(d.key, 0):
                            need_d[d.key] = max(need_d.get(d.key, 0), d.val)
                    else:
                        if d.eng == e and not o.is_dma:
                            if e == "pe" or id(d) not in o.raw:
                                continue
                        if d.pos > seen_pos[d.eng]:
                            if d.eng not in need_c or need_c[d.eng].pos < d.pos:
                                need_c[d.eng] = d
                waits = []
                for k, v in need_d.items():
                    seen_dma[k] = v
                    waits.append(("d", k, v))
                for pe, d in need_c.items():
                    seen_pos[pe] = d.pos
                    d.signal = True
                    waits.append(("c", d, None))
                o.waits = waits
        for e in ENGS:
            c = 0
            for o in self.ops[e]:
                if o.signal and not o.is_dma:
                    c += 1
                o.count = c

    def emit(self, block, eng_sems, dma_sems):
        self.plan()

        def run(e, eng):
            for o in self.ops[e]:
                for w in o.waits:
                    if w[0] == "d":
                        eng.wait_ge(dma_sems[w[1]], w[2])
                    else:
                        eng.wait_ge(eng_sems[w[1].eng], w[1].count)
                if o.fn is None:
                    continue
                inst = o.fn(eng)
                if o.is_dma:
                    inst.then_inc(dma_sems[o.key], o.inc)
                elif o.signal:
                    inst.then_inc(eng_sems[e], 1)

        @block.tensor
        def _(eng):
            run("pe", eng)

        @block.scalar
        def _(eng):
            run("act", eng)

        @block.vector
        def _(eng):
            run("dve", eng)

        @block.gpsimd
        def _(eng):
            run("pool", eng)

        @block.sync
        def _(eng):
            run("sp", eng)


def _wblk(h, s):
    if s in (1, 2):
        return h * 2 + (s - 1)
    return 2 * H + h * 3 + {0: 0, 3: 1, 4: 2}[s]


def _layout_weights(conv_in_w, conv_out_w, mlstm_in_w, mlstm_out_w):
    ci = conv_in_w[0].reshape(8, 128, 4, 16, 128)
    wcin = np.ascontiguousarray(ci.transpose(3, 1, 0, 2, 4)).reshape(16 * 128, RSLOT)
    wcout = np.ascontiguousarray(conv_out_w[0].reshape(16, 128, D).transpose(1, 0, 2)).reshape(128, 16 * D)
    wm = mlstm_in_w[0]
    wmin = np.zeros((H, 5, 128, RSLOT), np.float32)
    for h in range(H):
        q = wm[:, h * DK:(h + 1) * DK].reshape(8, 128, 2, 128)
        k = wm[:, D + h * DK:D + (h + 1) * DK].reshape(8, 128, 2, 128)
        qk = np.concatenate([q, k], axis=2)
        wmin[h, 0] = qk.transpose(1, 0, 2, 3).reshape(128, RSLOT)
        kt = wm[:, D + h * DK:D + (h + 1) * DK].reshape(8, 128, DK).transpose(1, 0, 2)
        wmin[h, 1, :, :8 * DK] = kt.reshape(128, 8 * DK)
        for s, base in ((2, 2 * D), (3, 2 * D + E), (4, 2 * D + 2 * E)):
            blk = wm[:, base + h * DV:base + (h + 1) * DV].reshape(8, 128, DV).transpose(1, 0, 2)
            wmin[h, s] = blk.reshape(128, RSLOT)
    wperm = np.zeros((H * 5, 128, RSLOT), np.float32)
    for h in range(H):
        for sl in range(5):
            wperm[_wblk(h, sl)] = wmin[h, sl]
    wmin = wperm.reshape(H * 5 * 128, RSLOT)
    wg = np.ascontiguousarray(wm[:, 2 * D + 3 * E:].reshape(8, 128, 8).transpose(1, 0, 2)).reshape(128, 64)
    wmout = np.ascontiguousarray(mlstm_out_w[0].reshape(16, 128, D).transpose(1, 0, 2)).reshape(128, 16 * D)
    return wcin, wcout, wmin, wg, wmout


def _layout_params(norm_w, conv_w, gate_b, head_norm_w, final_norm_w, pre_flag, state_flag):
    p = np.zeros((128, PCOLS), np.float32)
    p[:, P_ID:P_ID + 128] = np.eye(128, dtype=np.float32)
    p[:, P_TRI:P_TRI + 128] = np.triu(np.ones((128, 128), np.float32))
    p[:, P_ONES:P_ONES + 128] = 1.0
    for l in range(2):
        p[:, P_NW + 8 * l:P_NW + 8 * l + 8] = norm_w[l].reshape(8, 128).T
    for k in range(3):
        p[:, P_CW + 16 * k:P_CW + 16 * k + 16] = conv_w[0, k].reshape(16, 128).T
    p[:, P_GB:P_GB + 8] = gate_b[0][None, :]
    p[:, P_FL] = pre_flag
    p[:, P_FL + 1] = state_flag
    p[:NPRE, P_PM] = pre_flag
    p[:, P_FNW:P_FNW + D] = final_norm_w[None, :]
    p[:, P_HNW:P_HNW + E] = head_norm_w[0][None, :]
    return p


def _tiles(n_main):
    tiles = []
    for i in range(n_main):
        if i == 0:
            tiles.append(dict(t0=0, W=NPRE + TT, segs=[(0, NPRE), (NPRE, TT)],
                              subs=[(0, NPRE)] + [(NPRE + 128 * j, 128) for j in range(4)]))
        else:
            tiles.append(dict(t0=NPRE + TT * i, W=TT, segs=[(0, TT)],
                              subs=[(128 * j, 128) for j in range(4)]))
    return tiles


def build_program(mode, n_main=8, n_cores=8):
    do1 = mode in ("p1", "fused")
    do2 = mode in ("p2", "fused")
    ntok = NPRE + TT * n_main
    tiles = _tiles(n_main)
    nc = bass.Bass("TRN2", target_bir_lowering=False)
    S = Sched()
    es = ExitStack()
    bufs = {}

    def B(name):
        if name not in bufs:
            bufs[name] = Buf(name)
        return bufs[name]

    def dram(name, shape, dt, kind):
        return nc.dram_tensor(name, shape, dt, kind=kind).ap()

    def sb(name, shape, dt):
        return es.enter_context(nc.sbuf_tensor(name, shape, dt))

    params_d = dram("params", [128, PCOLS], F32, "ExternalInput")
    if do1:
        xin_d = dram("xin", [ntok, D], F32, "ExternalInput")
        wcin_d = dram("wcin", [16 * 128, RSLOT], F32, "ExternalInput")
        wcout_d = dram("wcout", [128, 16 * D], F32, "ExternalInput")
        wcin_b = dram("wcin_b", [16 * 128, RSLOT], BF16, "Internal")
        wcout_b = dram("wcout_b", [128, 16 * D], BF16, "Internal")
    wmin_d = dram("wmin", [H * 5 * 128, RSLOT], F32, "ExternalInput")
    wg_d = dram("wg", [128, 64], F32, "ExternalInput")
    wmin_b = dram("wmin_b", [H * 5 * 128, RSLOT], BF16, "Internal")
    if do2:
        wmout_d = dram("wmout", [128, 16 * D], F32, "ExternalInput")
        wmout_b = dram("wmout_b", [128, 16 * D], BF16, "Internal")
        out_d = dram("out", [ntok - NPRE, D], F32, "ExternalOutput")
    NST = H * 2 * DV + 8
    if mode == "p1":
        h1_d = dram("h1", [ntok, D], F32, "ExternalOutput")
        st_d = dram("st_out", [128, NST], F32, "ExternalOutput")
    elif mode == "p2":
        h1_d = dram("h1", [ntok, D], F32, "ExternalInput")
        st_d = dram("st_in", [128, NST], F32, "ExternalInput")
    else:
        h1_d = dram("h1", [ntok, D], F32, "Internal")
        kv_d = dram("kv_scr", [H, ntok, DK + DV], BF16, "Internal")
        NT_ = H * 2 * DV
        st_locT = nc.dram_tensor("st_locT", [128, NT_], F32).ap()
        st_locN = nc.dram_tensor("st_locN", [128, 8], F32).ap()
        st_allT = nc.dram_tensor("st_allT", [256, NT_], F32).ap()
        st_allN = nc.dram_tensor("st_allN", [256, 8], F32).ap()

    prm = sb("prm", [128, PCOLS], F32)
    idb = sb("idb", [128, 128], BF16)
    xt = [sb(f"xt{j}", [128, D], F32) for j in range(5)]
    junk = sb("junk", [128, D], BF16)
    xs0 = sb("xs0", [128, D], BF16)
    xs = [xs0, xs0]
    stat = sb("stat", [128, 64], F32)
    uT = sb("uT", [128, 8, NPRE + TT], BF16)
    u1T = sb("u1T", [128, 8, NPRE + TT], BF16)
    gT = sb("gT", [128, 16, NPRE + TT], BF16)
    ring = [sb(f"ring{k}", [128, RSLOT], BF16) for k in range(NRING)]
    wout = sb("wout", [128, 16, D], BF16)
    wgs = sb("wgs", [128, 8, 8], BF16)
    tA = sb("tA", [128, TT], F32)
    tB = sb("tB", [128, TT], F32)
    if do1:
        cx = sb("cx", [128, NPRE + TT + 2], F32)
        bg = sb("bg", [128, NPRE + TT], F32)
        yv = sb("yv", [128, NPRE + TT], F32)
        hal = sb("hal", [128, 16, 2], F32)
    nset = 2 if do2 else 1
    kh2 = [[sb(f"kh{st}_{j}", [128, DK], BF16) for j in range(5)] for st in range(nset)]
    vh2 = [[sb(f"vh{st}_{j}", [128, DV], BF16) for j in range(5)] for st in range(nset)]
    k_h, v_h = kh2[0], vh2[0]
    vp = [sb(f"vp{k}", [128, DV], BF16) for k in range(2)]
    gt = sb("gt", [128, 5, 8], F32)
    nlf = sb("nlf", [128, 5, 4], F32)
    gtmp = sb("gtmp", [128, 5, 4], F32)
    ee = sb("ee", [128, 5, 4], F32)
    eeb = sb("eeb", [128, 5, 4], BF16)
    av = sb("av", [128, 5, 4], F32)
    dec = sb("dec", [128, 11, 4], F32)
    Tst = sb("Tst", [128, H, 2, DV], F32)
    Tn = sb("Tn", [128, 8], F32)
    if do2:
        qT2 = [uT[:, 4 * st:4 * st + 2, :] for st in range(2)]
        kT2 = [uT[:, 4 * st + 2:4 * st + 4, :] for st in range(2)]
        p2buf = sb("p2buf", [128, 32 * DV], BF16)
        pv = lambda i: p2buf[:, i * DV:(i + 1) * DV]
        tw2 = [[pv(st * 5 + j) for j in range(5)] for st in range(2)]
        wgh2 = [[pv(10 + st * 5 + j) for j in range(5)] for st in range(2)]
        smT = [sb(f"smT{k}", [128, 128], BF16) for k in range(2)]
        hg = [pv(25 + k) for k in range(2)]
        stg = [p2buf[:, g * 8192:(g + 1) * 8192].bitcast(F32) for g in range(2)]
        Sbf = sb("Sbf", [128, 2, DV], BF16)
        nbf = sb("nbf", [128, 8], BF16)
        yout = p2buf[:, 28 * DV:32 * DV].bitcast(F32)
        sstat = sb("sstat", [128, 16], F32)
        gctr = [0]
        if do1:
            cxb, bgb = cx[:].bitcast(BF16), bg[:].bitcast(BF16)
            tG = [cxb[:, 0:DV], cxb[:, DV:2 * DV], bgb[:, 0:DV], bgb[:, DV:2 * DV]]
        else:
            tG = [sb(f"tG{k}", [128, DV], BF16) for k in range(4)]
        bst = sb("bst", [128, 64], F32)
        numS = [pv(20 + j) for j in range(5)]
    ps = [es.enter_context(nc.psum_tensor(f"ps{i}", [128, 512], F32)) for i in range(8)]
    eng_sems = {e: es.enter_context(nc.semaphore(f"s_{e}")) for e in ENGS}

    bank_ctr = [0]
    nbig = [6]

    def bank():
        i = bank_ctr[0] % nbig[0]
        bank_ctr[0] += 1
        return ps[i], B(f"ps{i}")

    def bank_fixed(i):
        return ps[i], B(f"ps{i}")

    small = ps[6]
    small2 = ps[7]

    ident = prm[:, P_ID:P_ID + 128]
    tri = prm[:, P_TRI:P_TRI + 128]
    ones = prm[:, P_ONES:P_ONES + 128]

    dma_keys = set()

    def dma(q, out, in_, key, reads, writes, **kw):
        dma_keys.add(key)
        return S.dma(q, lambda e: e.dma_start(out=out, in_=in_, **kw), key, reads=reads, writes=writes)

    dma("sp", prm[:], params_d, "prm", [], [B("prm")])
    S.op("dve", lambda e: e.tensor_copy(out=idb[:], in_=ident), [B("prm")], [B("idb")])
    dma("pool", wgs[:].rearrange("p c g -> p (c g)"), wg_d, "wgs", [], [B("wgs")])

    def cast_dram(src, dst, rows, name, r0=0, r1=None, step=512):
        s2 = src.rearrange("r (a k) -> (r a) k", k=2048)
        d2 = dst.rearrange("r (a k) -> (r a) k", k=2048)
        per = src.shape[1] // 2048
        chunks = []
        r1 = rows if r1 is None else r1
        for lo in range(r0, r1, step):
            hi = min(r1, lo + step)
            b = B(f"{name}_c{lo}")
            dma("pool", d2[lo * per:hi * per, :], s2[lo * per:hi * per, :], f"{name}_c{lo}", [], [b])
            chunks.append((lo, hi, b))
        return chunks

    def chunk_bufs(chunks, lo, hi):
        return [b for (a, z, b) in chunks if a < hi and z > lo]

    STAGED = mode == "fused"
    cin_chunks, cout_chunks, min_chunks = [], [], []
    if do1 and not STAGED:
        cin_chunks = cast_dram(wcin_d, wcin_b, 16 * 128, "wcin", 0, 256, step=128)
        cin_chunks += cast_dram(wcin_d, wcin_b, 16 * 128, "wcin", 256, 2048, step=256)
        cout_chunks = cast_dram(wcout_d, wcout_b, 128, "wcout")
    if not STAGED:
        min_chunks = cast_dram(wmin_d, wmin_b, H * 5 * 128, "wmin", 0, 2 * H * 128, step=256)
    late = []
    for r0_ in range(2 * H * 128, H * 5 * 128, 384):
        late.append(lambda r0_=r0_: min_chunks.extend(
            cast_dram(wmin_d, wmin_b, H * 5 * 128, "wmin", r0_, min(r0_ + 384, H * 5 * 128), step=384)))
    if do2:
        late.append(lambda: mout_chunks.extend(cast_dram(wmout_d, wmout_b, 128, "wmout")))

    def late_casts(n=None):
        k = len(late) if n is None else min(n, len(late))
        for _ in range(k):
            late.pop(0)()
    mout_chunks = []
    if not STAGED:
        late_casts()

    plan = []
    for ti in range(n_main):
        if do1:
            for fc in range(16):
                plan.append(("cin", wcin_b, fc * 128, cin_chunks))
            for h in range(H):
                for s in (1, 2):
                    plan.append((f"m{s}", wmin_b, _wblk(h, s) * 128, min_chunks))
    n_p1 = len(plan)
    if do2:
        for ti in range(n_main):
            for h in range(H):
                for s in ((0, 3, 4) if mode == "fused" else range(5)):
                    plan.append((f"m{s}", wmin_b, _wblk(h, s) * 128, min_chunks))
    issued = [0]
    taken = [0]

    def staged_cast(dst_flat, g, dbuf):
        cuts = [(0, 1024, "pool"), (1024, 2560, "act"), (2560, 4096, "dve")]
        for a, b_, eng in cuts:
            if eng == "act":
                S.op("act", lambda e, a=a, b_=b_: e.activation(out=dst_flat[:, a:b_], in_=stg[g][:, a:b_],
                                                               func=AF.Copy), [B(f"stg{g}")], [dbuf])
            else:
                S.op(eng, lambda e, a=a, b_=b_: e.tensor_copy(out=dst_flat[:, a:b_], in_=stg[g][:, a:b_]),
                     [B(f"stg{g}")], [dbuf])

    n_stage = (16 + 2 * H) if (STAGED and do1) else 0
    wbmap = {}
    stgc = [0]

    def ring_issue(upto):
        while issued[0] < min(upto, len(plan)):
            k = issued[0]
            name, src, lo, chunks = plan[k]
            slot = k % NRING
            if k < n_stage:
                g = stgc[0] % 2
                stgc[0] += 1
                src32 = wcin_d if name == "cin" else wmin_d
                dma("sp", stg[g], src32[lo:lo + 128, :], f"stg{g}", [], [B(f"stg{g}")])
                staged_cast(ring[slot][:], g, B(f"ring{slot}"))
                wb = B(f"wb_{name}_{lo}")
                wbmap[(name == "cin", lo)] = wb
                dma("pool", src[lo:lo + 128, :], ring[slot][:], f"wbk{slot}", [B(f"ring{slot}")], [wb])
            else:
                key = (name == "cin", lo)
                deps = [wbmap[key]] if key in wbmap else chunk_bufs(chunks, lo, lo + 128)
                dma("sp", ring[slot][:], src[lo:lo + 128, :], f"ring{slot}", deps, [B(f"ring{slot}")])
            issued[0] += 1

    def ring_next(name):
        k = taken[0]
        assert plan[k][0] == name, (plan[k][0], name)
        ring_issue(k + NRING)
        taken[0] += 1
        slot = k % NRING
        return ring[slot], B(f"ring{slot}")

    def norm_to_uT(tile, layer, dstT, dstname, src_loader, only=None):
        for j, (off, n) in enumerate(tile["subs"]):
            if only is not None and j not in only:
                continue
            if src_loader is not None:
                src_loader(j, off, n)
            ssj, rsj = B(f"ss{j}"), B(f"rs{j}")
            S.op("act", lambda e, j=j, n=n: (e.activation(out=junk[:n, :], in_=xt[j][:n, :], func=AF.Square,
                                                           accum_out=stat[:n, j:j + 1]), e.drain())[-1],
                 [B(f"xt{j}")], [B("junk"), ssj])
            S.op("act", lambda e, j=j, n=n: e.activation(out=stat[:n, 8 + j:9 + j], in_=stat[:n, j:j + 1],
                                                          func=AF.Sqrt, scale=1.0 / D, bias=EPS), [ssj], [rsj])
            S.op("dve", lambda e, j=j, n=n: e.reciprocal(out=stat[:n, 8 + j:9 + j], in_=stat[:n, 8 + j:9 + j]),
                 [rsj], [rsj])
            k = 0
            S.op("dve", lambda e, j=j, n=n, k=k: e.tensor_scalar(out=xs[k][:n, :], in0=xt[j][:n, :],
                                                                  scalar1=stat[:n, 8 + j:9 + j], scalar2=None,
                                                                  op0=ALU.mult),
                 [B(f"xt{j}"), rsj], [B(f"xs{k}")])
            pt, pb = bank()
            ptv = pt[:].bitcast(BF16).rearrange("p (c t) -> p c t", c=8)

            def tr(e, n=n, k=k, ptv=ptv):
                i = None
                for c in range(8):
                    i = e.transpose(out=ptv[:, c, :n], in_=xs[k][:n, c * 128:(c + 1) * 128], identity=idb[:n, :n])
                return i
            S.op("pe", tr, [B(f"xs{k}"), B("idb")], [pb])
            nwb = prm[:, P_NW + 8 * layer:P_NW + 8 * layer + 8].unsqueeze(2).to_broadcast([128, 8, n])
            S.op("act" if False else "dve",
                 lambda e, n=n, off=off, ptv=ptv, nwb=nwb: e.tensor_tensor(out=dstT[:, :, off:off + n],
                                                                           in0=ptv[:, :, :n], in1=nwb, op=ALU.mult),
                 [pb, B("prm")], [B(dstname)])

    def load_x(src_d, t0):
        def f(j, off, n):
            extra = [b for k, b in bufs.items() if "_c" in k] if CASTBAR else []
            dma("sp", xt[j][:n, :], src_d[t0 + off:t0 + off + n, :], f"xt{j}", extra, [B(f"xt{j}")])
        return f

    def load_h1(t0, ti):
        def f(j, off, n):
            dma("sp", xt[j][:n, :], h1_d[t0 + off:t0 + off + n, :], f"xt{j}", [B(f"h1d_{ti}_{j}")],
                [B(f"xt{j}")])
        return f

    def gates_stage(tile, first_tile, dbase):
        subs_ = tile["subs"]
        nc_ = len(subs_)
        sm = B("ps6")
        allj = lambda nm: [B(f"{nm}{j}") for j in range(nc_)]

        def gate_mm(e):
            i = None
            for j, (off, n) in enumerate(subs_):
                if first_tile and j == 0:
                    off, n = 0, 128
                for c in range(8):
                    i = e.matmul(small[:n, 8 * j:8 * j + 8], lhsT=u1T[:, c, off:off + n], rhs=wgs[:, c, :],
                                 start=(c == 0), stop=(c == 7))
            return i
        S.op("pe", gate_mm, [B("u1T"), B("wgs")], [sm])
        gb_b = prm[:, P_GB:P_GB + 8].unsqueeze(1).to_broadcast([128, nc_, 8])
        S.op("dve", lambda e: e.tensor_tensor(out=gt[:, 0:nc_, :],
                                              in0=small[:, 0:8 * nc_].rearrange("p (j g) -> p j g", g=8),
                                              in1=gb_b, op=ALU.add), [sm, B("prm")], allj("gt"))
        S.op("act", lambda e: e.activation(out=gtmp[:, 0:nc_, :], in_=gt[:, 0:nc_, 4:8], func=AF.Exp, scale=-1.0),
             allj("gt"), allj("gtmp"))
        S.op("act", lambda e: e.activation(out=nlf[:, 0:nc_, :], in_=gtmp[:, 0:nc_, :], func=AF.Ln, bias=1.0),
             allj("gtmp"), allj("nlf"))
        if first_tile:
            S.op("dve", lambda e: e.tensor_scalar(out=nlf[:, 0, :], in0=nlf[:, 0, :], scalar1=prm[:, P_PM:P_PM + 1],
                                                  scalar2=None, op0=ALU.mult), [B("nlf0"), B("prm")], [B("nlf0")])
        nlf_flat = nlf[:, 0:nc_, :].rearrange("p j g -> p (j g)")
        S.op("pe", lambda e: (e.matmul(small[:, 40:40 + 4 * nc_], lhsT=tri, rhs=nlf_flat, start=True, stop=True),
                              e.matmul(small[:, 64:64 + 4 * nc_], lhsT=ones, rhs=nlf_flat, start=True, stop=True))[-1],
             allj("nlf") + [B("prm")], [sm])
        cs = small[:, 40:40 + 4 * nc_].rearrange("p (j g) -> p j g", g=4)
        tot = small[:, 64:64 + 4 * nc_].rearrange("p (j g) -> p j g", g=4)
        S.op("act", lambda e: e.activation(out=av[:, 0:nc_, :], in_=cs, func=AF.Exp, scale=-1.0), [sm], allj("av"))
        S.op("dve", lambda e: e.tensor_tensor(out=gtmp[:, 0:nc_, :], in0=cs, in1=gt[:, 0:nc_, 0:4], op=ALU.add),
             [sm] + allj("gt"), allj("gtmp"))
        S.op("act", lambda e: e.activation(out=ee[:, 0:nc_, :], in_=gtmp[:, 0:nc_, :], func=AF.Exp),
             allj("gtmp"), allj("ee"))
        if first_tile:
            S.op("dve", lambda e: e.tensor_scalar(out=ee[:, 0, :], in0=ee[:, 0, :], scalar1=prm[:, P_PM:P_PM + 1],
                                                  scalar2=None, op0=ALU.mult), [B("ee0"), B("prm")], [B("ee0")])
        S.op("dve", lambda e: e.tensor_copy(out=eeb[:, 0:nc_, :], in_=ee[:, 0:nc_, :]), allj("ee"), allj("eeb"))
        S.op("act", lambda e: e.activation(out=dec[:, dbase:dbase + nc_, :], in_=tot, func=AF.Exp, scale=-1.0),
             [sm], [B(f"dec{dbase + j}") for j in range(nc_)])

    def proj_tok(tile, slot, sbuf, ncols, evac):
        sv = slot[:, 0:8 * ncols].rearrange("p (c n) -> p c n", c=8)
        for j, (off, n) in enumerate(tile["subs"]):
            pt, pb = bank()
            S.op("pe", lambda e, off=off, n=n, pt=pt, sv=sv: [e.matmul(pt[:n, :ncols], lhsT=u1T[:, c, off:off + n],
                                                                        rhs=sv[:, c, :], start=(c == 0), stop=(c == 7))
                                                               for c in range(8)][-1],
                 [B("u1T"), sbuf], [pb])
            evac(j, off, n, pt, pb)

    ONESI = 10
    prev_dec = {h: ONESI for h in range(H)}

    def state_update(j, n, h, k_src, v_src, st=0):
        pd = prev_dec[h]
        vk = (j + h) % 2
        S.op("act", lambda e, j=j, n=n, h=h, vk=vk: e.activation(out=vp[vk][:n, :], in_=v_src[j][:n, :], func=AF.Copy,
                                                                   scale=ee[:n, j, h:h + 1]),
             [B(f"vh{st}_{j}"), B(f"ee{j}")], [B(f"vp{vk}")])
        ups = []
        for half in range(2):
            pt, pb = bank()
            S.op("pe", lambda e, j=j, n=n, half=half, pt=pt, vk=vk: e.matmul(
                pt[:, :], lhsT=k_src[j][:n, half * 128:(half + 1) * 128], rhs=vp[vk][:n, :], start=True, stop=True),
                [B(f"kh{st}_{j}"), B(f"vp{vk}")], [pb])
            ups.append((pt, pb))
        sn = B("ps7")
        S.op("pe", lambda e, j=j, n=n, h=h: [e.matmul(small2[:, 8 + half:9 + half],
                                                      lhsT=k_src[j][:n, half * 128:(half + 1) * 128],
                                                      rhs=eeb[:n, j, h:h + 1], start=True, stop=True)
                                             for half in range(2)][-1],
             [B(f"kh{st}_{j}"), B(f"eeb{j}")], [sn])
        return ups, sn, pd

    def state_commit(j, h, ups, sn, pd, dbase):
        for half, (pt, pb) in enumerate(ups):
            S.op("dve", lambda e, h=h, half=half, pt=pt, pd=pd: e.scalar_tensor_tensor(
                out=Tst[:, h, half, :], in0=Tst[:, h, half, :], scalar=dec[:, pd, h:h + 1], in1=pt[:, :],
                op0=ALU.mult, op1=ALU.add), [B(f"T{h}"), pb, B(f"dec{pd}")], [B(f"T{h}")])
        S.op("dve", lambda e, h=h, pd=pd: e.scalar_tensor_tensor(
            out=Tn[:, 2 * h:2 * h + 2], in0=Tn[:, 2 * h:2 * h + 2], scalar=dec[:, pd, h:h + 1],
            in1=small2[:, 8:10], op0=ALU.mult, op1=ALU.add), [B(f"Tn{h}"), sn, B(f"dec{pd}")], [B(f"Tn{h}")])
        prev_dec[h] = dbase + j

    def copy_evac(dst_list, dname, ncols, eng="act"):
        def f(j, off, n, pt, pb):
            if eng == "act":
                S.op("act", lambda e: e.activation(out=dst_list[j][:n, :ncols], in_=pt[:n, :ncols], func=AF.Copy),
                     [pb], [B(f"{dname}{j}")])
            else:
                S.op("dve", lambda e: e.tensor_copy(out=dst_list[j][:n, :ncols], in_=pt[:n, :ncols]),
                     [pb], [B(f"{dname}{j}")])
        return f

    S.op("pool", lambda e: e.memset(dec[:, ONESI, :], 1.0), [], [B(f"dec{ONESI}")])
    if do1:
        S.op("pool", lambda e: e.memset(hal[:], 0.0), [], [B("hal")])
        S.op("pool", lambda e: e.memset(Tst[:].rearrange("p h a v -> p (h a v)"), 0.0), [],
             [B(f"T{h}") for h in range(H)])
        S.op("pool", lambda e: e.memset(Tn[:], 0.0), [], [B(f"Tn{h}") for h in range(H)])

    if do1:
        for ti, tile in enumerate(tiles):
            W_, t0 = tile["W"], tile["t0"]
            if STAGED and ti >= 1:
                late_casts(1)
            if DBG >= 2 and ti == 0:
                norm_to_uT(tile, 0, uT, "uT", load_x(xin_d, t0))
            nbig[0] = 8
            for fc in range(16 if DBG >= 3 else 0):
                if ti == 0 and STAGED and fc == 8:
                    wflat = wout[:].rearrange("p c n -> p (c n)")
                    for q in range(4):
                        g = stgc[0] % 2
                        stgc[0] += 1
                        dma("sp", stg[g], wcout_d[:, q * 4096:(q + 1) * 4096], f"stg{g}", [], [B(f"stg{g}")])
                        staged_cast(wflat[:, q * 4096:(q + 1) * 4096], g, B("wout"))
                slot, sbuf = ring_next("cin")
                sv = slot[:].rearrange("p (c g j) -> p c g j", c=8, g=4)
                S.op("pool", lambda e, fc=fc: e.tensor_copy(out=cx[:, 0:2], in_=hal[:, fc, :]), [B("hal")], [B("cx")])
                for (off, n) in tile["segs"]:
                    pbk = {}
                    for g in (1, 2, 3, 0):
                        pt, pb = bank()
                        pbk[g] = (pt, pb)
                        S.op("pe", lambda e, off=off, n=n, g=g, pt=pt, sv=sv: [e.matmul(
                            pt[:, :n], lhsT=sv[:, c, g, :], rhs=uT[:, c, off:off + n], start=(c == 0), stop=(c == 7))
                            for c in range(8)][-1], [B("uT"), sbuf], [pb])
                    S.op("act", lambda e, n=n, p=pbk[2][0]: e.activation(out=tA[:, :n], in_=p[:, :n], func=AF.Copy),
                         [pbk[2][1]], [B("tA")])
                    S.op("dve", lambda e, off=off, n=n, p=pbk[1][0]: e.tensor_tensor(
                        out=cx[:, 2 + off:2 + off + n], in0=p[:, :n], in1=tA[:, :n], op=ALU.mult),
                        [pbk[1][1], B("tA")], [B("cx")])
                    S.op("act", lambda e, n=n, p=pbk[3][0]: e.activation(out=tB[:, :n], in_=p[:, :n], func=AF.Silu),
                         [pbk[3][1]], [B("tB")])
                    S.op("dve", lambda e, off=off, n=n, p=pbk[0][0]: e.tensor_tensor(
                        out=bg[:, off:off + n], in0=p[:, :n], in1=tB[:, :n], op=ALU.mult),
                        [pbk[0][1], B("tB")], [B("bg")])
                cw = lambda k, fc=fc: prm[:, P_CW + 16 * k + fc:P_CW + 16 * k + fc + 1]
                S.op("act", lambda e, W_=W_, cw=cw: e.activation(out=yv[:, :W_], in_=cx[:, 2:2 + W_], func=AF.Copy,
                                                                  scale=cw(2)), [B("cx"), B("prm")], [B("yv")])
                S.op("dve", lambda e, W_=W_, cw=cw: e.scalar_tensor_tensor(
                    out=yv[:, :W_], in0=cx[:, 1:1 + W_], scalar=cw(1), in1=yv[:, :W_], op0=ALU.mult, op1=ALU.add),
                    [B("cx"), B("yv"), B("prm")], [B("yv")])
                S.op("dve", lambda e, W_=W_, cw=cw: e.scalar_tensor_tensor(
                    out=yv[:, :W_], in0=cx[:, 0:W_], scalar=cw(0), in1=yv[:, :W_], op0=ALU.mult, op1=ALU.add),
                    [B("cx"), B("yv"), B("prm")], [B("yv")])
                S.op("dve", lambda e, W_=W_, fc=fc: e.tensor_tensor(out=gT[:, fc, :W_], in0=yv[:, :W_],
                                                                   in1=bg[:, :W_], op=ALU.mult),
                     [B("yv"), B("bg")], [B("gT")])
                S.op("pool", lambda e, W_=W_, fc=fc: e.tensor_copy(out=hal[:, fc, :], in_=cx[:, W_:W_ + 2]),
                     [B("cx")], [B("hal")])
            nbig[0] = 6
            if False:
                wflat = wout[:].rearrange("p c n -> p (c n)")
                for q in range(4):
                    g = stgc[0] % 2
                    stgc[0] += 1
                    dma("sp", stg[g], wcout_d[:, q * 4096:(q + 1) * 4096], f"stg{g}", [], [B(f"stg{g}")])
                    staged_cast(wflat[:, q * 4096:(q + 1) * 4096], g, B("wout"))
            elif ti == 0 and not STAGED:
                dma("sp", wout[:].rearrange("p c n -> p (c n)"), wcout_b, "wout", chunk_bufs(cout_chunks, 0, 128),
                    [B("wout")])
            for j, (off, n) in enumerate(tile["subs"] if DBG >= 4 else []):
                for half in range(2):
                    pt, pb = bank()
                    S.op("pe", lambda e, off=off, n=n, half=half, pt=pt: [e.matmul(
                        pt[:n, :], lhsT=gT[:, fc, off:off + n], rhs=wout[:, fc, half * 512:(half + 1) * 512],
                        start=(fc == 0), stop=(fc == 15)) for fc in range(16)][-1], [B("gT"), B("wout")], [pb])
                    S.op("dve", lambda e, j=j, n=n, half=half, pt=pt: e.tensor_tensor(
                        out=xt[j][:n, half * 512:(half + 1) * 512], in0=pt[:n, :],
                        in1=xt[j][:n, half * 512:(half + 1) * 512], op=ALU.add), [pb, B(f"xt{j}")], [B(f"xt{j}")])
                dma("sp", h1_d[t0 + off:t0 + off + n, :], xt[j][:n, :], f"h1s{j}", [B(f"xt{j}")],
                    [B(f"h1d_{ti}_{j}")])
                if DBG >= 5 and j >= 1:
                    norm_to_uT(tile, 1, u1T, "u1T", None, only=[j - 1])
            if DBG >= 5:
                norm_to_uT(tile, 1, u1T, "u1T", None, only=[len(tile["subs"]) - 1])
            if DBG >= 6:
                gates_stage(tile, ti == 0, 5 * (ti % 2))
            def p1_proj(h, st):
                slot, sbuf = ring_next("m1")
                proj_tok(tile, slot, sbuf, DK, copy_evac(kh2[st], f"kh{st}_", DK, "act"))
                slot, sbuf = ring_next("m2")
                proj_tok(tile, slot, sbuf, DV, copy_evac(vh2[st], f"vh{st}_", DV, "dve"))
                if mode == "fused":
                    for j, (off, n) in enumerate(tile["subs"]):
                        dma("sp", kv_d[h, t0 + off:t0 + off + n, 0:DK], kh2[st][j][:n, :], f"ks{st}_{j}",
                            [B(f"kh{st}_{j}")], [B(f"kvd_{h}_{ti}_{j}")])
                        dma("sp", kv_d[h, t0 + off:t0 + off + n, DK:DK + DV], vh2[st][j][:n, :], f"vs{st}_{j}",
                            [B(f"vh{st}_{j}")], [B(f"kvd_{h}_{ti}_{j}")])

            def p1_state(h, st):
                for j, (off, n) in enumerate(tile["subs"]):
                    ups, sn, pd = state_update(j, n, h, kh2[st], vh2[st], st)
                    state_commit(j, h, ups, sn, pd, 5 * (ti % 2))
            nset1 = len(kh2)
            if ti + 1 < len(tiles):
                nt = tiles[ti + 1]
                ld = load_x(xin_d, nt["t0"])
                for j, (off, n) in enumerate(nt["subs"]):
                    ld(j, off, n)
            p1_proj(0, 0)
            for h in range(H):
                if h + 1 < H and nset1 > 1:
                    p1_proj(h + 1, (h + 1) % 2)
                p1_state(h, h % nset1)
                if h + 1 < H and nset1 == 1:
                    p1_proj(h + 1, 0)
                if h == 2 and ti + 1 < len(tiles):
                    norm_to_uT(tiles[ti + 1], 0, uT, "uT", None)
        late_casts()
        for h in range(H):
            pd = prev_dec[h]
            S.op("dve", lambda e, h=h, pd=pd: e.tensor_scalar(
                out=Tst[:, h, :, :].rearrange("p a v -> p (a v)"), in0=Tst[:, h, :, :].rearrange("p a v -> p (a v)"),
                scalar1=dec[:, pd, h:h + 1], scalar2=None, op0=ALU.mult), [B(f"T{h}"), B(f"dec{pd}")], [B(f"T{h}")])
            S.op("dve", lambda e, h=h, pd=pd: e.tensor_scalar(
                out=Tn[:, 2 * h:2 * h + 2], in0=Tn[:, 2 * h:2 * h + 2], scalar1=dec[:, pd, h:h + 1], scalar2=None,
                op0=ALU.mult), [B(f"Tn{h}"), B(f"dec{pd}")], [B(f"Tn{h}")])
            prev_dec[h] = ONESI
        Tall = [B(f"T{h}") for h in range(H)]
        Tnall = [B(f"Tn{h}") for h in range(H)]
        if mode == "p1":
            dT, dN = st_d[:, 0:H * 2 * DV], st_d[:, H * 2 * DV:NST]
        else:
            dT, dN = st_locT, st_locN
        dma("sp", dT, Tst[:].rearrange("p h a v -> p (h a v)"), "st_s", Tall, [B("st_d")])
        dma("sp", dN, Tn[:], "st_s", Tnall, [B("st_d")])

    if mode == "fused":
        dma_keys.add("ccT")
        dma_keys.add("ccN")
        groups = [[2 * i, 2 * i + 1] for i in range(n_cores // 2)]
        S.dma("pool", lambda e: e.collective_compute("AllGather", ALU.bypass, replica_groups=groups,
                                                     ins=[st_locT], outs=[st_allT]),
              "ccT", reads=[B("st_d")], writes=[B("st_all")], inc=1)
        S.dma("pool", lambda e: e.collective_compute("AllGather", ALU.bypass, replica_groups=groups,
                                                     ins=[st_locN], outs=[st_allN]),
              "ccN", reads=[B("st_d")], writes=[B("st_all")], inc=1)
        srcT, srcN = st_allT[0:128, :], st_allN[0:128, :]
        st_dep = [B("st_all")]
    elif mode == "p2":
        srcT, srcN = st_d[:, 0:H * 2 * DV], st_d[:, H * 2 * DV:NST]
        st_dep = []
    if do2:
        Tall = [B(f"T{h}") for h in range(H)]
        Tnall = [B(f"Tn{h}") for h in range(H)]
        dma("sp", Tst[:].rearrange("p h a v -> p (h a v)"), srcT, "st_l", st_dep, Tall)
        dma("sp", Tn[:], srcN, "st_l", st_dep, Tnall)
        for h in range(H):
            S.op("dve", lambda e, h=h: e.tensor_scalar(
                out=Tst[:, h, :, :].rearrange("p a v -> p (a v)"), in0=Tst[:, h, :, :].rearrange("p a v -> p (a v)"),
                scalar1=prm[:, P_FL + 1:P_FL + 2], scalar2=None, op0=ALU.mult), [B(f"T{h}"), B("prm")], [B(f"T{h}")])
            S.op("dve", lambda e, h=h: e.tensor_scalar(
                out=Tn[:, 2 * h:2 * h + 2], in0=Tn[:, 2 * h:2 * h + 2], scalar1=prm[:, P_FL + 1:P_FL + 2],
                scalar2=None, op0=ALU.mult), [B(f"Tn{h}"), B("prm")], [B(f"Tn{h}")])
        dma("sp", wout[:].rearrange("p c n -> p (c n)"), wmout_b, "wout", chunk_bufs(mout_chunks, 0, 128),
            [B("wout")])

    if do2:
        hnw = prm[:, P_HNW:P_HNW + E]
        fnw = prm[:, P_FNW:P_FNW + D]
        nbig[0] = 6
        for ti, tile in enumerate(tiles):
            W_, t0 = tile["W"], tile["t0"]
            norm_to_uT(tile, 1, u1T, "u1T", load_h1(t0, ti))
            gates_stage(tile, ti == 0, 5 * (ti % 2))
            def P_items(h, st):
                items = []
                box = {}

                def it_qk(blk, off, n):
                    def f():
                        if "qk" not in box:
                            box["qk"] = ring_next("m0")
                            if mode == "fused":
                                for j, (o2, n2) in enumerate(tile["subs"]):
                                    dma("sp", kh2[st][j][:n2, :], kv_d[h, t0 + o2:t0 + o2 + n2, 0:DK], f"kl{st}_{j}",
                                        [B(f"kvd_{h}_{ti}_{j}")], [B(f"kh{st}_{j}")])
                                    dma("sp", vh2[st][j][:n2, :], kv_d[h, t0 + o2:t0 + o2 + n2, DK:DK + DV],
                                        f"vl{st}_{j}", [B(f"kvd_{h}_{ti}_{j}")], [B(f"vh{st}_{j}")])
                        slot, sbuf = box["qk"]
                        sv = slot[:].rearrange("p (c g j) -> p c g j", c=8, g=4)
                        pt, pb = bank()
                        S.op("pe", lambda e: [e.matmul(pt[:, :n], lhsT=sv[:, c, blk, :], rhs=u1T[:, c, off:off + n],
                                                       start=(c == 0), stop=(c == 7)) for c in range(8)][-1],
                             [B("u1T"), sbuf], [pb])
                        if blk < 2:
                            S.op("act", lambda e: e.activation(out=qT2[st][:, blk, off:off + n], in_=pt[:, :n],
                                                               func=AF.Copy, scale=DK ** -0.5), [pb], [B(f"qT{st}")])
                        else:
                            S.op("dve", lambda e: e.tensor_copy(out=kT2[st][:, blk - 2, off:off + n], in_=pt[:, :n]),
                                 [pb], [B(f"kT{st}")])
                    return f
                for blk in range(4):
                    for (off, n) in tile["segs"]:
                        items.append(it_qk(blk, off, n))

                def it_tok(name, j, off, n):
                    def f():
                        if name not in box:
                            box[name] = ring_next(name)
                        slot, sbuf = box[name]
                        sv = slot[:].rearrange("p (c n) -> p c n", c=8)
                        pt, pb = bank()
                        S.op("pe", lambda e: [e.matmul(pt[:n, :], lhsT=u1T[:, c, off:off + n], rhs=sv[:, c, :],
                                                       start=(c == 0), stop=(c == 7)) for c in range(8)][-1],
                             [B("u1T"), sbuf], [pb])
                        if name == "m1":
                            S.op("act", lambda e: e.activation(out=kh2[st][j][:n, :], in_=pt[:n, :DK], func=AF.Copy),
                                 [pb], [B(f"kh{st}_{j}")])
                        elif name == "m2":
                            S.op("dve", lambda e: e.tensor_copy(out=vh2[st][j][:n, :], in_=pt[:n, :]),
                                 [pb], [B(f"vh{st}_{j}")])
                        elif name == "m3":
                            g = gctr[0] % 2
                            gctr[0] += 1
                            tg, tgb = tG[g], B(f"tG{g}")
                            S.op("act", lambda e: e.activation(out=tg[:n, :], in_=pt[:n, :], func=AF.Tanh, scale=0.5),
                                 [pb], [tgb])
                            S.op("dve", lambda e: e.scalar_tensor_tensor(
                                out=tw2[st][j][:n, :], in0=tg[:n, :], scalar=1.0, in1=hnw[:n, h * DV:(h + 1) * DV],
                                op0=ALU.add, op1=ALU.mult), [tgb, B("prm")], [B(f"tw{st}_{j}")])
                        else:
                            g = gctr[0] % 2
                            gctr[0] += 1
                            tg, tgb = tG[2 + g], B(f"tG{2 + g}")
                            S.op("act", lambda e: e.activation(out=tg[:n, :], in_=pt[:n, :], func=AF.Tanh, scale=0.5),
                                 [pb], [tgb])
                            S.op("dve", lambda e: e.scalar_tensor_tensor(
                                out=tg[:n, :], in0=tg[:n, :], scalar=1.0, in1=tw2[st][j][:n, :], op0=ALU.add,
                                op1=ALU.mult), [tgb, B(f"tw{st}_{j}")], [tgb])
                            S.op("dve", lambda e: e.scalar_tensor_tensor(
                                out=wgh2[st][j][:n, :], in0=pt[:n, :], scalar=0.25, in1=tg[:n, :], op0=ALU.mult,
                                op1=ALU.mult), [pb, tgb], [B(f"wgh{st}_{j}")])
                    return f
                for name in (("m3", "m4") if mode == "fused" else ("m1", "m2", "m3", "m4")):
                    for j, (off, n) in enumerate(tile["subs"]):
                        items.append(it_tok(name, j, off, n))
                return items

            def S_gen(h, st):
                qT, kT = qT2[st], kT2[st]
                ctx = {}

                def stage_a1(j, off, n):
                    vk = j % 2
                    S.op("act", lambda e: e.activation(out=vp[vk][:n, :], in_=vh2[st][j][:n, :], func=AF.Copy,
                                                       scale=ee[:n, j, h:h + 1]),
                         [B(f"vh{st}_{j}"), B(f"ee{j}")], [B(f"vp{vk}")])
                    pS, pSb = bank()
                    S.op("pe", lambda e: [e.matmul(
                        pS[:n, :n], lhsT=kT[:, a, off:off + n], rhs=qT[:, a, off:off + n], start=(a == 0),
                        stop=(a == 1)) for a in range(2)][-1], [B(f"kT{st}"), B(f"qT{st}")], [pSb])
                    mk = j % 2
                    S.op("dve", lambda e: e.tensor_tensor(
                        out=smT[mk][:n, :n], in0=pS[:n, :n], in1=tri[:n, :n], op=ALU.mult),
                        [pSb, B("prm")], [B(f"smT{mk}")])
                    yield

                def stage_a(j, off, n):
                    pd = prev_dec[h]
                    vk = mk = j % 2
                    kx = kh2[st]
                    ups = []
                    for half in range(2):
                        pt, pb = bank()
                        S.op("pe", lambda e, half=half, pt=pt: e.matmul(
                            pt[:, :], lhsT=kx[j][:n, half * 128:(half + 1) * 128], rhs=vp[vk][:n, :], start=True,
                            stop=True), [B(f"kh{st}_{j}"), B(f"vp{vk}")], [pb])
                        ups.append((pt, pb))
                    sn = B("ps7")
                    S.op("pe", lambda e: [e.matmul(small2[:, 8 + half:9 + half],
                                                   lhsT=kx[j][:n, half * 128:(half + 1) * 128],
                                                   rhs=eeb[:n, j, h:h + 1], start=True, stop=True)
                                          for half in range(2)][-1], [B(f"kh{st}_{j}"), B(f"eeb{j}")], [sn])
                    S.op("act", lambda e, pd=pd: e.activation(
                        out=Sbf[:].rearrange("p a v -> p (a v)"), in_=Tst[:, h, :, :].rearrange("p a v -> p (a v)"),
                        func=AF.Copy, scale=dec[:, pd, h:h + 1]), [B(f"T{h}"), B(f"dec{pd}")], [B("Sbf")])
                    S.op("dve", lambda e, pd=pd: e.tensor_scalar(
                        out=nbf[:, 2 * h:2 * h + 2], in0=Tn[:, 2 * h:2 * h + 2], scalar1=dec[:, pd, h:h + 1],
                        scalar2=None, op0=ALU.mult), [B(f"Tn{h}"), B(f"dec{pd}")], [B("nbf")])
                    yield
                    pN, pNb = bank()

                    def num_mm(e):
                        e.matmul(pN[:n, :], lhsT=smT[mk][:n, :n], rhs=vp[vk][:n, :], start=True, stop=False)
                        e.matmul(pN[:n, :], lhsT=qT[:, 0, off:off + n], rhs=Sbf[:, 0, :], start=False, stop=False)
                        return e.matmul(pN[:n, :], lhsT=qT[:, 1, off:off + n], rhs=Sbf[:, 1, :], start=False,
                                        stop=True)
                    S.op("pe", num_mm, [B(f"smT{mk}"), B(f"vp{vk}"), B(f"qT{st}"), B("Sbf")], [pNb])
                    sd = B("ps7")
                    dc = j % 2

                    def den_mm(e):
                        e.matmul(small2[:n, dc:dc + 1], lhsT=smT[mk][:n, :n], rhs=eeb[:n, j, h:h + 1], start=True,
                                 stop=False)
                        e.matmul(small2[:n, dc:dc + 1], lhsT=qT[:, 0, off:off + n], rhs=nbf[:, 2 * h:2 * h + 1],
                                 start=False, stop=False)
                        return e.matmul(small2[:n, dc:dc + 1], lhsT=qT[:, 1, off:off + n],
                                        rhs=nbf[:, 2 * h + 1:2 * h + 2], start=False, stop=True)
                    S.op("pe", den_mm, [B(f"smT{mk}"), B(f"eeb{j}"), B(f"qT{st}"), B("nbf")], [sd])
                    state_commit(j, h, ups, sn, pd, 5 * (ti % 2))
                    S.op("act", lambda e: e.activation(
                        out=bst[:n, j:j + 1], in_=small2[:n, dc:dc + 1], func=AF.Abs,
                        scale=av[:n, j, h:h + 1]), [sd, B(f"av{j}")], [B(f"bst_d{j}")])
                    S.op("act", lambda e: (e.activation(out=junk[:n, :DV], in_=pN[:n, :], func=AF.Square,
                                                        accum_out=bst[:n, 8 + j:9 + j]), e.drain())[-1],
                         [pNb], [B("junk"), B(f"bst_q{j}")])
                    S.op("act", lambda e: e.activation(out=numS[j][:n, :], in_=pN[:n, :], func=AF.Copy),
                         [pNb], [B(f"numS{j}")])
                    yield

                def stage_b_all():
                    subs_ = tile["subs"]
                    nc_ = len(subs_)
                    rd = [B(f"bst_d{j}") for j in range(nc_)] + [B(f"bst_q{j}") for j in range(nc_)]
                    bb = B("bst")
                    avh = av[:, 0:nc_, h]
                    S.op("dve", lambda e: e.tensor_scalar(out=bst[:, 16:16 + nc_], in0=bst[:, 0:nc_], scalar1=1.0,
                                                          scalar2=None, op0=ALU.max), rd, [bb])
                    S.op("dve", lambda e: e.reciprocal(out=bst[:, 16:16 + nc_], in_=bst[:, 16:16 + nc_]), [bb], [bb])
                    S.op("dve", lambda e: e.tensor_tensor(out=bst[:, 24:24 + nc_], in0=bst[:, 16:16 + nc_], in1=avh,
                                                          op=ALU.mult),
                         [bb] + [B(f"av{j}") for j in range(nc_)], [bb])
                    S.op("dve", lambda e: e.tensor_tensor(out=bst[:, 32:32 + nc_], in0=bst[:, 8:8 + nc_],
                                                          in1=bst[:, 24:24 + nc_], op=ALU.mult), [bb] + rd, [bb])
                    S.op("dve", lambda e: e.tensor_tensor(out=bst[:, 32:32 + nc_], in0=bst[:, 32:32 + nc_],
                                                          in1=bst[:, 24:24 + nc_], op=ALU.mult), [bb], [bb])
                    S.op("act", lambda e: e.activation(out=bst[:, 40:40 + nc_], in_=bst[:, 32:32 + nc_], func=AF.Sqrt,
                                                       scale=1.0 / DV, bias=EPS), [bb], [bb])
                    S.op("dve", lambda e: e.reciprocal(out=bst[:, 40:40 + nc_], in_=bst[:, 40:40 + nc_]), [bb], [bb])
                    S.op("dve", lambda e: e.tensor_tensor(out=bst[:, 48:48 + nc_], in0=bst[:, 40:40 + nc_],
                                                          in1=bst[:, 24:24 + nc_], op=ALU.mult), [bb], [bb])

                    def gate_stt(j, off, n):
                        hk = j % 2
                        S.op("dve", lambda e: e.scalar_tensor_tensor(
                            out=hg[hk][:n, :], in0=numS[j][:n, :], scalar=bst[:n, 48 + j:49 + j],
                            in1=wgh2[st][j][:n, :], op0=ALU.mult, op1=ALU.mult),
                            [B(f"numS{j}"), bb, B(f"wgh{st}_{j}")], [B(f"hg{hk}")])

                    def tr_evac(j, off, n):
                        hk = j % 2
                        pT, pTb = bank()
                        pTv = pT[:].bitcast(BF16).rearrange("p (c t) -> p c t", c=8)
                        S.op("pe", lambda e: [e.transpose(
                            out=pTv[:, c, :n], in_=hg[hk][:n, c * 128:(c + 1) * 128], identity=idb[:n, :n])
                            for c in range(4)][-1], [B(f"hg{hk}"), B("idb")], [pTb])
                        S.op("act", lambda e: e.activation(
                            out=gT[:, 4 * h:4 * h + 4, off:off + n], in_=pTv[:, 0:4, :n], func=AF.Copy),
                            [pTb], [B(f"gTh{h}_{j}")])
                    gate_stt(0, *subs_[0])
                    gate_stt(1, *subs_[1])
                    yield
                    for j in range(nc_):
                        tr_evac(j, *subs_[j])
                        if j + 2 < nc_:
                            gate_stt(j + 2, *subs_[j + 2])
                        yield

                subs = tile["subs"]
                yield from stage_a1(0, *subs[0])
                for j, (off, n) in enumerate(subs):
                    if j + 1 < len(subs):
                        yield from stage_a1(j + 1, *subs[j + 1])
                    yield from stage_a(j, off, n)
                yield from stage_b_all()

            for it in P_items(0, 0):
                it()
            for h in range(H):
                st = h % 2
                filler = P_items(h + 1, 1 - st) if h + 1 < H else []
                fi = 0
                nsub_ = len(tile["subs"])
                n_a = 3 * nsub_
                RES = min(5, len(filler))
                per = -(-(len(filler) - RES) // n_a) if filler else 0
                yi = 0
                for _ in S_gen(h, st):
                    yi += 1
                    if yi <= n_a:
                        lim, k = len(filler) - RES, per
                    elif yi == n_a + 1:
                        lim, k = len(filler), RES
                    else:
                        lim, k = len(filler), per
                    for _k in range(k):
                        if fi < lim:
                            filler[fi]()
                            fi += 1
                while fi < len(filler):
                    filler[fi]()
                    fi += 1
            opb = {}

            def op_first(j, off, n):
                for half in range(2):
                    pt, pb = bank()
                    opb[(j, half)] = (pt, pb)
                    S.op("pe", lambda e, half=half, pt=pt: [e.matmul(
                        pt[:n, :], lhsT=gT[:, ec, off:off + n], rhs=wout[:, ec, half * 512:(half + 1) * 512],
                        start=(ec == 0), stop=False) for ec in range(12)][-1],
                        [B(f"gTh{hh}_{j}") for hh in range(3)] + [B("wout")], [pb])

            def op_second(j, off, n):
                for half in range(2):
                    pt, pb = opb.pop((j, half))
                    S.op("pe", lambda e, half=half, pt=pt: [e.matmul(
                        pt[:n, :], lhsT=gT[:, ec, off:off + n], rhs=wout[:, ec, half * 512:(half + 1) * 512],
                        start=False, stop=(ec == 15)) for ec in range(12, 16)][-1],
                        [B(f"gTh3_{j}"), B("wout")], [pb])
                    S.op("dve", lambda e, half=half, pt=pt: e.tensor_tensor(
                        out=xt[j][:n, half * 512:(half + 1) * 512], in0=pt[:n, :],
                        in1=xt[j][:n, half * 512:(half + 1) * 512], op=ALU.add), [pb, B(f"xt{j}")], [B(f"xt{j}")])
                final_norm(j, off, n)

            def final_norm(j, off, n):
                if ti == 0 and j == 0:
                    return
                fs = B("fstat")
                S.op("act", lambda e, j=j, n=n: (e.activation(out=junk[:n, :], in_=xt[j][:n, :], func=AF.Square,
                                                               accum_out=sstat[:n, 8:9]), e.drain())[-1],
                     [B(f"xt{j}")], [B("junk"), fs])
                S.op("act", lambda e, n=n: e.activation(out=sstat[:n, 9:10], in_=sstat[:n, 8:9], func=AF.Sqrt,
                                                         scale=1.0 / D, bias=EPS), [fs], [fs])
                S.op("dve", lambda e, n=n: e.reciprocal(out=sstat[:n, 10:11], in_=sstat[:n, 9:10]), [fs], [fs])
                S.op("dve", lambda e, j=j, n=n: e.scalar_tensor_tensor(
                    out=yout[:n, :], in0=xt[j][:n, :], scalar=sstat[:n, 10:11], in1=fnw[:n, :], op0=ALU.mult,
                    op1=ALU.mult), [B(f"xt{j}"), fs, B("prm")], [B("yout")])
                o0 = t0 + off - NPRE
                dma("sp", out_d[o0:o0 + n, :], yout[:n, :], "yout", [B("yout")], [B(f"outd_{ti}_{j}")])
            psubs = tile["subs"]
            for j, (off, n) in enumerate(psubs):
                op_first(j, off, n)
                if j >= 1:
                    op_second(j - 1, *psubs[j - 1])
            op_second(len(psubs) - 1, *psubs[-1])

    fin = [b for k, b in bufs.items() if k.startswith("outd_") or k.startswith("h1d_") or k == "st_d"]
    S.op("sp", None, fin, [])

    dma_sems = {k: es.enter_context(nc.semaphore(f"d_{k}")) for k in sorted(dma_keys)}
    with nc.Block() as block:
        S.emit(block, eng_sems, dma_sems)
    es.close()
    return nc


_PROG = {}


def _prog(mode):
    if mode not in _PROG:
        _PROG[mode] = build_program(mode)
    return _PROG[mode]


def _core_inputs(x, meta_tokens):
    xs = []
    for c in range(8):
        b, half = divmod(c, 2)
        if half == 0:
            xs.append(np.ascontiguousarray(np.concatenate([meta_tokens, x[b, :4096]], axis=0)))
        else:
            xs.append(np.ascontiguousarray(x[b, 4096 - NPRE:]))
    return xs


def kernel(x, meta_tokens, norm_w, conv_in_w, conv_w, conv_out_w, mlstm_in_w, mlstm_gate_b,
           mlstm_head_norm_w, mlstm_out_w, final_norm_w):
    x = np.asarray(x, np.float32)
    f = lambda a: np.asarray(a, np.float32)
    wcin, wcout, wmin, wg, wmout = _layout_weights(f(conv_in_w), f(conv_out_w), f(mlstm_in_w), f(mlstm_out_w))
    xs = _core_inputs(x, f(meta_tokens))
    prm = [_layout_params(f(norm_w), f(conv_w), f(mlstm_gate_b), f(mlstm_head_norm_w), f(final_norm_w),
                          1.0 if c % 2 == 0 else 0.0, 0.0 if c % 2 == 0 else 1.0) for c in range(8)]
    cores = list(range(8))
    r2 = run_bass_kernel_spmd(_prog("fused"), [dict(params=prm[c], xin=xs[c], wcin=wcin, wcout=wcout, wmin=wmin,
                                                    wg=wg, wmout=wmout) for c in cores], core_ids=cores).results
    out = np.empty((4, 8192, D), np.float32)
    for c in cores:
        b, half = divmod(c, 2)
        out[b, half * 4096:(half + 1) * 4096] = r2[c]["out"]
    return out
```

```python
import numpy as np
from contextlib import ExitStack
import concourse.bass as bass
import concourse.mybir as mybir
from concourse.bass_utils import run_bass_kernel_spmd

F32 = mybir.dt.float32
BF16 = mybir.dt.bfloat16
AF = mybir.ActivationFunctionType
ALU = mybir.AluOpType

D = 1024
E = 2048
NPRE = 16
TT = 512
H = 4
DK = 256
DV = 512
EPS = 1e-6
NRING = 3
DBG = 99
CASTBAR = False
GSUB = 99
RSLOT = 4096

P_ID, P_TRI, P_ONES = 0, 128, 256
P_NW = 384
P_CW = 400
P_GB = 448
P_FL = 456
P_PM = 458
P_FNW = 464
P_HNW = 464 + 1024
PCOLS = 464 + 1024 + 2048

ENGS = ("pe", "act", "dve", "pool", "sp")


class Buf:
    __slots__ = ("name", "w", "r", "excl")

    def __init__(self, name):
        self.name = name
        self.w = None
        self.r = []
        self.excl = name.startswith("ps")


class Op:
    __slots__ = ("eng", "fn", "deps", "raw", "pos", "is_dma", "key", "val", "inc", "signal", "count", "waits")


class Sched:
    def __init__(self):
        self.ops = {e: [] for e in ENGS}
        self.dma_cnt = {}

    def _rec(self, o, reads, writes):
        deps = []
        for b in reads:
            if b.w is not None:
                deps.append(b.w)
        o.raw = set(id(d) for d in deps)
        for b in reads:
            if b.excl:
                deps.extend(b.r)
        for b in writes:
            if b.w is not None:
                deps.append(b.w)
            deps.extend(b.r)
        o.deps = deps
        o.signal = False
        o.count = 0
        o.pos = len(self.ops[o.eng])
        self.ops[o.eng].append(o)
        for b in reads:
            if b.excl:
                b.w = o
                b.r = []
            else:
                b.r.append(o)
        for b in writes:
            b.w = o
            b.r = []
        return o

    def op(self, eng, fn, reads=(), writes=()):
        o = Op()
        o.eng, o.fn, o.is_dma, o.key, o.val = eng, fn, False, None, 0
        return self._rec(o, reads, writes)

    def dma(self, q, fn, key, reads=(), writes=(), inc=16):
        o = Op()
        o.eng, o.fn, o.is_dma, o.key = q, fn, True, key
        o.inc = inc
        o.val = self.dma_cnt.get(key, 0) + inc
        self.dma_cnt[key] = o.val
        return self._rec(o, reads, writes)

    def plan(self):
        for e in ENGS:
            seen_pos = {p: -1 for p in ENGS}
            seen_dma = {}
            for o in self.ops[e]:
                need_c, need_d = {}, {}
                for d in o.deps:
                    if d.is_dma:
                        if d.val > seen_dma.get(d.key, 0):
                            need_d[d.key] = max(need_d.get(d.key, 0), d.val)
                    else:
                        if d.eng == e and not o.is_dma:
                            if e == "pe" or id(d) not in o.raw:
                                continue
                        if d.pos > seen_pos[d.eng]:
                            if d.eng not in need_c or need_c[d.eng].pos < d.pos:
                                need_c[d.eng] = d
                waits = []
                for k, v in need_d.items():
                    seen_dma[k] = v
                    waits.append(("d", k, v))
                for pe, d in need_c.items():
                    seen_pos[pe] = d.pos
                    d.signal = True
                    waits.append(("c", d, None))
                o.waits = waits
        for e in ENGS:
            c = 0
            for o in self.ops[e]:
                if o.signal and not o.is_dma:
                    c += 1
                o.count = c

    def emit(self, block, eng_sems, dma_sems):
        self.plan()

        def run(e, eng):
            for o in self.ops[e]:
                for w in o.waits:
                    if w[0] == "d":
                        eng.wait_ge(dma_sems[w[1]], w[2])
                    else:
                        eng.wait_ge(eng_sems[w[1].eng], w[1].count)
                if o.fn is None:
                    continue
                inst = o.fn(eng)
                if o.is_dma:
                    inst.then_inc(dma_sems[o.key], o.inc)
                elif o.signal:
                    inst.then_inc(eng_sems[e], 1)

        @block.tensor
        def _(eng):
            run("pe", eng)

        @block.scalar
        def _(eng):
            run("act", eng)

        @block.vector
        def _(eng):
            run("dve", eng)

        @block.gpsimd
        def _(eng):
            run("pool", eng)

        @block.sync
        def _(eng):
            run("sp", eng)


def _wblk(h, s):
    if s in (1, 2):
        return h * 2 + (s - 1)
    return 2 * H + h * 3 + {0: 0, 3: 1, 4: 2}[s]


def _layout_weights(conv_in_w, conv_out_w, mlstm_in_w, mlstm_out_w):
    ci = conv_in_w[0].reshape(8, 128, 4, 16, 128)
    wcin = np.ascontiguousarray(ci.transpose(3, 1, 0, 2, 4)).reshape(16 * 128, RSLOT)
    wcout = np.ascontiguousarray(conv_out_w[0].reshape(16, 128, D).transpose(1, 0, 2)).reshape(128, 16 * D)
    wm = mlstm_in_w[0]
    wmin = np.zeros((H, 5, 128, RSLOT), np.float32)
    for h in range(H):
        q = wm[:, h * DK:(h + 1) * DK].reshape(8, 128, 2, 128)
        k = wm[:, D + h * DK:D + (h + 1) * DK].reshape(8, 128, 2, 128)
        qk = np.concatenate([q, k], axis=2)
        wmin[h, 0] = qk.transpose(1, 0, 2, 3).reshape(128, RSLOT)
        kt = wm[:, D + h * DK:D + (h + 1) * DK].reshape(8, 128, DK).transpose(1, 0, 2)
        wmin[h, 1, :, :8 * DK] = kt.reshape(128, 8 * DK)
        for s, base in ((2, 2 * D), (3, 2 * D + E), (4, 2 * D + 2 * E)):
            blk = wm[:, base + h * DV:base + (h + 1) * DV].reshape(8, 128, DV).transpose(1, 0, 2)
            wmin[h, s] = blk.reshape(128, RSLOT)
    wperm = np.zeros((H * 5, 128, RSLOT), np.float32)
    for h in range(H):
        for sl in range(5):
            wperm[_wblk(h, sl)] = wmin[h, sl]
    wmin = wperm.reshape(H * 5 * 128, RSLOT)
    wg = np.ascontiguousarray(wm[:, 2 * D + 3 * E:].reshape(8, 128, 8).transpose(1, 0, 2)).reshape(128, 64)
    wmout = np.ascontiguousarray(mlstm_out_w[0].reshape(16, 128, D).transpose(1, 0, 2)).reshape(128, 16 * D)
    return wcin, wcout, wmin, wg, wmout


def _layout_params(norm_w, conv_w, gate_b, head_norm_w, final_norm_w, pre_flag, state_flag):
    p = np.zeros((128, PCOLS), np.float32)
    p[:, P_ID:P_ID + 128] = np.eye(128, dtype=np.float32)
    p[:, P_TRI:P_TRI + 128] = np.triu(np.ones((128, 128), np.float32))
    p[:, P_ONES:P_ONES + 128] = 1.0
    for l in range(2):
        p[:, P_NW + 8 * l:P_NW + 8 * l + 8] = norm_w[l].reshape(8, 128).T
    for k in range(3):
        p[:, P_CW + 16 * k:P_CW + 16 * k + 16] = conv_w[0, k].reshape(16, 128).T
    p[:, P_GB:P_GB + 8] = gate_b[0][None, :]
    p[:, P_FL] = pre_flag
    p[:, P_FL + 1] = state_flag
    p[:NPRE, P_PM] = pre_flag
    p[:, P_FNW:P_FNW + D] = final_norm_w[None, :]
    p[:, P_HNW:P_HNW + E] = head_norm_w[0][None, :]
    return p


def _tiles(n_main):
    tiles = []
    for i in range(n_main):
        if i == 0:
            tiles.append(dict(t0=0, W=NPRE + TT, segs=[(0, NPRE), (NPRE, TT)],
                              subs=[(0, NPRE)] + [(NPRE + 128 * j, 128) for j in range(4)]))
        else:
            tiles.append(dict(t0=NPRE + TT * i, W=TT, segs=[(0, TT)],
                              subs=[(128 * j, 128) for j in range(4)]))
    return tiles


def build_program(mode, n_main=8, n_cores=8):
    do1 = mode in ("p1", "fused")
    do2 = mode in ("p2", "fused")
    ntok = NPRE + TT * n_main
    tiles = _tiles(n_main)
    nc = bass.Bass("TRN2", target_bir_lowering=False)
    S = Sched()
    es = ExitStack()
    bufs = {}

    def B(name):
        if name not in bufs:
            bufs[name] = Buf(name)
        return bufs[name]

    def dram(name, shape, dt, kind):
        return nc.dram_tensor(name, shape, dt, kind=kind).ap()

    def sb(name, shape, dt):
        return es.enter_context(nc.sbuf_tensor(name, shape, dt))

    params_d = dram("params", [128, PCOLS], F32, "ExternalInput")
    if do1:
        xin_d = dram("xin", [ntok, D], F32, "ExternalInput")
        wcin_d = dram("wcin", [16 * 128, RSLOT], F32, "ExternalInput")
        wcout_d = dram("wcout", [128, 16 * D], F32, "ExternalInput")
        wcin_b = dram("wcin_b", [16 * 128, RSLOT], BF16, "Internal")
        wcout_b = dram("wcout_b", [128, 16 * D], BF16, "Internal")
    wmin_d = dram("wmin", [H * 5 * 128, RSLOT], F32, "ExternalInput")
    wg_d = dram("wg", [128, 64], F32, "ExternalInput")
    wmin_b = dram("wmin_b", [H * 5 * 128, RSLOT], BF16, "Internal")
    if do2:
        wmout_d = dram("wmout", [128, 16 * D], F32, "ExternalInput")
        wmout_b = dram("wmout_b", [128, 16 * D], BF16, "Internal")
        out_d = dram("out", [ntok - NPRE, D], F32, "ExternalOutput")
    NST = H * 2 * DV + 8
    if mode == "p1":
        h1_d = dram("h1", [ntok, D], F32, "ExternalOutput")
        st_d = dram("st_out", [128, NST], F32, "ExternalOutput")
    elif mode == "p2":
        h1_d = dram("h1", [ntok, D], F32, "ExternalInput")
        st_d = dram("st_in", [128, NST], F32, "ExternalInput")
    else:
        h1_d = dram("h1", [ntok, D], F32, "Internal")
        kv_d = dram("kv_scr", [H, ntok, DK + DV], BF16, "Internal")
        NT_ = H * 2 * DV
        st_locT = nc.dram_tensor("st_locT", [128, NT_], F32).ap()
        st_locN = nc.dram_tensor("st_locN", [128, 8], F32).ap()
        st_allT = nc.dram_tensor("st_allT", [256, NT_], F32).ap()
        st_allN = nc.dram_tensor("st_allN", [256, 8], F32).ap()

    prm = sb("prm", [128, PCOLS], F32)
    idb = sb("idb", [128, 128], BF16)
    xt = [sb(f"xt{j}", [128, D], F32) for j in range(5)]
    junk = sb("junk", [128, D], BF16)
    xs0 = sb("xs0", [128, D], BF16)
    xs = [xs0, xs0]
    stat = sb("stat", [128, 64], F32)
    uT = sb("uT", [128, 8, NPRE + TT], BF16)
    u1T = sb("u1T", [128, 8, NPRE + TT], BF16)
    gT = sb("gT", [128, 16, NPRE + TT], BF16)
    ring = [sb(f"ring{k}", [128, RSLOT], BF16) for k in range(NRING)]
    wout = sb("wout", [128, 16, D], BF16)
    wgs = sb("wgs", [128, 8, 8], BF16)
    tA = sb("tA", [128, TT], F32)
    tB = sb("tB", [128, TT], F32)
    if do1:
        cx = sb("cx", [128, NPRE + TT + 2], F32)
        bg = sb("bg", [128, NPRE + TT], F32)
        yv = sb("yv", [128, NPRE + TT], F32)
        hal = sb("hal", [128, 16, 2], F32)
    nset = 2 if do2 else 1
    kh2 = [[sb(f"kh{st}_{j}", [128, DK], BF16) for j in range(5)] for st in range(nset)]
    vh2 = [[sb(f"vh{st}_{j}", [128, DV], BF16) for j in range(5)] for st in range(nset)]
    k_h, v_h = kh2[0], vh2[0]
    vp = [sb(f"vp{k}", [128, DV], BF16) for k in range(2)]
    gt = sb("gt", [128, 5, 8], F32)
    nlf = sb("nlf", [128, 5, 4], F32)
    gtmp = sb("gtmp", [128, 5, 4], F32)
    ee = sb("ee", [128, 5, 4], F32)
    eeb = sb("eeb", [128, 5, 4], BF16)
    av = sb("av", [128, 5, 4], F32)
    dec = sb("dec", [128, 11, 4], F32)
    Tst = sb("Tst", [128, H, 2, DV], F32)
    Tn = sb("Tn", [128, 8], F32)
    if do2:
        qT2 = [uT[:, 4 * st:4 * st + 2, :] for st in range(2)]
        kT2 = [uT[:, 4 * st + 2:4 * st + 4, :] for st in range(2)]
        p2buf = sb("p2buf", [128, 32 * DV], BF16)
        pv = lambda i: p2buf[:, i * DV:(i + 1) * DV]
        tw2 = [[pv(st * 5 + j) for j in range(5)] for st in range(2)]
        wgh2 = [[pv(10 + st * 5 + j) for j in range(5)] for st in range(2)]
        smT = [sb(f"smT{k}", [128, 128], BF16) for k in range(2)]
        hg = [pv(25 + k) for k in range(2)]
        stg = [p2buf[:, g * 8192:(g + 1) * 8192].bitcast(F32) for g in range(2)]
        Sbf = sb("Sbf", [128, 2, DV], BF16)
        nbf = sb("nbf", [128, 8], BF16)
        yout = p2buf[:, 28 * DV:32 * DV].bitcast(F32)
        sstat = sb("sstat", [128, 16], F32)
        gctr = [0]
        if do1:
            cxb, bgb = cx[:].bitcast(BF16), bg[:].bitcast(BF16)
            tG = [cxb[:, 0:DV], cxb[:, DV:2 * DV], bgb[:, 0:DV], bgb[:, DV:2 * DV]]
        else:
            tG = [sb(f"tG{k}", [128, DV], BF16) for k in range(4)]
        bst = sb("bst", [128, 64], F32)
        numS = [pv(20 + j) for j in range(5)]
    ps = [es.enter_context(nc.psum_tensor(f"ps{i}", [128, 512], F32)) for i in range(8)]
    eng_sems = {e: es.enter_context(nc.semaphore(f"s_{e}")) for e in ENGS}

    bank_ctr = [0]
    nbig = [6]

    def bank():
        i = bank_ctr[0] % nbig[0]
        bank_ctr[0] += 1
        return ps[i], B(f"ps{i}")

    def bank_fixed(i):
        return ps[i], B(f"ps{i}")

    small = ps[6]
    small2 = ps[7]

    ident = prm[:, P_ID:P_ID + 128]
    tri = prm[:, P_TRI:P_TRI + 128]
    ones = prm[:, P_ONES:P_ONES + 128]

    dma_keys = set()

    def dma(q, out, in_, key, reads, writes, **kw):
        dma_keys.add(key)
        return S.dma(q, lambda e: e.dma_start(out=out, in_=in_, **kw), key, reads=reads, writes=writes)

    dma("sp", prm[:], params_d, "prm", [], [B("prm")])
    S.op("dve", lambda e: e.tensor_copy(out=idb[:], in_=ident), [B("prm")], [B("idb")])
    dma("pool", wgs[:].rearrange("p c g -> p (c g)"), wg_d, "wgs", [], [B("wgs")])

    def cast_dram(src, dst, rows, name, r0=0, r1=None, step=512):
        s2 = src.rearrange("r (a k) -> (r a) k", k=2048)
        d2 = dst.rearrange("r (a k) -> (r a) k", k=2048)
        per = src.shape[1] // 2048
        chunks = []
        r1 = rows if r1 is None else r1
        for lo in range(r0, r1, step):
            hi = min(r1, lo + step)
            b = B(f"{name}_c{lo}")
            dma("pool", d2[lo * per:hi * per, :], s2[lo * per:hi * per, :], f"{name}_c{lo}", [], [b])
            chunks.append((lo, hi, b))
        return chunks

    def chunk_bufs(chunks, lo, hi):
        return [b for (a, z, b) in chunks if a < hi and z > lo]

    STAGED = mode == "fused"
    cin_chunks, cout_chunks, min_chunks = [], [], []
    if do1 and not STAGED:
        cin_chunks = cast_dram(wcin_d, wcin_b, 16 * 128, "wcin", 0, 256, step=128)
        cin_chunks += cast_dram(wcin_d, wcin_b, 16 * 128, "wcin", 256, 2048, step=256)
        cout_chunks = cast_dram(wcout_d, wcout_b, 128, "wcout")
    if not STAGED:
        min_chunks = cast_dram(wmin_d, wmin_b, H * 5 * 128, "wmin", 0, 2 * H * 128, step=256)
    late = []
    for r0_ in range(2 * H * 128, H * 5 * 128, 384):
        late.append(lambda r0_=r0_: min_chunks.extend(
            cast_dram(wmin_d, wmin_b, H * 5 * 128, "wmin", r0_, min(r0_ + 384, H * 5 * 128), step=384)))
    if do2:
        late.append(lambda: mout_chunks.extend(cast_dram(wmout_d, wmout_b, 128, "wmout")))

    def late_casts(n=None):
        k = len(late) if n is None else min(n, len(late))
        for _ in range(k):
            late.pop(0)()
    mout_chunks = []
    if not STAGED:
        late_casts()

    plan = []
    for ti in range(n_main):
        if do1:
            for fc in range(16):
                plan.append(("cin", wcin_b, fc * 128, cin_chunks))
            for h in range(H):
                for s in (1, 2):
                    plan.append((f"m{s}", wmin_b, _wblk(h, s) * 128, min_chunks))
    n_p1 = len(plan)
    if do2:
        for ti in range(n_main):
            for h in range(H):
                for s in ((0, 3, 4) if mode == "fused" else range(5)):
                    plan.append((f"m{s}", wmin_b, _wblk(h, s) * 128, min_chunks))
    issued = [0]
    taken = [0]

    def staged_cast(dst_flat, g, dbuf):
        cuts = [(0, 1024, "pool"), (1024, 2560, "act"), (2560, 4096, "dve")]
        for a, b_, eng in cuts:
            if eng == "act":
                S.op("act", lambda e, a=a, b_=b_: e.activation(out=dst_flat[:, a:b_], in_=stg[g][:, a:b_],
                                                               func=AF.Copy), [B(f"stg{g}")], [dbuf])
            else:
                S.op(eng, lambda e, a=a, b_=b_: e.tensor_copy(out=dst_flat[:, a:b_], in_=stg[g][:, a:b_]),
                     [B(f"stg{g}")], [dbuf])

    n_stage = (16 + 2 * H) if (STAGED and do1) else 0
    wbmap = {}
    stgc = [0]

    def ring_issue(upto):
        while issued[0] < min(upto, len(plan)):
            k = issued[0]
            name, src, lo, chunks = plan[k]
            slot = k % NRING
            if k < n_stage:
                g = stgc[0] % 2
                stgc[0] += 1
                src32 = wcin_d if name == "cin" else wmin_d
                dma("sp", stg[g], src32[lo:lo + 128, :], f"stg{g}", [], [B(f"stg{g}")])
                staged_cast(ring[slot][:], g, B(f"ring{slot}"))
                wb = B(f"wb_{name}_{lo}")
                wbmap[(name == "cin", lo)] = wb
                dma("pool", src[lo:lo + 128, :], ring[slot][:], f"wbk{slot}", [B(f"ring{slot}")], [wb])
            else:
                key = (name == "cin", lo)
                deps = [wbmap[key]] if key in wbmap else chunk_bufs(chunks, lo, lo + 128)
                dma("sp", ring[slot][:], src[lo:lo + 128, :], f"ring{slot}", deps, [B(f"ring{slot}")])
            issued[0] += 1

    def ring_next(name):
        k = taken[0]
        assert plan[k][0] == name, (plan[k][0], name)
        ring_issue(k + NRING)
        taken[0] += 1
        slot = k % NRING
        return ring[slot], B(f"ring{slot}")

    def norm_to_uT(tile, layer, dstT, dstname, src_loader, only=None):
        for j, (off, n) in enumerate(tile["subs"]):
            if only is not None and j not in only:
                continue
            if src_loader is not None:
                src_loader(j, off, n)
            ssj, rsj = B(f"ss{j}"), B(f"rs{j}")
            S.op("act", lambda e, j=j, n=n: (e.activation(out=junk[:n, :], in_=xt[j][:n, :], func=AF.Square,
                                                           accum_out=stat[:n, j:j + 1]), e.drain())[-1],
                 [B(f"xt{j}")], [B("junk"), ssj])
            S.op("act", lambda e, j=j, n=n: e.activation(out=stat[:n, 8 + j:9 + j], in_=stat[:n, j:j + 1],
                                                          func=AF.Sqrt, scale=1.0 / D, bias=EPS), [ssj], [rsj])
            S.op("dve", lambda e, j=j, n=n: e.reciprocal(out=stat[:n, 8 + j:9 + j], in_=stat[:n, 8 + j:9 + j]),
                 [rsj], [rsj])
            k = 0
            S.op("dve", lambda e, j=j, n=n, k=k: e.tensor_scalar(out=xs[k][:n, :], in0=xt[j][:n, :],
                                                                  scalar1=stat[:n, 8 + j:9 + j], scalar2=None,
                                                                  op0=ALU.mult),
                 [B(f"xt{j}"), rsj], [B(f"xs{k}")])
            pt, pb = bank()
            ptv = pt[:].bitcast(BF16).rearrange("p (c t) -> p c t", c=8)

            def tr(e, n=n, k=k, ptv=ptv):
                i = None
                for c in range(8):
                    i = e.transpose(out=ptv[:, c, :n], in_=xs[k][:n, c * 128:(c + 1) * 128], identity=idb[:n, :n])
                return i
            S.op("pe", tr, [B(f"xs{k}"), B("idb")], [pb])
            nwb = prm[:, P_NW + 8 * layer:P_NW + 8 * layer + 8].unsqueeze(2).to_broadcast([128, 8, n])
            S.op("act" if False else "dve",
                 lambda e, n=n, off=off, ptv=ptv, nwb=nwb: e.tensor_tensor(out=dstT[:, :, off:off + n],
                                                                           in0=ptv[:, :, :n], in1=nwb, op=ALU.mult),
                 [pb, B("prm")], [B(dstname)])

    def load_x(src_d, t0):
        def f(j, off, n):
            extra = [b for k, b in bufs.items() if "_c" in k] if CASTBAR else []
            dma("sp", xt[j][:n, :], src_d[t0 + off:t0 + off + n, :], f"xt{j}", extra, [B(f"xt{j}")])
        return f

    def load_h1(t0, ti):
        def f(j, off, n):
            dma("sp", xt[j][:n, :], h1_d[t0 + off:t0 + off + n, :], f"xt{j}", [B(f"h1d_{ti}_{j}")],
                [B(f"xt{j}")])
        return f

    def gates_stage(tile, first_tile, dbase):
        subs_ = tile["subs"]
        nc_ = len(subs_)
        sm = B("ps6")
        allj = lambda nm: [B(f"{nm}{j}") for j in range(nc_)]

        def gate_mm(e):
            i = None
            for j, (off, n) in enumerate(subs_):
                if first_tile and j == 0:
                    off, n = 0, 128
                for c in range(8):
                    i = e.matmul(small[:n, 8 * j:8 * j + 8], lhsT=u1T[:, c, off:off + n], rhs=wgs[:, c, :],
                                 start=(c == 0), stop=(c == 7))
            return i
        S.op("pe", gate_mm, [B("u1T"), B("wgs")], [sm])
        gb_b = prm[:, P_GB:P_GB + 8].unsqueeze(1).to_broadcast([128, nc_, 8])
        S.op("dve", lambda e: e.tensor_tensor(out=gt[:, 0:nc_, :],
                                              in0=small[:, 0:8 * nc_].rearrange("p (j g) -> p j g", g=8),
                                              in1=gb_b, op=ALU.add), [sm, B("prm")], allj("gt"))
        S.op("act", lambda e: e.activation(out=gtmp[:, 0:nc_, :], in_=gt[:, 0:nc_, 4:8], func=AF.Exp, scale=-1.0),
             allj("gt"), allj("gtmp"))
        S.op("act", lambda e: e.activation(out=nlf[:, 0:nc_, :], in_=gtmp[:, 0:nc_, :], func=AF.Ln, bias=1.0),
             allj("gtmp"), allj("nlf"))
        if first_tile:
            S.op("dve", lambda e: e.tensor_scalar(out=nlf[:, 0, :], in0=nlf[:, 0, :], scalar1=prm[:, P_PM:P_PM + 1],
                                                  scalar2=None, op0=ALU.mult), [B("nlf0"), B("prm")], [B("nlf0")])
        nlf_flat = nlf[:, 0:nc_, :].rearrange("p j g -> p (j g)")
        S.op("pe", lambda e: (e.matmul(small[:, 40:40 + 4 * nc_], lhsT=tri, rhs=nlf_flat, start=True, stop=True),
                              e.matmul(small[:, 64:64 + 4 * nc_], lhsT=ones, rhs=nlf_flat, start=True, stop=True))[-1],
             allj("nlf") + [B("prm")], [sm])
        cs = small[:, 40:40 + 4 * nc_].rearrange("p (j g) -> p j g", g=4)
        tot = small[:, 64:64 + 4 * nc_].rearrange("p (j g) -> p j g", g=4)
        S.op("act", lambda e: e.activation(out=av[:, 0:nc_, :], in_=cs, func=AF.Exp, scale=-1.0), [sm], allj("av"))
        S.op("dve", lambda e: e.tensor_tensor(out=gtmp[:, 0:nc_, :], in0=cs, in1=gt[:, 0:nc_, 0:4], op=ALU.add),
             [sm] + allj("gt"), allj("gtmp"))
        S.op("act", lambda e: e.activation(out=ee[:, 0:nc_, :], in_=gtmp[:, 0:nc_, :], func=AF.Exp),
             allj("gtmp"), allj("ee"))
        if first_tile:
            S.op("dve", lambda e: e.tensor_scalar(out=ee[:, 0, :], in0=ee[:, 0, :], scalar1=prm[:, P_PM:P_PM + 1],
                                                  scalar2=None, op0=ALU.mult), [B("ee0"), B("prm")], [B("ee0")])
        S.op("dve", lambda e: e.tensor_copy(out=eeb[:, 0:nc_, :], in_=ee[:, 0:nc_, :]), allj("ee"), allj("eeb"))
        S.op("act", lambda e: e.activation(out=dec[:, dbase:dbase + nc_, :], in_=tot, func=AF.Exp, scale=-1.0),
             [sm], [B(f"dec{dbase + j}") for j in range(nc_)])

    def proj_tok(tile, slot, sbuf, ncols, evac):
        sv = slot[:, 0:8 * ncols].rearrange("p (c n) -> p c n", c=8)
        for j, (off, n) in enumerate(tile["subs"]):
            pt, pb = bank()
            S.op("pe", lambda e, off=off, n=n, pt=pt, sv=sv: [e.matmul(pt[:n, :ncols], lhsT=u1T[:, c, off:off + n],
                                                                        rhs=sv[:, c, :], start=(c == 0), stop=(c == 7))
                                                               for c in range(8)][-1],
                 [B("u1T"), sbuf], [pb])
            evac(j, off, n, pt, pb)

    ONESI = 10
    prev_dec = {h: ONESI for h in range(H)}

    def state_update(j, n, h, k_src, v_src, st=0):
        pd = prev_dec[h]
        vk = (j + h) % 2
        S.op("act", lambda e, j=j, n=n, h=h, vk=vk: e.activation(out=vp[vk][:n, :], in_=v_src[j][:n, :], func=AF.Copy,
                                                                   scale=ee[:n, j, h:h + 1]),
             [B(f"vh{st}_{j}"), B(f"ee{j}")], [B(f"vp{vk}")])
        ups = []
        for half in range(2):
            pt, pb = bank()
            S.op("pe", lambda e, j=j, n=n, half=half, pt=pt, vk=vk: e.matmul(
                pt[:, :], lhsT=k_src[j][:n, half * 128:(half + 1) * 128], rhs=vp[vk][:n, :], start=True, stop=True),
                [B(f"kh{st}_{j}"), B(f"vp{vk}")], [pb])
            ups.append((pt, pb))
        sn = B("ps7")
        S.op("pe", lambda e, j=j, n=n, h=h: [e.matmul(small2[:, 8 + half:9 + half],
                                                      lhsT=k_src[j][:n, half * 128:(half + 1) * 128],
                                                      rhs=eeb[:n, j, h:h + 1], start=True, stop=True)
                                             for half in range(2)][-1],
             [B(f"kh{st}_{j}"), B(f"eeb{j}")], [sn])
        return ups, sn, pd

    def state_commit(j, h, ups, sn, pd, dbase):
        for half, (pt, pb) in enumerate(ups):
            S.op("dve", lambda e, h=h, half=half, pt=pt, pd=pd: e.scalar_tensor_tensor(
                out=Tst[:, h, half, :], in0=Tst[:, h, half, :], scalar=dec[:, pd, h:h + 1], in1=pt[:, :],
                op0=ALU.mult, op1=ALU.add), [B(f"T{h}"), pb, B(f"dec{pd}")], [B(f"T{h}")])
        S.op("dve", lambda e, h=h, pd=pd: e.scalar_tensor_tensor(
            out=Tn[:, 2 * h:2 * h + 2], in0=Tn[:, 2 * h:2 * h + 2], scalar=dec[:, pd, h:h + 1],
            in1=small2[:, 8:10], op0=ALU.mult, op1=ALU.add), [B(f"Tn{h}"), sn, B(f"dec{pd}")], [B(f"Tn{h}")])
        prev_dec[h] = dbase + j

    def copy_evac(dst_list, dname, ncols, eng="act"):
        def f(j, off, n, pt, pb):
            if eng == "act":
                S.op("act", lambda e: e.activation(out=dst_list[j][:n, :ncols], in_=pt[:n, :ncols], func=AF.Copy),
                     [pb], [B(f"{dname}{j}")])
            else:
                S.op("dve", lambda e: e.tensor_copy(out=dst_list[j][:n, :ncols], in_=pt[:n, :ncols]),
                     [pb], [B(f"{dname}{j}")])
        return f

    S.op("pool", lambda e: e.memset(dec[:, ONESI, :], 1.0), [], [B(f"dec{ONESI}")])
    if do1:
        S.op("pool", lambda e: e.memset(hal[:], 0.0), [], [B("hal")])
        S.op("pool", lambda e: e.memset(Tst[:].rearrange("p h a v -> p (h a v)"), 0.0), [],
             [B(f"T{h}") for h in range(H)])
        S.op("pool", lambda e: e.memset(Tn[:], 0.0), [], [B(f"Tn{h}") for h in range(H)])

    if do1:
        for ti, tile in enumerate(tiles):
            W_, t0 = tile["W"], tile["t0"]
            if STAGED and ti >= 1:
                late_casts(1)
            if DBG >= 2 and ti == 0:
                norm_to_uT(tile, 0, uT, "uT", load_x(xin_d, t0))
            nbig[0] = 8
            for fc in range(16 if DBG >= 3 else 0):
                if ti == 0 and STAGED and fc in (8, 10, 12, 14):
                    wflat = wout[:].rearrange("p c n -> p (c n)")
                    for q in [(fc - 8) // 2]:
                        g = stgc[0] % 2
                        stgc[0] += 1
                        dma("sp", stg[g], wcout_d[:, q * 4096:(q + 1) * 4096], f"stg{g}", [], [B(f"stg{g}")])
                        staged_cast(wflat[:, q * 4096:(q + 1) * 4096], g, B("wout"))
                slot, sbuf = ring_next("cin")
                sv = slot[:].rearrange("p (c g j) -> p c g j", c=8, g=4)
                S.op("pool", lambda e, fc=fc: e.tensor_copy(out=cx[:, 0:2], in_=hal[:, fc, :]), [B("hal")], [B("cx")])
                for (off, n) in tile["segs"]:
                    pbk = {}
                    for g in (1, 2, 3, 0):
                        pt, pb = bank()
                        pbk[g] = (pt, pb)
                        S.op("pe", lambda e, off=off, n=n, g=g, pt=pt, sv=sv: [e.matmul(
                            pt[:, :n], lhsT=sv[:, c, g, :], rhs=uT[:, c, off:off + n], start=(c == 0), stop=(c == 7))
                            for c in range(8)][-1], [B("uT"), sbuf], [pb])
                    S.op("act", lambda e, n=n, p=pbk[2][0]: e.activation(out=tA[:, :n], in_=p[:, :n], func=AF.Copy),
                         [pbk[2][1]], [B("tA")])
                    S.op("dve", lambda e, off=off, n=n, p=pbk[1][0]: e.tensor_tensor(
                        out=cx[:, 2 + off:2 + off + n], in0=p[:, :n], in1=tA[:, :n], op=ALU.mult),
                        [pbk[1][1], B("tA")], [B("cx")])
                    S.op("act", lambda e, n=n, p=pbk[3][0]: e.activation(out=tB[:, :n], in_=p[:, :n], func=AF.Silu),
                         [pbk[3][1]], [B("tB")])
                    S.op("dve", lambda e, off=off, n=n, p=pbk[0][0]: e.tensor_tensor(
                        out=bg[:, off:off + n], in0=p[:, :n], in1=tB[:, :n], op=ALU.mult),
                        [pbk[0][1], B("tB")], [B("bg")])
                cw = lambda k, fc=fc: prm[:, P_CW + 16 * k + fc:P_CW + 16 * k + fc + 1]
                S.op("act", lambda e, W_=W_, cw=cw: e.activation(out=yv[:, :W_], in_=cx[:, 2:2 + W_], func=AF.Copy,
                                                                  scale=cw(2)), [B("cx"), B("prm")], [B("yv")])
                S.op("dve", lambda e, W_=W_, cw=cw: e.scalar_tensor_tensor(
                    out=yv[:, :W_], in0=cx[:, 1:1 + W_], scalar=cw(1), in1=yv[:, :W_], op0=ALU.mult, op1=ALU.add),
                    [B("cx"), B("yv"), B("prm")], [B("yv")])
                S.op("dve", lambda e, W_=W_, cw=cw: e.scalar_tensor_tensor(
                    out=yv[:, :W_], in0=cx[:, 0:W_], scalar=cw(0), in1=yv[:, :W_], op0=ALU.mult, op1=ALU.add),
                    [B("cx"), B("yv"), B("prm")], [B("yv")])
                S.op("dve", lambda e, W_=W_, fc=fc: e.tensor_tensor(out=gT[:, fc, :W_], in0=yv[:, :W_],
                                                                   in1=bg[:, :W_], op=ALU.mult),
                     [B("yv"), B("bg")], [B("gT")])
                S.op("pool", lambda e, W_=W_, fc=fc: e.tensor_copy(out=hal[:, fc, :], in_=cx[:, W_:W_ + 2]),
                     [B("cx")], [B("hal")])
            nbig[0] = 6
            if False:
                wflat = wout[:].rearrange("p c n -> p (c n)")
                for q in range(4):
                    g = stgc[0] % 2
                    stgc[0] += 1
                    dma("sp", stg[g], wcout_d[:, q * 4096:(q + 1) * 4096], f"stg{g}", [], [B(f"stg{g}")])
                    staged_cast(wflat[:, q * 4096:(q + 1) * 4096], g, B("wout"))
            elif ti == 0 and not STAGED:
                dma("sp", wout[:].rearrange("p c n -> p (c n)"), wcout_b, "wout", chunk_bufs(cout_chunks, 0, 128),
                    [B("wout")])
            for j, (off, n) in enumerate(tile["subs"] if DBG >= 4 else []):
                for half in range(2):
                    pt, pb = bank()
                    S.op("pe", lambda e, off=off, n=n, half=half, pt=pt: [e.matmul(
                        pt[:n, :], lhsT=gT[:, fc, off:off + n], rhs=wout[:, fc, half * 512:(half + 1) * 512],
                        start=(fc == 0), stop=(fc == 15)) for fc in range(16)][-1], [B("gT"), B("wout")], [pb])
                    S.op("dve", lambda e, j=j, n=n, half=half, pt=pt: e.tensor_tensor(
                        out=xt[j][:n, half * 512:(half + 1) * 512], in0=pt[:n, :],
                        in1=xt[j][:n, half * 512:(half + 1) * 512], op=ALU.add), [pb, B(f"xt{j}")], [B(f"xt{j}")])
                dma("sp", h1_d[t0 + off:t0 + off + n, :], xt[j][:n, :], f"h1s{j}", [B(f"xt{j}")],
                    [B(f"h1d_{ti}_{j}")])
                if DBG >= 5 and j >= 1:
                    norm_to_uT(tile, 1, u1T, "u1T", None, only=[j - 1])
            if DBG >= 5:
                norm_to_uT(tile, 1, u1T, "u1T", None, only=[len(tile["subs"]) - 1])
            if DBG >= 6:
                gates_stage(tile, ti == 0, 5 * (ti % 2))
            def p1_proj(h, st):
                slot, sbuf = ring_next("m1")
                proj_tok(tile, slot, sbuf, DK, copy_evac(kh2[st], f"kh{st}_", DK, "act"))
                slot, sbuf = ring_next("m2")
                proj_tok(tile, slot, sbuf, DV, copy_evac(vh2[st], f"vh{st}_", DV, "dve"))
                if mode == "fused":
                    for j, (off, n) in enumerate(tile["subs"]):
                        dma("sp", kv_d[h, t0 + off:t0 + off + n, 0:DK], kh2[st][j][:n, :], f"ks{st}_{j}",
                            [B(f"kh{st}_{j}")], [B(f"kvd_{h}_{ti}_{j}")])
                        dma("sp", kv_d[h, t0 + off:t0 + off + n, DK:DK + DV], vh2[st][j][:n, :], f"vs{st}_{j}",
                            [B(f"vh{st}_{j}")], [B(f"kvd_{h}_{ti}_{j}")])

            def p1_state(h, st):
                for j, (off, n) in enumerate(tile["subs"]):
                    ups, sn, pd = state_update(j, n, h, kh2[st], vh2[st], st)
                    state_commit(j, h, ups, sn, pd, 5 * (ti % 2))
            nset1 = len(kh2)
            if ti + 1 < len(tiles):
                nt = tiles[ti + 1]
                ld = load_x(xin_d, nt["t0"])
                for j, (off, n) in enumerate(nt["subs"]):
                    ld(j, off, n)
            p1_proj(0, 0)
            for h in range(H):
                if h + 1 < H and nset1 > 1:
                    p1_proj(h + 1, (h + 1) % 2)
                p1_state(h, h % nset1)
                if h + 1 < H and nset1 == 1:
                    p1_proj(h + 1, 0)
                if h == 2 and ti + 1 < len(tiles):
                    norm_to_uT(tiles[ti + 1], 0, uT, "uT", None)
        late_casts()
        for h in range(H):
            pd = prev_dec[h]
            S.op("dve", lambda e, h=h, pd=pd: e.tensor_scalar(
                out=Tst[:, h, :, :].rearrange("p a v -> p (a v)"), in0=Tst[:, h, :, :].rearrange("p a v -> p (a v)"),
                scalar1=dec[:, pd, h:h + 1], scalar2=None, op0=ALU.mult), [B(f"T{h}"), B(f"dec{pd}")], [B(f"T{h}")])
            S.op("dve", lambda e, h=h, pd=pd: e.tensor_scalar(
                out=Tn[:, 2 * h:2 * h + 2], in0=Tn[:, 2 * h:2 * h + 2], scalar1=dec[:, pd, h:h + 1], scalar2=None,
                op0=ALU.mult), [B(f"Tn{h}"), B(f"dec{pd}")], [B(f"Tn{h}")])
            prev_dec[h] = ONESI
        Tall = [B(f"T{h}") for h in range(H)]
        Tnall = [B(f"Tn{h}") for h in range(H)]
        if mode == "p1":
            dT, dN = st_d[:, 0:H * 2 * DV], st_d[:, H * 2 * DV:NST]
        else:
            dT, dN = st_locT, st_locN
        dma("sp", dT, Tst[:].rearrange("p h a v -> p (h a v)"), "st_s", Tall, [B("st_d")])
        dma("sp", dN, Tn[:], "st_s", Tnall, [B("st_d")])

    if mode == "fused":
        dma_keys.add("ccT")
        dma_keys.add("ccN")
        groups = [[2 * i, 2 * i + 1] for i in range(n_cores // 2)]
        S.dma("pool", lambda e: e.collective_compute("AllGather", ALU.bypass, replica_groups=groups,
                                                     ins=[st_locT], outs=[st_allT]),
              "ccT", reads=[B("st_d")], writes=[B("st_all")], inc=1)
        S.dma("pool", lambda e: e.collective_compute("AllGather", ALU.bypass, replica_groups=groups,
                                                     ins=[st_locN], outs=[st_allN]),
              "ccN", reads=[B("st_d")], writes=[B("st_all")], inc=1)
        srcT, srcN = st_allT[0:128, :], st_allN[0:128, :]
        st_dep = [B("st_all")]
    elif mode == "p2":
        srcT, srcN = st_d[:, 0:H * 2 * DV], st_d[:, H * 2 * DV:NST]
        st_dep = []
    if do2:
        Tall = [B(f"T{h}") for h in range(H)]
        Tnall = [B(f"Tn{h}") for h in range(H)]
        dma("sp", Tst[:].rearrange("p h a v -> p (h a v)"), srcT, "st_l", st_dep, Tall)
        dma("sp", Tn[:], srcN, "st_l", st_dep, Tnall)
        for h in range(H):
            S.op("dve", lambda e, h=h: e.tensor_scalar(
                out=Tst[:, h, :, :].rearrange("p a v -> p (a v)"), in0=Tst[:, h, :, :].rearrange("p a v -> p (a v)"),
                scalar1=prm[:, P_FL + 1:P_FL + 2], scalar2=None, op0=ALU.mult), [B(f"T{h}"), B("prm")], [B(f"T{h}")])
            S.op("dve", lambda e, h=h: e.tensor_scalar(
                out=Tn[:, 2 * h:2 * h + 2], in0=Tn[:, 2 * h:2 * h + 2], scalar1=prm[:, P_FL + 1:P_FL + 2],
                scalar2=None, op0=ALU.mult), [B(f"Tn{h}"), B("prm")], [B(f"Tn{h}")])
        dma("sp", wout[:].rearrange("p c n -> p (c n)"), wmout_b, "wout", chunk_bufs(mout_chunks, 0, 128),
            [B("wout")])

    if do2:
        hnw = prm[:, P_HNW:P_HNW + E]
        fnw = prm[:, P_FNW:P_FNW + D]
        nbig[0] = 6
        for ti, tile in enumerate(tiles):
            W_, t0 = tile["W"], tile["t0"]
            norm_to_uT(tile, 1, u1T, "u1T", load_h1(t0, ti))
            gates_stage(tile, ti == 0, 5 * (ti % 2))
            def P_items(h, st):
                items = []
                box = {}

                def it_qk(blk, off, n):
                    def f():
                        if "qk" not in box:
                            box["qk"] = ring_next("m0")
                            if mode == "fused":
                                for j, (o2, n2) in enumerate(tile["subs"]):
                                    dma("sp", kh2[st][j][:n2, :], kv_d[h, t0 + o2:t0 + o2 + n2, 0:DK], f"kl{st}_{j}",
                                        [B(f"kvd_{h}_{ti}_{j}")], [B(f"kh{st}_{j}")])
                                    dma("sp", vh2[st][j][:n2, :], kv_d[h, t0 + o2:t0 + o2 + n2, DK:DK + DV],
                                        f"vl{st}_{j}", [B(f"kvd_{h}_{ti}_{j}")], [B(f"vh{st}_{j}")])
                        slot, sbuf = box["qk"]
                        sv = slot[:].rearrange("p (c g j) -> p c g j", c=8, g=4)
                        pt, pb = bank()
                        S.op("pe", lambda e: [e.matmul(pt[:, :n], lhsT=sv[:, c, blk, :], rhs=u1T[:, c, off:off + n],
                                                       start=(c == 0), stop=(c == 7)) for c in range(8)][-1],
                             [B("u1T"), sbuf], [pb])
                        if blk < 2:
                            S.op("act", lambda e: e.activation(out=qT2[st][:, blk, off:off + n], in_=pt[:, :n],
                                                               func=AF.Copy, scale=DK ** -0.5), [pb], [B(f"qT{st}")])
                        else:
                            S.op("dve", lambda e: e.tensor_copy(out=kT2[st][:, blk - 2, off:off + n], in_=pt[:, :n]),
                                 [pb], [B(f"kT{st}")])
                    return f
                for blk in range(4):
                    for (off, n) in tile["segs"]:
                        items.append(it_qk(blk, off, n))

                def it_tok(name, j, off, n):
                    def f():
                        if name not in box:
                            box[name] = ring_next(name)
                        slot, sbuf = box[name]
                        sv = slot[:].rearrange("p (c n) -> p c n", c=8)
                        pt, pb = bank()
                        S.op("pe", lambda e: [e.matmul(pt[:n, :], lhsT=u1T[:, c, off:off + n], rhs=sv[:, c, :],
                                                       start=(c == 0), stop=(c == 7)) for c in range(8)][-1],
                             [B("u1T"), sbuf], [pb])
                        if name == "m1":
                            S.op("act", lambda e: e.activation(out=kh2[st][j][:n, :], in_=pt[:n, :DK], func=AF.Copy),
                                 [pb], [B(f"kh{st}_{j}")])
                        elif name == "m2":
                            S.op("dve", lambda e: e.tensor_copy(out=vh2[st][j][:n, :], in_=pt[:n, :]),
                                 [pb], [B(f"vh{st}_{j}")])
                        elif name == "m3":
                            g = gctr[0] % 2
                            gctr[0] += 1
                            tg, tgb = tG[g], B(f"tG{g}")
                            S.op("act", lambda e: e.activation(out=tg[:n, :], in_=pt[:n, :], func=AF.Tanh, scale=0.5),
                                 [pb], [tgb])
                            S.op("dve", lambda e: e.scalar_tensor_tensor(
                                out=tw2[st][j][:n, :], in0=tg[:n, :], scalar=1.0, in1=hnw[:n, h * DV:(h + 1) * DV],
                                op0=ALU.add, op1=ALU.mult), [tgb, B("prm")], [B(f"tw{st}_{j}")])
                        else:
                            g = gctr[0] % 2
                            gctr[0] += 1
                            tg, tgb = tG[2 + g], B(f"tG{2 + g}")
                            S.op("act", lambda e: e.activation(out=tg[:n, :], in_=pt[:n, :], func=AF.Tanh, scale=0.5),
                                 [pb], [tgb])
                            S.op("dve", lambda e: e.scalar_tensor_tensor(
                                out=tg[:n, :], in0=tg[:n, :], scalar=1.0, in1=tw2[st][j][:n, :], op0=ALU.add,
                                op1=ALU.mult), [tgb, B(f"tw{st}_{j}")], [tgb])
                            S.op("dve", lambda e: e.scalar_tensor_tensor(
                                out=wgh2[st][j][:n, :], in0=pt[:n, :], scalar=0.25, in1=tg[:n, :], op0=ALU.mult,
                                op1=ALU.mult), [pb, tgb], [B(f"wgh{st}_{j}")])
                    return f
                for name in (("m3", "m4") if mode == "fused" else ("m1", "m2", "m3", "m4")):
                    for j, (off, n) in enumerate(tile["subs"]):
                        items.append(it_tok(name, j, off, n))
                return items

            def S_gen(h, st):
                qT, kT = qT2[st], kT2[st]
                ctx = {}

                def stage_a1(j, off, n):
                    vk = j % 2
                    S.op("act", lambda e: e.activation(out=vp[vk][:n, :], in_=vh2[st][j][:n, :], func=AF.Copy,
                                                       scale=ee[:n, j, h:h + 1]),
                         [B(f"vh{st}_{j}"), B(f"ee{j}")], [B(f"vp{vk}")])
                    pS, pSb = bank()
                    S.op("pe", lambda e: [e.matmul(
                        pS[:n, :n], lhsT=kT[:, a, off:off + n], rhs=qT[:, a, off:off + n], start=(a == 0),
                        stop=(a == 1)) for a in range(2)][-1], [B(f"kT{st}"), B(f"qT{st}")], [pSb])
                    mk = j % 2
                    S.op("dve", lambda e: e.tensor_tensor(
                        out=smT[mk][:n, :n], in0=pS[:n, :n], in1=tri[:n, :n], op=ALU.mult),
                        [pSb, B("prm")], [B(f"smT{mk}")])
                    yield

                def stage_a(j, off, n):
                    pd = prev_dec[h]
                    vk = mk = j % 2
                    kx = kh2[st]
                    ups = []
                    for half in range(2):
                        pt, pb = bank()
                        S.op("pe", lambda e, half=half, pt=pt: e.matmul(
                            pt[:, :], lhsT=kx[j][:n, half * 128:(half + 1) * 128], rhs=vp[vk][:n, :], start=True,
                            stop=True), [B(f"kh{st}_{j}"), B(f"vp{vk}")], [pb])
                        ups.append((pt, pb))
                    sn = B("ps7")
                    S.op("pe", lambda e: [e.matmul(small2[:, 8 + half:9 + half],
                                                   lhsT=kx[j][:n, half * 128:(half + 1) * 128],
                                                   rhs=eeb[:n, j, h:h + 1], start=True, stop=True)
                                          for half in range(2)][-1], [B(f"kh{st}_{j}"), B(f"eeb{j}")], [sn])
                    S.op("act", lambda e, pd=pd: e.activation(
                        out=Sbf[:].rearrange("p a v -> p (a v)"), in_=Tst[:, h, :, :].rearrange("p a v -> p (a v)"),
                        func=AF.Copy, scale=dec[:, pd, h:h + 1]), [B(f"T{h}"), B(f"dec{pd}")], [B("Sbf")])
                    S.op("dve", lambda e, pd=pd: e.tensor_scalar(
                        out=nbf[:, 2 * h:2 * h + 2], in0=Tn[:, 2 * h:2 * h + 2], scalar1=dec[:, pd, h:h + 1],
                        scalar2=None, op0=ALU.mult), [B(f"Tn{h}"), B(f"dec{pd}")], [B("nbf")])
                    yield
                    pN, pNb = bank()

                    def num_mm(e):
                        e.matmul(pN[:n, :], lhsT=smT[mk][:n, :n], rhs=vp[vk][:n, :], start=True, stop=False)
                        e.matmul(pN[:n, :], lhsT=qT[:, 0, off:off + n], rhs=Sbf[:, 0, :], start=False, stop=False)
                        return e.matmul(pN[:n, :], lhsT=qT[:, 1, off:off + n], rhs=Sbf[:, 1, :], start=False,
                                        stop=True)
                    S.op("pe", num_mm, [B(f"smT{mk}"), B(f"vp{vk}"), B(f"qT{st}"), B("Sbf")], [pNb])
                    sd = B("ps7")
                    dc = j % 2

                    def den_mm(e):
                        e.matmul(small2[:n, dc:dc + 1], lhsT=smT[mk][:n, :n], rhs=eeb[:n, j, h:h + 1], start=True,
                                 stop=False)
                        e.matmul(small2[:n, dc:dc + 1], lhsT=qT[:, 0, off:off + n], rhs=nbf[:, 2 * h:2 * h + 1],
                                 start=False, stop=False)
                        return e.matmul(small2[:n, dc:dc + 1], lhsT=qT[:, 1, off:off + n],
                                        rhs=nbf[:, 2 * h + 1:2 * h + 2], start=False, stop=True)
                    S.op("pe", den_mm, [B(f"smT{mk}"), B(f"eeb{j}"), B(f"qT{st}"), B("nbf")], [sd])
                    state_commit(j, h, ups, sn, pd, 5 * (ti % 2))
                    S.op("act", lambda e: e.activation(
                        out=bst[:n, j:j + 1], in_=small2[:n, dc:dc + 1], func=AF.Abs,
                        scale=av[:n, j, h:h + 1]), [sd, B(f"av{j}")], [B(f"bst_d{j}")])
                    S.op("act", lambda e: (e.activation(out=junk[:n, :DV], in_=pN[:n, :], func=AF.Square,
                                                        accum_out=bst[:n, 8 + j:9 + j]), e.drain())[-1],
                         [pNb], [B("junk"), B(f"bst_q{j}")])
                    S.op("act", lambda e: e.activation(out=numS[j][:n, :], in_=pN[:n, :], func=AF.Copy),
                         [pNb], [B(f"numS{j}")])
                    yield

                def stage_b_all():
                    subs_ = tile["subs"]
                    nc_ = len(subs_)
                    rd = [B(f"bst_d{j}") for j in range(nc_)] + [B(f"bst_q{j}") for j in range(nc_)]
                    bb = B("bst")
                    avh = av[:, 0:nc_, h]
                    S.op("dve", lambda e: e.tensor_scalar(out=bst[:, 16:16 + nc_], in0=bst[:, 0:nc_], scalar1=1.0,
                                                          scalar2=None, op0=ALU.max), rd, [bb])
                    S.op("dve", lambda e: e.reciprocal(out=bst[:, 16:16 + nc_], in_=bst[:, 16:16 + nc_]), [bb], [bb])
                    S.op("dve", lambda e: e.tensor_tensor(out=bst[:, 24:24 + nc_], in0=bst[:, 16:16 + nc_], in1=avh,
                                                          op=ALU.mult),
                         [bb] + [B(f"av{j}") for j in range(nc_)], [bb])
                    S.op("dve", lambda e: e.tensor_tensor(out=bst[:, 32:32 + nc_], in0=bst[:, 8:8 + nc_],
                                                          in1=bst[:, 24:24 + nc_], op=ALU.mult), [bb] + rd, [bb])
                    S.op("dve", lambda e: e.tensor_tensor(out=bst[:, 32:32 + nc_], in0=bst[:, 32:32 + nc_],
                                                          in1=bst[:, 24:24 + nc_], op=ALU.mult), [bb], [bb])
                    S.op("act", lambda e: e.activation(out=bst[:, 40:40 + nc_], in_=bst[:, 32:32 + nc_], func=AF.Sqrt,
                                                       scale=1.0 / DV, bias=EPS), [bb], [bb])
                    S.op("dve", lambda e: e.reciprocal(out=bst[:, 40:40 + nc_], in_=bst[:, 40:40 + nc_]), [bb], [bb])
                    S.op("dve", lambda e: e.tensor_tensor(out=bst[:, 48:48 + nc_], in0=bst[:, 40:40 + nc_],
                                                          in1=bst[:, 24:24 + nc_], op=ALU.mult), [bb], [bb])

                    def gate_stt(j, off, n):
                        hk = j % 2
                        S.op("dve", lambda e: e.scalar_tensor_tensor(
                            out=hg[hk][:n, :], in0=numS[j][:n, :], scalar=bst[:n, 48 + j:49 + j],
                            in1=wgh2[st][j][:n, :], op0=ALU.mult, op1=ALU.mult),
                            [B(f"numS{j}"), bb, B(f"wgh{st}_{j}")], [B(f"hg{hk}")])

                    def tr_evac(j, off, n):
                        hk = j % 2
                        pT, pTb = bank()
                        pTv = pT[:].bitcast(BF16).rearrange("p (c t) -> p c t", c=8)
                        S.op("pe", lambda e: [e.transpose(
                            out=pTv[:, c, :n], in_=hg[hk][:n, c * 128:(c + 1) * 128], identity=idb[:n, :n])
                            for c in range(4)][-1], [B(f"hg{hk}"), B("idb")], [pTb])
                        S.op("act", lambda e: e.activation(
                            out=gT[:, 4 * h:4 * h + 4, off:off + n], in_=pTv[:, 0:4, :n], func=AF.Copy),
                            [pTb], [B(f"gTh{h}_{j}")])
                    gate_stt(0, *subs_[0])
                    gate_stt(1, *subs_[1])
                    yield
                    for j in range(nc_):
                        tr_evac(j, *subs_[j])
                        if j + 2 < nc_:
                            gate_stt(j + 2, *subs_[j + 2])
                        yield

                subs = tile["subs"]
                yield from stage_a1(0, *subs[0])
                for j, (off, n) in enumerate(subs):
                    if j + 1 < len(subs):
                        yield from stage_a1(j + 1, *subs[j + 1])
                    yield from stage_a(j, off, n)
                yield from stage_b_all()

            for it in P_items(0, 0):
                it()
            for h in range(H):
                st = h % 2
                filler = P_items(h + 1, 1 - st) if h + 1 < H else []
                fi = 0
                nsub_ = len(tile["subs"])
                n_a = 3 * nsub_
                RES = min(5, len(filler))
                per = -(-(len(filler) - RES) // n_a) if filler else 0
                yi = 0
                for _ in S_gen(h, st):
                    yi += 1
                    if yi <= n_a:
                        lim, k = len(filler) - RES, per
                    elif yi == n_a + 1:
                        lim, k = len(filler), RES
                    else:
                        lim, k = len(filler), per
                    for _k in range(k):
                        if fi < lim:
                            filler[fi]()
                            fi += 1
                while fi < len(filler):
                    filler[fi]()
                    fi += 1
            opb = {}

            def op_first(j, off, n):
                for half in range(2):
                    pt, pb = bank()
                    opb[(j, half)] = (pt, pb)
                    S.op("pe", lambda e, half=half, pt=pt: [e.matmul(
                        pt[:n, :], lhsT=gT[:, ec, off:off + n], rhs=wout[:, ec, half * 512:(half + 1) * 512],
                        start=(ec == 0), stop=False) for ec in range(12)][-1],
                        [B(f"gTh{hh}_{j}") for hh in range(3)] + [B("wout")], [pb])

            def op_second(j, off, n):
                for half in range(2):
                    pt, pb = opb.pop((j, half))
                    S.op("pe", lambda e, half=half, pt=pt: [e.matmul(
                        pt[:n, :], lhsT=gT[:, ec, off:off + n], rhs=wout[:, ec, half * 512:(half + 1) * 512],
                        start=False, stop=(ec == 15)) for ec in range(12, 16)][-1],
                        [B(f"gTh3_{j}"), B("wout")], [pb])
                    S.op("dve", lambda e, half=half, pt=pt: e.tensor_tensor(
                        out=xt[j][:n, half * 512:(half + 1) * 512], in0=pt[:n, :],
                        in1=xt[j][:n, half * 512:(half + 1) * 512], op=ALU.add), [pb, B(f"xt{j}")], [B(f"xt{j}")])
                final_norm(j, off, n)

            def final_norm(j, off, n):
                if ti == 0 and j == 0:
                    return
                fs = B("fstat")
                S.op("act", lambda e, j=j, n=n: (e.activation(out=junk[:n, :], in_=xt[j][:n, :], func=AF.Square,
                                                               accum_out=sstat[:n, 8:9]), e.drain())[-1],
                     [B(f"xt{j}")], [B("junk"), fs])
                S.op("act", lambda e, n=n: e.activation(out=sstat[:n, 9:10], in_=sstat[:n, 8:9], func=AF.Sqrt,
                                                         scale=1.0 / D, bias=EPS), [fs], [fs])
                S.op("dve", lambda e, n=n: e.reciprocal(out=sstat[:n, 10:11], in_=sstat[:n, 9:10]), [fs], [fs])
                S.op("dve", lambda e, j=j, n=n: e.scalar_tensor_tensor(
                    out=yout[:n, :], in0=xt[j][:n, :], scalar=sstat[:n, 10:11], in1=fnw[:n, :], op0=ALU.mult,
                    op1=ALU.mult), [B(f"xt{j}"), fs, B("prm")], [B("yout")])
                o0 = t0 + off - NPRE
                dma("sp", out_d[o0:o0 + n, :], yout[:n, :], "yout", [B("yout")], [B(f"outd_{ti}_{j}")])
            psubs = tile["subs"]
            for j, (off, n) in enumerate(psubs):
                op_first(j, off, n)
                if j >= 1:
                    op_second(j - 1, *psubs[j - 1])
            op_second(len(psubs) - 1, *psubs[-1])

    fin = [b for k, b in bufs.items() if k.startswith("outd_") or k.startswith("h1d_") or k == "st_d"]
    S.op("sp", None, fin, [])

    dma_sems = {k: es.enter_context(nc.semaphore(f"d_{k}")) for k in sorted(dma_keys)}
    with nc.Block() as block:
        S.emit(block, eng_sems, dma_sems)
    es.close()
    return nc


_PROG = {}


def _prog(mode):
    if mode not in _PROG:
        _PROG[mode] = build_program(mode)
    return _PROG[mode]


def _core_inputs(x, meta_tokens):
    xs = []
    for c in range(8):
        b, half = divmod(c, 2)
        if half == 0:
            xs.append(np.ascontiguousarray(np.concatenate([meta_tokens, x[b, :4096]], axis=0)))
        else:
            xs.append(np.ascontiguousarray(x[b, 4096 - NPRE:]))
    return xs


def kernel(x, meta_tokens, norm_w, conv_in_w, conv_w, conv_out_w, mlstm_in_w, mlstm_gate_b,
           mlstm_head_norm_w, mlstm_out_w, final_norm_w):
    x = np.asarray(x, np.float32)
    f = lambda a: np.asarray(a, np.float32)
    wcin, wcout, wmin, wg, wmout = _layout_weights(f(conv_in_w), f(conv_out_w), f(mlstm_in_w), f(mlstm_out_w))
    xs = _core_inputs(x, f(meta_tokens))
    prm = [_layout_params(f(norm_w), f(conv_w), f(mlstm_gate_b), f(mlstm_head_norm_w), f(final_norm_w),
                          1.0 if c % 2 == 0 else 0.0, 0.0 if c % 2 == 0 else 1.0) for c in range(8)]
    cores = list(range(8))
    r2 = run_bass_kernel_spmd(_prog("fused"), [dict(params=prm[c], xin=xs[c], wcin=wcin, wcout=wcout, wmin=wmin,
                                                    wg=wg, wmout=wmout) for c in cores], core_ids=cores).results
    out = np.empty((4, 8192, D), np.float32)
    for c in cores:
        b, half = divmod(c, 2)
        out[b, half * 4096:(half + 1) * 4096] = r2[c]["out"]
    return out
```

```python
import numpy as np
from contextlib import ExitStack
import concourse.bass as bass
import concourse.mybir as mybir
from concourse.bass_utils import run_bass_kernel_spmd

F32 = mybir.dt.float32
BF16 = mybir.dt.bfloat16
AF = mybir.ActivationFunctionType
ALU = mybir.AluOpType

D = 1024
E = 2048
NPRE = 16
TT = 512
H = 4
DK = 256
DV = 512
EPS = 1e-6
NRING = 3
DBG = 99
CASTBAR = False
GSUB = 99
RSLOT = 4096

P_ID, P_TRI, P_ONES = 0, 128, 256
P_NW = 384
P_CW = 400
P_GB = 448
P_FL = 456
P_PM = 458
P_FNW = 464
P_HNW = 464 + 1024
PCOLS = 464 + 1024 + 2048

ENGS = ("pe", "act", "dve", "pool", "sp")


class Buf:
    __slots__ = ("name", "w", "r", "excl")

    def __init__(self, name):
        self.name = name
        self.w = None
        self.r = []
        self.excl = name.startswith("ps")


class Op:
    __slots__ = ("eng", "fn", "deps", "raw", "pos", "is_dma", "key", "val", "inc", "signal", "count", "waits")


class Sched:
    def __init__(self):
        self.ops = {e: [] for e in ENGS}
        self.dma_cnt = {}

    def _rec(self, o, reads, writes):
        deps = []
        for b in reads:
            if b.w is not None:
                deps.append(b.w)
        o.raw = set(id(d) for d in deps)
        for b in reads:
            if b.excl:
                deps.extend(b.r)
        for b in writes:
            if b.w is not None:
                deps.append(b.w)
            deps.extend(b.r)
        o.deps = deps
        o.signal = False
        o.count = 0
        o.pos = len(self.ops[o.eng])
        self.ops[o.eng].append(o)
        for b in reads:
            if b.excl:
                b.w = o
                b.r = []
            else:
                b.r.append(o)
        for b in writes:
            b.w = o
            b.r = []
        return o

    def op(self, eng, fn, reads=(), writes=()):
        o = Op()
        o.eng, o.fn, o.is_dma, o.key, o.val = eng, fn, False, None, 0
        return self._rec(o, reads, writes)

    def dma(self, q, fn, key, reads=(), writes=(), inc=16):
        o = Op()
        o.eng, o.fn, o.is_dma, o.key = q, fn, True, key
        o.inc = inc
        o.val = self.dma_cnt.get(key, 0) + inc
        self.dma_cnt[key] = o.val
        return self._rec(o, reads, writes)

    def plan(self):
        for e in ENGS:
            seen_pos = {p: -1 for p in ENGS}
            seen_dma = {}
            for o in self.ops[e]:
                need_c, need_d = {}, {}
                for d in o.deps:
                    if d.is_dma:
                        if d.val > seen_dma.get(d.key, 0):
                            need_d[d.key] = max(need_d.get(d.key, 0), d.val)
                    else:
                        if d.eng == e and not o.is_dma:
                            if e == "pe" or id(d) not in o.raw:
                                continue
                        if d.pos > seen_pos[d.eng]:
                            if d.eng not in need_c or need_c[d.eng].pos < d.pos:
                                need_c[d.eng] = d
                waits = []
                for k, v in need_d.items():
                    seen_dma[k] = v
                    waits.append(("d", k, v))
                for pe, d in need_c.items():
                    seen_pos[pe] = d.pos
                    d.signal = True
                    waits.append(("c", d, None))
                o.waits = waits
        for e in ENGS:
            c = 0
            for o in self.ops[e]:
                if o.signal and not o.is_dma:
                    c += 1
                o.count = c

    def emit(self, block, eng_sems, dma_sems):
        self.plan()

        def run(e, eng):
            for o in self.ops[e]:
                for w in o.waits:
                    if w[0] == "d":
                        eng.wait_ge(dma_sems[w[1]], w[2])
                    else:
                        eng.wait_ge(eng_sems[w[1].eng], w[1].count)
                if o.fn is None:
                    continue
                inst = o.fn(eng)
                if o.is_dma:
                    inst.then_inc(dma_sems[o.key], o.inc)
                elif o.signal:
                    inst.then_inc(eng_sems[e], 1)

        @block.tensor
        def _(eng):
            run("pe", eng)

        @block.scalar
        def _(eng):
            run("act", eng)

        @block.vector
        def _(eng):
            run("dve", eng)

        @block.gpsimd
        def _(eng):
            run("pool", eng)

        @block.sync
        def _(eng):
            run("sp", eng)


def _wblk(h, s):
    if s in (1, 2):
        return h * 2 + (s - 1)
    return 2 * H + h * 3 + {0: 0, 3: 1, 4: 2}[s]


def _layout_weights(conv_in_w, conv_out_w, mlstm_in_w, mlstm_out_w):
    ci = conv_in_w[0].reshape(8, 128, 4, 16, 128)
    wcin = np.ascontiguousarray(ci.transpose(3, 1, 0, 2, 4)).reshape(16 * 128, RSLOT)
    wcout = np.ascontiguousarray(conv_out_w[0].reshape(16, 128, D).transpose(1, 0, 2)).reshape(128, 16 * D)
    wm = mlstm_in_w[0]
    wmin = np.zeros((H, 5, 128, RSLOT), np.float32)
    for h in range(H):
        q = wm[:, h * DK:(h + 1) * DK].reshape(8, 128, 2, 128)
        k = wm[:, D + h * DK:D + (h + 1) * DK].reshape(8, 128, 2, 128)
        qk = np.concatenate([q, k], axis=2)
        wmin[h, 0] = qk.transpose(1, 0, 2, 3).reshape(128, RSLOT)
        kt = wm[:, D + h * DK:D + (h + 1) * DK].reshape(8, 128, DK).transpose(1, 0, 2)
        wmin[h, 1, :, :8 * DK] = kt.reshape(128, 8 * DK)
        for s, base in ((2, 2 * D), (3, 2 * D + E), (4, 2 * D + 2 * E)):
            blk = wm[:, base + h * DV:base + (h + 1) * DV].reshape(8, 128, DV).transpose(1, 0, 2)
            wmin[h, s] = blk.reshape(128, RSLOT)
    wperm = np.zeros((H * 5, 128, RSLOT), np.float32)
    for h in range(H):
        for sl in range(5):
            wperm[_wblk(h, sl)] = wmin[h, sl]
    wmin = wperm.reshape(H * 5 * 128, RSLOT)
    wg = np.ascontiguousarray(wm[:, 2 * D + 3 * E:].reshape(8, 128, 8).transpose(1, 0, 2)).reshape(128, 64)
    wmout = np.ascontiguousarray(mlstm_out_w[0].reshape(16, 128, D).transpose(1, 0, 2)).reshape(128, 16 * D)
    return wcin, wcout, wmin, wg, wmout


def _layout_params(norm_w, conv_w, gate_b, head_norm_w, final_norm_w, pre_flag, state_flag):
    p = np.zeros((128, PCOLS), np.float32)
    p[:, P_ID:P_ID + 128] = np.eye(128, dtype=np.float32)
    p[:, P_TRI:P_TRI + 128] = np.triu(np.ones((128, 128), np.float32))
    p[:, P_ONES:P_ONES + 128] = 1.0
    for l in range(2):
        p[:, P_NW + 8 * l:P_NW + 8 * l + 8] = norm_w[l].reshape(8, 128).T
    for k in range(3):
        p[:, P_CW + 16 * k:P_CW + 16 * k + 16] = conv_w[0, k].reshape(16, 128).T
    p[:, P_GB:P_GB + 8] = gate_b[0][None, :]
    p[:, P_FL] = pre_flag
    p[:, P_FL + 1] = state_flag
    p[:NPRE, P_PM] = pre_flag
    p[:, P_FNW:P_FNW + D] = final_norm_w[None, :]
    p[:, P_HNW:P_HNW + E] = head_norm_w[0][None, :]
    return p


def _tiles(n_main):
    tiles = []
    for i in range(n_main):
        if i == 0:
            tiles.append(dict(t0=0, W=NPRE + TT, segs=[(0, NPRE), (NPRE, TT)],
                              subs=[(0, NPRE)] + [(NPRE + 128 * j, 128) for j in range(4)]))
        else:
            tiles.append(dict(t0=NPRE + TT * i, W=TT, segs=[(0, TT)],
                              subs=[(128 * j, 128) for j in range(4)]))
    return tiles


def build_program(mode, n_main=8, n_cores=8):
    do1 = mode in ("p1", "fused")
    do2 = mode in ("p2", "fused")
    ntok = NPRE + TT * n_main
    tiles = _tiles(n_main)
    nc = bass.Bass("TRN2", target_bir_lowering=False)
    S = Sched()
    es = ExitStack()
    bufs = {}

    def B(name):
        if name not in bufs:
            bufs[name] = Buf(name)
        return bufs[name]

    def dram(name, shape, dt, kind):
        return nc.dram_tensor(name, shape, dt, kind=kind).ap()

    def sb(name, shape, dt):
        return es.enter_context(nc.sbuf_tensor(name, shape, dt))

    params_d = dram("params", [128, PCOLS], F32, "ExternalInput")
    if do1:
        xin_d = dram("xin", [ntok, D], F32, "ExternalInput")
        wcin_d = dram("wcin", [16 * 128, RSLOT], F32, "ExternalInput")
        wcout_d = dram("wcout", [128, 16 * D], F32, "ExternalInput")
        wcin_b = dram("wcin_b", [16 * 128, RSLOT], BF16, "Internal")
        wcout_b = dram("wcout_b", [128, 16 * D], BF16, "Internal")
    wmin_d = dram("wmin", [H * 5 * 128, RSLOT], F32, "ExternalInput")
    wg_d = dram("wg", [128, 64], F32, "ExternalInput")
    wmin_b = dram("wmin_b", [H * 5 * 128, RSLOT], BF16, "Internal")
    if do2:
        wmout_d = dram("wmout", [128, 16 * D], F32, "ExternalInput")
        wmout_b = dram("wmout_b", [128, 16 * D], BF16, "Internal")
        out_d = dram("out", [ntok - NPRE, D], F32, "ExternalOutput")
    NST = H * 2 * DV + 8
    if mode == "p1":
        h1_d = dram("h1", [ntok, D], F32, "ExternalOutput")
        st_d = dram("st_out", [128, NST], F32, "ExternalOutput")
    elif mode == "p2":
        h1_d = dram("h1", [ntok, D], F32, "ExternalInput")
        st_d = dram("st_in", [128, NST], F32, "ExternalInput")
    else:
        h1_d = dram("h1", [ntok, D], F32, "Internal")
        kv_d = dram("kv_scr", [H, ntok, DK + DV], BF16, "Internal")
        NT_ = H * 2 * DV
        st_locT = nc.dram_tensor("st_locT", [128, NT_], F32).ap()
        st_locN = nc.dram_tensor("st_locN", [128, 8], F32).ap()
        st_allT = nc.dram_tensor("st_allT", [256, NT_], F32).ap()
        st_allN = nc.dram_tensor("st_allN", [256, 8], F32).ap()

    prm = sb("prm", [128, PCOLS], F32)
    idb = sb("idb", [128, 128], BF16)
    xt = [sb(f"xt{j}", [128, D], F32) for j in range(5)]
    junk = sb("junk", [128, D], BF16)
    xs0 = sb("xs0", [128, D], BF16)
    xs = [xs0, xs0]
    stat = sb("stat", [128, 64], F32)
    uT = sb("uT", [128, 8, NPRE + TT], BF16)
    u1T = sb("u1T", [128, 8, NPRE + TT], BF16)
    gT = sb("gT", [128, 16, NPRE + TT], BF16)
    ring = [sb(f"ring{k}", [128, RSLOT], BF16) for k in range(NRING)]
    wout = sb("wout", [128, 16, D], BF16)
    wgs = sb("wgs", [128, 8, 8], BF16)
    tA = sb("tA", [128, TT], F32)
    tB = sb("tB", [128, TT], F32)
    if do1:
        cx = sb("cx", [128, NPRE + TT + 2], F32)
        bg = sb("bg", [128, NPRE + TT], F32)
        yv = sb("yv", [128, NPRE + TT], F32)
        hal = sb("hal", [128, 16, 2], F32)
    nset = 2 if do2 else 1
    kh2 = [[sb(f"kh{st}_{j}", [128, DK], BF16) for j in range(5)] for st in range(nset)]
    vh2 = [[sb(f"vh{st}_{j}", [128, DV], BF16) for j in range(5)] for st in range(nset)]
    k_h, v_h = kh2[0], vh2[0]
    vp = [sb(f"vp{k}", [128, DV], BF16) for k in range(2)]
    gt = sb("gt", [128, 5, 8], F32)
    nlf = sb("nlf", [128, 5, 4], F32)
    gtmp = sb("gtmp", [128, 5, 4], F32)
    ee = sb("ee", [128, 5, 4], F32)
    eeb = sb("eeb", [128, 5, 4], BF16)
    av = sb("av", [128, 5, 4], F32)
    dec = sb("dec", [128, 11, 4], F32)
    Tst = sb("Tst", [128, H, 2, DV], F32)
    Tn = sb("Tn", [128, 8], F32)
    if do2:
        qT2 = [uT[:, 4 * st:4 * st + 2, :] for st in range(2)]
        kT2 = [uT[:, 4 * st + 2:4 * st + 4, :] for st in range(2)]
        p2buf = sb("p2buf", [128, 32 * DV], BF16)
        pv = lambda i: p2buf[:, i * DV:(i + 1) * DV]
        tw2 = [[pv(st * 5 + j) for j in range(5)] for st in range(2)]
        wgh2 = [[pv(10 + st * 5 + j) for j in range(5)] for st in range(2)]
        smT = [sb(f"smT{k}", [128, 128], BF16) for k in range(2)]
        hg = [pv(25 + k) for k in range(2)]
        stg = [p2buf[:, g * 8192:(g + 1) * 8192].bitcast(F32) for g in range(2)]
        Sbf = sb("Sbf", [128, 2, DV], BF16)
        nbf = sb("nbf", [128, 8], BF16)
        yout = p2buf[:, 28 * DV:32 * DV].bitcast(F32)
        sstat = sb("sstat", [128, 16], F32)
        gctr = [0]
        if do1:
            cxb, bgb = cx[:].bitcast(BF16), bg[:].bitcast(BF16)
            tG = [cxb[:, 0:DV], cxb[:, DV:2 * DV], bgb[:, 0:DV], bgb[:, DV:2 * DV]]
        else:
            tG = [sb(f"tG{k}", [128, DV], BF16) for k in range(4)]
        bst = sb("bst", [128, 64], F32)
        numS = [pv(20 + j) for j in range(5)]
    ps = [es.enter_context(nc.psum_tensor(f"ps{i}", [128, 512], F32)) for i in range(8)]
    eng_sems = {e: es.enter_context(nc.semaphore(f"s_{e}")) for e in ENGS}

    bank_ctr = [0]
    nbig = [6]

    def bank():
        i = bank_ctr[0] % nbig[0]
        bank_ctr[0] += 1
        return ps[i], B(f"ps{i}")

    def bank_fixed(i):
        return ps[i], B(f"ps{i}")

    small = ps[6]
    small2 = ps[7]

    ident = prm[:, P_ID:P_ID + 128]
    tri = prm[:, P_TRI:P_TRI + 128]
    ones = prm[:, P_ONES:P_ONES + 128]

    dma_keys = set()

    def dma(q, out, in_, key, reads, writes, **kw):
        dma_keys.add(key)
        return S.dma(q, lambda e: e.dma_start(out=out, in_=in_, **kw), key, reads=reads, writes=writes)

    dma("sp", prm[:], params_d, "prm", [], [B("prm")])
    S.op("dve", lambda e: e.tensor_copy(out=idb[:], in_=ident), [B("prm")], [B("idb")])
    dma("pool", wgs[:].rearrange("p c g -> p (c g)"), wg_d, "wgs", [], [B("wgs")])

    def cast_dram(src, dst, rows, name, r0=0, r1=None, step=512):
        s2 = src.rearrange("r (a k) -> (r a) k", k=2048)
        d2 = dst.rearrange("r (a k) -> (r a) k", k=2048)
        per = src.shape[1] // 2048
        chunks = []
        r1 = rows if r1 is None else r1
        for lo in range(r0, r1, step):
            hi = min(r1, lo + step)
            b = B(f"{name}_c{lo}")
            dma("pool", d2[lo * per:hi * per, :], s2[lo * per:hi * per, :], f"{name}_c{lo}", [], [b])
            chunks.append((lo, hi, b))
        return chunks

    def chunk_bufs(chunks, lo, hi):
        return [b for (a, z, b) in chunks if a < hi and z > lo]

    STAGED = mode == "fused"
    cin_chunks, cout_chunks, min_chunks = [], [], []
    if do1 and not STAGED:
        cin_chunks = cast_dram(wcin_d, wcin_b, 16 * 128, "wcin", 0, 256, step=128)
        cin_chunks += cast_dram(wcin_d, wcin_b, 16 * 128, "wcin", 256, 2048, step=256)
        cout_chunks = cast_dram(wcout_d, wcout_b, 128, "wcout")
    if not STAGED:
        min_chunks = cast_dram(wmin_d, wmin_b, H * 5 * 128, "wmin", 0, 2 * H * 128, step=256)
    late = []
    for r0_ in range(2 * H * 128, H * 5 * 128, 384):
        late.append(lambda r0_=r0_: min_chunks.extend(
            cast_dram(wmin_d, wmin_b, H * 5 * 128, "wmin", r0_, min(r0_ + 384, H * 5 * 128), step=384)))
    if do2:
        late.append(lambda: mout_chunks.extend(cast_dram(wmout_d, wmout_b, 128, "wmout")))

    def late_casts(n=None):
        k = len(late) if n is None else min(n, len(late))
        for _ in range(k):
            late.pop(0)()
    mout_chunks = []
    if not STAGED:
        late_casts()

    plan = []
    for ti in range(n_main):
        if do1:
            for fc in range(16):
                plan.append(("cin", wcin_b, fc * 128, cin_chunks))
            for h in range(H):
                for s in (1, 2):
                    plan.append((f"m{s}", wmin_b, _wblk(h, s) * 128, min_chunks))
    n_p1 = len(plan)
    if do2:
        for ti in range(n_main):
            for h in range(H):
                for s in ((0, 3, 4) if mode == "fused" else range(5)):
                    plan.append((f"m{s}", wmin_b, _wblk(h, s) * 128, min_chunks))
    issued = [0]
    taken = [0]

    def staged_cast(dst_flat, g, dbuf):
        cuts = [(0, 1024, "pool"), (1024, 2560, "act"), (2560, 4096, "dve")]
        for a, b_, eng in cuts:
            if eng == "act":
                S.op("act", lambda e, a=a, b_=b_: e.activation(out=dst_flat[:, a:b_], in_=stg[g][:, a:b_],
                                                               func=AF.Copy), [B(f"stg{g}")], [dbuf])
            else:
                S.op(eng, lambda e, a=a, b_=b_: e.tensor_copy(out=dst_flat[:, a:b_], in_=stg[g][:, a:b_]),
                     [B(f"stg{g}")], [dbuf])

    n_stage = (16 + 2 * H) if (STAGED and do1) else 0
    wbmap = {}
    stgc = [0]

    def ring_issue(upto):
        while issued[0] < min(upto, len(plan)):
            k = issued[0]
            name, src, lo, chunks = plan[k]
            slot = k % NRING
            if k < n_stage:
                g = stgc[0] % 2
                stgc[0] += 1
                src32 = wcin_d if name == "cin" else wmin_d
                dma("sp", stg[g], src32[lo:lo + 128, :], f"stg{g}", [], [B(f"stg{g}")])
                staged_cast(ring[slot][:], g, B(f"ring{slot}"))
                wb = B(f"wb_{name}_{lo}")
                wbmap[(name == "cin", lo)] = wb
                dma("pool", src[lo:lo + 128, :], ring[slot][:], f"wbk{slot}", [B(f"ring{slot}")], [wb])
            else:
                key = (name == "cin", lo)
                deps = [wbmap[key]] if key in wbmap else chunk_bufs(chunks, lo, lo + 128)
                dma("sp", ring[slot][:], src[lo:lo + 128, :], f"ring{slot}", deps, [B(f"ring{slot}")])
            issued[0] += 1

    def ring_next(name):
        k = taken[0]
        assert plan[k][0] == name, (plan[k][0], name)
        ring_issue(k + NRING)
        taken[0] += 1
        slot = k % NRING
        return ring[slot], B(f"ring{slot}")

    def norm_to_uT(tile, layer, dstT, dstname, src_loader, only=None):
        for j, (off, n) in enumerate(tile["subs"]):
            if only is not None and j not in only:
                continue
            if src_loader is not None:
                src_loader(j, off, n)
            ssj, rsj = B(f"ss{j}"), B(f"rs{j}")
            S.op("act", lambda e, j=j, n=n: (e.activation(out=junk[:n, :], in_=xt[j][:n, :], func=AF.Square,
                                                           accum_out=stat[:n, j:j + 1]), e.drain())[-1],
                 [B(f"xt{j}")], [B("junk"), ssj])
            S.op("act", lambda e, j=j, n=n: e.activation(out=stat[:n, 8 + j:9 + j], in_=stat[:n, j:j + 1],
                                                          func=AF.Sqrt, scale=1.0 / D, bias=EPS), [ssj], [rsj])
            S.op("dve", lambda e, j=j, n=n: e.reciprocal(out=stat[:n, 8 + j:9 + j], in_=stat[:n, 8 + j:9 + j]),
                 [rsj], [rsj])
            k = 0
            S.op("dve", lambda e, j=j, n=n, k=k: e.tensor_scalar(out=xs[k][:n, :], in0=xt[j][:n, :],
                                                                  scalar1=stat[:n, 8 + j:9 + j], scalar2=None,
                                                                  op0=ALU.mult),
                 [B(f"xt{j}"), rsj], [B(f"xs{k}")])
            pt, pb = bank()
            ptv = pt[:].bitcast(BF16).rearrange("p (c t) -> p c t", c=8)

            def tr(e, n=n, k=k, ptv=ptv):
                i = None
                for c in range(8):
                    i = e.transpose(out=ptv[:, c, :n], in_=xs[k][:n, c * 128:(c + 1) * 128], identity=idb[:n, :n])
                return i
            S.op("pe", tr, [B(f"xs{k}"), B("idb")], [pb])
            nwb = prm[:, P_NW + 8 * layer:P_NW + 8 * layer + 8].unsqueeze(2).to_broadcast([128, 8, n])
            S.op("act" if False else "dve",
                 lambda e, n=n, off=off, ptv=ptv, nwb=nwb: e.tensor_tensor(out=dstT[:, :, off:off + n],
                                                                           in0=ptv[:, :, :n], in1=nwb, op=ALU.mult),
                 [pb, B("prm")], [B(dstname)])

    def load_x(src_d, t0):
        def f(j, off, n):
            extra = [b for k, b in bufs.items() if "_c" in k] if CASTBAR else []
            dma("sp", xt[j][:n, :], src_d[t0 + off:t0 + off + n, :], f"xt{j}", extra, [B(f"xt{j}")])
        return f

    def load_h1(t0, ti):
        def f(j, off, n):
            dma("sp", xt[j][:n, :], h1_d[t0 + off:t0 + off + n, :], f"xt{j}", [B(f"h1d_{ti}_{j}")],
                [B(f"xt{j}")])
        return f

    def gates_stage(tile, first_tile, dbase):
        subs_ = tile["subs"]
        nc_ = len(subs_)
        sm = B("ps6")
        allj = lambda nm: [B(f"{nm}{j}") for j in range(nc_)]

        def gate_mm(e):
            i = None
            for j, (off, n) in enumerate(subs_):
                if first_tile and j == 0:
                    off, n = 0, 128
                for c in range(8):
                    i = e.matmul(small[:n, 8 * j:8 * j + 8], lhsT=u1T[:, c, off:off + n], rhs=wgs[:, c, :],
                                 start=(c == 0), stop=(c == 7))
            return i
        S.op("pe", gate_mm, [B("u1T"), B("wgs")], [sm])
        gb_b = prm[:, P_GB:P_GB + 8].unsqueeze(1).to_broadcast([128, nc_, 8])
        S.op("dve", lambda e: e.tensor_tensor(out=gt[:, 0:nc_, :],
                                              in0=small[:, 0:8 * nc_].rearrange("p (j g) -> p j g", g=8),
                                              in1=gb_b, op=ALU.add), [sm, B("prm")], allj("gt"))
        S.op("act", lambda e: e.activation(out=gtmp[:, 0:nc_, :], in_=gt[:, 0:nc_, 4:8], func=AF.Exp, scale=-1.0),
             allj("gt"), allj("gtmp"))
        S.op("act", lambda e: e.activation(out=nlf[:, 0:nc_, :], in_=gtmp[:, 0:nc_, :], func=AF.Ln, bias=1.0),
             allj("gtmp"), allj("nlf"))
        if first_tile:
            S.op("dve", lambda e: e.tensor_scalar(out=nlf[:, 0, :], in0=nlf[:, 0, :], scalar1=prm[:, P_PM:P_PM + 1],
                                                  scalar2=None, op0=ALU.mult), [B("nlf0"), B("prm")], [B("nlf0")])
        nlf_flat = nlf[:, 0:nc_, :].rearrange("p j g -> p (j g)")
        S.op("pe", lambda e: (e.matmul(small[:, 40:40 + 4 * nc_], lhsT=tri, rhs=nlf_flat, start=True, stop=True),
                              e.matmul(small[:, 64:64 + 4 * nc_], lhsT=ones, rhs=nlf_flat, start=True, stop=True))[-1],
             allj("nlf") + [B("prm")], [sm])
        cs = small[:, 40:40 + 4 * nc_].rearrange("p (j g) -> p j g", g=4)
        tot = small[:, 64:64 + 4 * nc_].rearrange("p (j g) -> p j g", g=4)
        S.op("act", lambda e: e.activation(out=av[:, 0:nc_, :], in_=cs, func=AF.Exp, scale=-1.0), [sm], allj("av"))
        S.op("dve", lambda e: e.tensor_tensor(out=gtmp[:, 0:nc_, :], in0=cs, in1=gt[:, 0:nc_, 0:4], op=ALU.add),
             [sm] + allj("gt"), allj("gtmp"))
        S.op("act", lambda e: e.activation(out=ee[:, 0:nc_, :], in_=gtmp[:, 0:nc_, :], func=AF.Exp),
             allj("gtmp"), allj("ee"))
        if first_tile:
            S.op("dve", lambda e: e.tensor_scalar(out=ee[:, 0, :], in0=ee[:, 0, :], scalar1=prm[:, P_PM:P_PM + 1],
                                                  scalar2=None, op0=ALU.mult), [B("ee0"), B("prm")], [B("ee0")])
        S.op("dve", lambda e: e.tensor_copy(out=eeb[:, 0:nc_, :], in_=ee[:, 0:nc_, :]), allj("ee"), allj("eeb"))
        S.op("act", lambda e: e.activation(out=dec[:, dbase:dbase + nc_, :], in_=tot, func=AF.Exp, scale=-1.0),
             [sm], [B(f"dec{dbase + j}") for j in range(nc_)])

    def proj_tok(tile, slot, sbuf, ncols, evac):
        sv = slot[:, 0:8 * ncols].rearrange("p (c n) -> p c n", c=8)
        for j, (off, n) in enumerate(tile["subs"]):
            pt, pb = bank()
            S.op("pe", lambda e, off=off, n=n, pt=pt, sv=sv: [e.matmul(pt[:n, :ncols], lhsT=u1T[:, c, off:off + n],
                                                                        rhs=sv[:, c, :], start=(c == 0), stop=(c == 7))
                                                               for c in range(8)][-1],
                 [B("u1T"), sbuf], [pb])
            evac(j, off, n, pt, pb)

    ONESI = 10
    prev_dec = {h: ONESI for h in range(H)}

    def state_update(j, n, h, k_src, v_src, st=0):
        pd = prev_dec[h]
        vk = (j + h) % 2
        S.op("act", lambda e, j=j, n=n, h=h, vk=vk: e.activation(out=vp[vk][:n, :], in_=v_src[j][:n, :], func=AF.Copy,
                                                                   scale=ee[:n, j, h:h + 1]),
             [B(f"vh{st}_{j}"), B(f"ee{j}")], [B(f"vp{vk}")])
        ups = []
        for half in range(2):
            pt, pb = bank()
            S.op("pe", lambda e, j=j, n=n, half=half, pt=pt, vk=vk: e.matmul(
                pt[:, :], lhsT=k_src[j][:n, half * 128:(half + 1) * 128], rhs=vp[vk][:n, :], start=True, stop=True),
                [B(f"kh{st}_{j}"), B(f"vp{vk}")], [pb])
            ups.append((pt, pb))
        sn = B("ps7")
        S.op("pe", lambda e, j=j, n=n, h=h: [e.matmul(small2[:, 8 + half:9 + half],
                                                      lhsT=k_src[j][:n, half * 128:(half + 1) * 128],
                                                      rhs=eeb[:n, j, h:h + 1], start=True, stop=True)
                                             for half in range(2)][-1],
             [B(f"kh{st}_{j}"), B(f"eeb{j}")], [sn])
        return ups, sn, pd

    def state_commit(j, h, ups, sn, pd, dbase):
        for half, (pt, pb) in enumerate(ups):
            S.op("dve", lambda e, h=h, half=half, pt=pt, pd=pd: e.scalar_tensor_tensor(
                out=Tst[:, h, half, :], in0=Tst[:, h, half, :], scalar=dec[:, pd, h:h + 1], in1=pt[:, :],
                op0=ALU.mult, op1=ALU.add), [B(f"T{h}"), pb, B(f"dec{pd}")], [B(f"T{h}")])
        S.op("dve", lambda e, h=h, pd=pd: e.scalar_tensor_tensor(
            out=Tn[:, 2 * h:2 * h + 2], in0=Tn[:, 2 * h:2 * h + 2], scalar=dec[:, pd, h:h + 1],
            in1=small2[:, 8:10], op0=ALU.mult, op1=ALU.add), [B(f"Tn{h}"), sn, B(f"dec{pd}")], [B(f"Tn{h}")])
        prev_dec[h] = dbase + j

    def copy_evac(dst_list, dname, ncols, eng="act"):
        def f(j, off, n, pt, pb):
            if eng == "act":
                S.op("act", lambda e: e.activation(out=dst_list[j][:n, :ncols], in_=pt[:n, :ncols], func=AF.Copy),
                     [pb], [B(f"{dname}{j}")])
            else:
                S.op("dve", lambda e: e.tensor_copy(out=dst_list[j][:n, :ncols], in_=pt[:n, :ncols]),
                     [pb], [B(f"{dname}{j}")])
        return f

    S.op("pool", lambda e: e.memset(dec[:, ONESI, :], 1.0), [], [B(f"dec{ONESI}")])
    if do1:
        S.op("pool", lambda e: e.memset(hal[:], 0.0), [], [B("hal")])
        S.op("pool", lambda e: e.memset(Tst[:].rearrange("p h a v -> p (h a v)"), 0.0), [],
             [B(f"T{h}") for h in range(H)])
        S.op("pool", lambda e: e.memset(Tn[:], 0.0), [], [B(f"Tn{h}") for h in range(H)])

    if do1:
        for ti, tile in enumerate(tiles):
            W_, t0 = tile["W"], tile["t0"]
            if STAGED and ti >= 2:
                late_casts(1)
            if DBG >= 2 and ti == 0:
                norm_to_uT(tile, 0, uT, "uT", load_x(xin_d, t0))
            nbig[0] = 8
            for fc in range(16 if DBG >= 3 else 0):
                if ti == 0 and STAGED and fc in (8, 10, 12, 14):
                    wflat = wout[:].rearrange("p c n -> p (c n)")
                    for q in [(fc - 8) // 2]:
                        g = stgc[0] % 2
                        stgc[0] += 1
                        dma("sp", stg[g], wcout_d[:, q * 4096:(q + 1) * 4096], f"stg{g}", [], [B(f"stg{g}")])
                        staged_cast(wflat[:, q * 4096:(q + 1) * 4096], g, B("wout"))
                slot, sbuf = ring_next("cin")
                sv = slot[:].rearrange("p (c g j) -> p c g j", c=8, g=4)
                S.op("pool", lambda e, fc=fc: e.tensor_copy(out=cx[:, 0:2], in_=hal[:, fc, :]), [B("hal")], [B("cx")])
                for (off, n) in tile["segs"]:
                    pbk = {}
                    for g in (1, 2, 3, 0):
                        pt, pb = bank()
                        pbk[g] = (pt, pb)
                        S.op("pe", lambda e, off=off, n=n, g=g, pt=pt, sv=sv: [e.matmul(
                            pt[:, :n], lhsT=sv[:, c, g, :], rhs=uT[:, c, off:off + n], start=(c == 0), stop=(c == 7))
                            for c in range(8)][-1], [B("uT"), sbuf], [pb])
                    S.op("act", lambda e, n=n, p=pbk[2][0]: e.activation(out=tA[:, :n], in_=p[:, :n], func=AF.Copy),
                         [pbk[2][1]], [B("tA")])
                    S.op("dve", lambda e, off=off, n=n, p=pbk[1][0]: e.tensor_tensor(
                        out=cx[:, 2 + off:2 + off + n], in0=p[:, :n], in1=tA[:, :n], op=ALU.mult),
                        [pbk[1][1], B("tA")], [B("cx")])
                    S.op("act", lambda e, n=n, p=pbk[3][0]: e.activation(out=tB[:, :n], in_=p[:, :n], func=AF.Silu),
                         [pbk[3][1]], [B("tB")])
                    S.op("dve", lambda e, off=off, n=n, p=pbk[0][0]: e.tensor_tensor(
                        out=bg[:, off:off + n], in0=p[:, :n], in1=tB[:, :n], op=ALU.mult),
                        [pbk[0][1], B("tB")], [B("bg")])
                cw = lambda k, fc=fc: prm[:, P_CW + 16 * k + fc:P_CW + 16 * k + fc + 1]
                S.op("act", lambda e, W_=W_, cw=cw: e.activation(out=yv[:, :W_], in_=cx[:, 2:2 + W_], func=AF.Copy,
                                                                  scale=cw(2)), [B("cx"), B("prm")], [B("yv")])
                S.op("dve", lambda e, W_=W_, cw=cw: e.scalar_tensor_tensor(
                    out=yv[:, :W_], in0=cx[:, 1:1 + W_], scalar=cw(1), in1=yv[:, :W_], op0=ALU.mult, op1=ALU.add),
                    [B("cx"), B("yv"), B("prm")], [B("yv")])
                S.op("dve", lambda e, W_=W_, cw=cw: e.scalar_tensor_tensor(
                    out=yv[:, :W_], in0=cx[:, 0:W_], scalar=cw(0), in1=yv[:, :W_], op0=ALU.mult, op1=ALU.add),
                    [B("cx"), B("yv"), B("prm")], [B("yv")])
                S.op("dve", lambda e, W_=W_, fc=fc: e.tensor_tensor(out=gT[:, fc, :W_], in0=yv[:, :W_],
                                                                   in1=bg[:, :W_], op=ALU.mult),
                     [B("yv"), B("bg")], [B("gT")])
                S.op("pool", lambda e, W_=W_, fc=fc: e.tensor_copy(out=hal[:, fc, :], in_=cx[:, W_:W_ + 2]),
                     [B("cx")], [B("hal")])
            nbig[0] = 6
            if False:
                wflat = wout[:].rearrange("p c n -> p (c n)")
                for q in range(4):
                    g = stgc[0] % 2
                    stgc[0] += 1
                    dma("sp", stg[g], wcout_d[:, q * 4096:(q + 1) * 4096], f"stg{g}", [], [B(f"stg{g}")])
                    staged_cast(wflat[:, q * 4096:(q + 1) * 4096], g, B("wout"))
            elif ti == 0 and not STAGED:
                dma("sp", wout[:].rearrange("p c n -> p (c n)"), wcout_b, "wout", chunk_bufs(cout_chunks, 0, 128),
                    [B("wout")])
            for j, (off, n) in enumerate(tile["subs"] if DBG >= 4 else []):
                for half in range(2):
                    pt, pb = bank()
                    S.op("pe", lambda e, off=off, n=n, half=half, pt=pt: [e.matmul(
                        pt[:n, :], lhsT=gT[:, fc, off:off + n], rhs=wout[:, fc, half * 512:(half + 1) * 512],
                        start=(fc == 0), stop=(fc == 15)) for fc in range(16)][-1], [B("gT"), B("wout")], [pb])
                    S.op("dve", lambda e, j=j, n=n, half=half, pt=pt: e.tensor_tensor(
                        out=xt[j][:n, half * 512:(half + 1) * 512], in0=pt[:n, :],
                        in1=xt[j][:n, half * 512:(half + 1) * 512], op=ALU.add), [pb, B(f"xt{j}")], [B(f"xt{j}")])
                dma("sp", h1_d[t0 + off:t0 + off + n, :], xt[j][:n, :], f"h1s{j}", [B(f"xt{j}")],
                    [B(f"h1d_{ti}_{j}")])
                if DBG >= 5 and j >= 1:
                    norm_to_uT(tile, 1, u1T, "u1T", None, only=[j - 1])
            if DBG >= 5:
                norm_to_uT(tile, 1, u1T, "u1T", None, only=[len(tile["subs"]) - 1])
            if DBG >= 6:
                gates_stage(tile, ti == 0, 5 * (ti % 2))
            def p1_proj(h, st):
                slot, sbuf = ring_next("m1")
                proj_tok(tile, slot, sbuf, DK, copy_evac(kh2[st], f"kh{st}_", DK, "act"))
                slot, sbuf = ring_next("m2")
                proj_tok(tile, slot, sbuf, DV, copy_evac(vh2[st], f"vh{st}_", DV, "dve"))
                if mode == "fused":
                    for j, (off, n) in enumerate(tile["subs"]):
                        dma("sp", kv_d[h, t0 + off:t0 + off + n, 0:DK], kh2[st][j][:n, :], f"ks{st}_{j}",
                            [B(f"kh{st}_{j}")], [B(f"kvd_{h}_{ti}_{j}")])
                        dma("sp", kv_d[h, t0 + off:t0 + off + n, DK:DK + DV], vh2[st][j][:n, :], f"vs{st}_{j}",
                            [B(f"vh{st}_{j}")], [B(f"kvd_{h}_{ti}_{j}")])

            def p1_state(h, st):
                for j, (off, n) in enumerate(tile["subs"]):
                    ups, sn, pd = state_update(j, n, h, kh2[st], vh2[st], st)
                    state_commit(j, h, ups, sn, pd, 5 * (ti % 2))
            nset1 = len(kh2)
            if ti + 1 < len(tiles):
                nt = tiles[ti + 1]
                ld = load_x(xin_d, nt["t0"])
                for j, (off, n) in enumerate(nt["subs"]):
                    ld(j, off, n)
            p1_proj(0, 0)
            for h in range(H):
                if h + 1 < H and nset1 > 1:
                    p1_proj(h + 1, (h + 1) % 2)
                p1_state(h, h % nset1)
                if h + 1 < H and nset1 == 1:
                    p1_proj(h + 1, 0)
                if h == 2 and ti + 1 < len(tiles):
                    norm_to_uT(tiles[ti + 1], 0, uT, "uT", None)
        late_casts()
        for h in range(H):
            pd = prev_dec[h]
            S.op("dve", lambda e, h=h, pd=pd: e.tensor_scalar(
                out=Tst[:, h, :, :].rearrange("p a v -> p (a v)"), in0=Tst[:, h, :, :].rearrange("p a v -> p (a v)"),
                scalar1=dec[:, pd, h:h + 1], scalar2=None, op0=ALU.mult), [B(f"T{h}"), B(f"dec{pd}")], [B(f"T{h}")])
            S.op("dve", lambda e, h=h, pd=pd: e.tensor_scalar(
                out=Tn[:, 2 * h:2 * h + 2], in0=Tn[:, 2 * h:2 * h + 2], scalar1=dec[:, pd, h:h + 1], scalar2=None,
                op0=ALU.mult), [B(f"Tn{h}"), B(f"dec{pd}")], [B(f"Tn{h}")])
            prev_dec[h] = ONESI
        Tall = [B(f"T{h}") for h in range(H)]
        Tnall = [B(f"Tn{h}") for h in range(H)]
        if mode == "p1":
            dT, dN = st_d[:, 0:H * 2 * DV], st_d[:, H * 2 * DV:NST]
        else:
            dT, dN = st_locT, st_locN
        dma("sp", dT, Tst[:].rearrange("p h a v -> p (h a v)"), "st_s", Tall, [B("st_d")])
        dma("sp", dN, Tn[:], "st_s", Tnall, [B("st_d")])

    if mode == "fused":
        dma_keys.add("ccT")
        dma_keys.add("ccN")
        groups = [[2 * i, 2 * i + 1] for i in range(n_cores // 2)]
        S.dma("pool", lambda e: e.collective_compute("AllGather", ALU.bypass, replica_groups=groups,
                                                     ins=[st_locT], outs=[st_allT]),
              "ccT", reads=[B("st_d")], writes=[B("st_all")], inc=1)
        S.dma("pool", lambda e: e.collective_compute("AllGather", ALU.bypass, replica_groups=groups,
                                                     ins=[st_locN], outs=[st_allN]),
              "ccN", reads=[B("st_d")], writes=[B("st_all")], inc=1)
        srcT, srcN = st_allT[0:128, :], st_allN[0:128, :]
        st_dep = [B("st_all")]
    elif mode == "p2":
        srcT, srcN = st_d[:, 0:H * 2 * DV], st_d[:, H * 2 * DV:NST]
        st_dep = []
    if do2:
        Tall = [B(f"T{h}") for h in range(H)]
        Tnall = [B(f"Tn{h}") for h in range(H)]
        dma("sp", Tst[:].rearrange("p h a v -> p (h a v)"), srcT, "st_l", st_dep, Tall)
        dma("sp", Tn[:], srcN, "st_l", st_dep, Tnall)
        for h in range(H):
            S.op("dve", lambda e, h=h: e.tensor_scalar(
                out=Tst[:, h, :, :].rearrange("p a v -> p (a v)"), in0=Tst[:, h, :, :].rearrange("p a v -> p (a v)"),
                scalar1=prm[:, P_FL + 1:P_FL + 2], scalar2=None, op0=ALU.mult), [B(f"T{h}"), B("prm")], [B(f"T{h}")])
            S.op("dve", lambda e, h=h: e.tensor_scalar(
                out=Tn[:, 2 * h:2 * h + 2], in0=Tn[:, 2 * h:2 * h + 2], scalar1=prm[:, P_FL + 1:P_FL + 2],
                scalar2=None, op0=ALU.mult), [B(f"Tn{h}"), B("prm")], [B(f"Tn{h}")])
        dma("sp", wout[:].rearrange("p c n -> p (c n)"), wmout_b, "wout", chunk_bufs(mout_chunks, 0, 128),
            [B("wout")])

    if do2:
        hnw = prm[:, P_HNW:P_HNW + E]
        fnw = prm[:, P_FNW:P_FNW + D]
        nbig[0] = 6
        for ti, tile in enumerate(tiles):
            W_, t0 = tile["W"], tile["t0"]
            norm_to_uT(tile, 1, u1T, "u1T", load_h1(t0, ti))
            gates_stage(tile, ti == 0, 5 * (ti % 2))
            def P_items(h, st):
                items = []
                box = {}

                def it_qk(blk, off, n):
                    def f():
                        if "qk" not in box:
                            box["qk"] = ring_next("m0")
                            if mode == "fused":
                                for j, (o2, n2) in enumerate(tile["subs"]):
                                    dma("sp", kh2[st][j][:n2, :], kv_d[h, t0 + o2:t0 + o2 + n2, 0:DK], f"kl{st}_{j}",
                                        [B(f"kvd_{h}_{ti}_{j}")], [B(f"kh{st}_{j}")])
                                    dma("sp", vh2[st][j][:n2, :], kv_d[h, t0 + o2:t0 + o2 + n2, DK:DK + DV],
                                        f"vl{st}_{j}", [B(f"kvd_{h}_{ti}_{j}")], [B(f"vh{st}_{j}")])
                        slot, sbuf = box["qk"]
                        sv = slot[:].rearrange("p (c g j) -> p c g j", c=8, g=4)
                        pt, pb = bank()
                        S.op("pe", lambda e: [e.matmul(pt[:, :n], lhsT=sv[:, c, blk, :], rhs=u1T[:, c, off:off + n],
                                                       start=(c == 0), stop=(c == 7)) for c in range(8)][-1],
                             [B("u1T"), sbuf], [pb])
                        if blk < 2:
                            S.op("act", lambda e: e.activation(out=qT2[st][:, blk, off:off + n], in_=pt[:, :n],
                                                               func=AF.Copy, scale=DK ** -0.5), [pb], [B(f"qT{st}")])
                        else:
                            S.op("dve", lambda e: e.tensor_copy(out=kT2[st][:, blk - 2, off:off + n], in_=pt[:, :n]),
                                 [pb], [B(f"kT{st}")])
                    return f
                for blk in range(4):
                    for (off, n) in tile["segs"]:
                        items.append(it_qk(blk, off, n))

                def it_tok(name, j, off, n):
                    def f():
                        if name not in box:
                            box[name] = ring_next(name)
                        slot, sbuf = box[name]
                        sv = slot[:].rearrange("p (c n) -> p c n", c=8)
                        pt, pb = bank()
                        S.op("pe", lambda e: [e.matmul(pt[:n, :], lhsT=u1T[:, c, off:off + n], rhs=sv[:, c, :],
                                                       start=(c == 0), stop=(c == 7)) for c in range(8)][-1],
                             [B("u1T"), sbuf], [pb])
                        if name == "m1":
                            S.op("act", lambda e: e.activation(out=kh2[st][j][:n, :], in_=pt[:n, :DK], func=AF.Copy),
                                 [pb], [B(f"kh{st}_{j}")])
                        elif name == "m2":
                            S.op("dve", lambda e: e.tensor_copy(out=vh2[st][j][:n, :], in_=pt[:n, :]),
                                 [pb], [B(f"vh{st}_{j}")])
                        elif name == "m3":
                            g = gctr[0] % 2
                            gctr[0] += 1
                            tg, tgb = tG[g], B(f"tG{g}")
                            S.op("act", lambda e: e.activation(out=tg[:n, :], in_=pt[:n, :], func=AF.Tanh, scale=0.5),
                                 [pb], [tgb])
                            S.op("dve", lambda e: e.scalar_tensor_tensor(
                                out=tw2[st][j][:n, :], in0=tg[:n, :], scalar=1.0, in1=hnw[:n, h * DV:(h + 1) * DV],
                                op0=ALU.add, op1=ALU.mult), [tgb, B("prm")], [B(f"tw{st}_{j}")])
                        else:
                            g = gctr[0] % 2
                            gctr[0] += 1
                            tg, tgb = tG[2 + g], B(f"tG{2 + g}")
                            S.op("act", lambda e: e.activation(out=tg[:n, :], in_=pt[:n, :], func=AF.Tanh, scale=0.5),
                                 [pb], [tgb])
                            S.op("dve", lambda e: e.scalar_tensor_tensor(
                                out=tg[:n, :], in0=tg[:n, :], scalar=1.0, in1=tw2[st][j][:n, :], op0=ALU.add,
                                op1=ALU.mult), [tgb, B(f"tw{st}_{j}")], [tgb])
                            S.op("dve", lambda e: e.scalar_tensor_tensor(
                                out=wgh2[st][j][:n, :], in0=pt[:n, :], scalar=0.25, in1=tg[:n, :], op0=ALU.mult,
                                op1=ALU.mult), [pb, tgb], [B(f"wgh{st}_{j}")])
                    return f
                for name in (("m3", "m4") if mode == "fused" else ("m1", "m2", "m3", "m4")):
                    for j, (off, n) in enumerate(tile["subs"]):
                        items.append(it_tok(name, j, off, n))
                return items

            def S_gen(h, st):
                qT, kT = qT2[st], kT2[st]
                ctx = {}

                def stage_a1(j, off, n):
                    vk = j % 2
                    S.op("act", lambda e: e.activation(out=vp[vk][:n, :], in_=vh2[st][j][:n, :], func=AF.Copy,
                                                       scale=ee[:n, j, h:h + 1]),
                         [B(f"vh{st}_{j}"), B(f"ee{j}")], [B(f"vp{vk}")])
                    pS, pSb = bank()
                    S.op("pe", lambda e: [e.matmul(
                        pS[:n, :n], lhsT=kT[:, a, off:off + n], rhs=qT[:, a, off:off + n], start=(a == 0),
                        stop=(a == 1)) for a in range(2)][-1], [B(f"kT{st}"), B(f"qT{st}")], [pSb])
                    mk = j % 2
                    S.op("dve", lambda e: e.tensor_tensor(
                        out=smT[mk][:n, :n], in0=pS[:n, :n], in1=tri[:n, :n], op=ALU.mult),
                        [pSb, B("prm")], [B(f"smT{mk}")])
                    yield

                def stage_a(j, off, n):
                    pd = prev_dec[h]
                    vk = mk = j % 2
                    kx = kh2[st]
                    ups = []
                    for half in range(2):
                        pt, pb = bank()
                        S.op("pe", lambda e, half=half, pt=pt: e.matmul(
                            pt[:, :], lhsT=kx[j][:n, half * 128:(half + 1) * 128], rhs=vp[vk][:n, :], start=True,
                            stop=True), [B(f"kh{st}_{j}"), B(f"vp{vk}")], [pb])
                        ups.append((pt, pb))
                    sn = B("ps7")
                    S.op("pe", lambda e: [e.matmul(small2[:, 8 + half:9 + half],
                                                   lhsT=kx[j][:n, half * 128:(half + 1) * 128],
                                                   rhs=eeb[:n, j, h:h + 1], start=True, stop=True)
                                          for half in range(2)][-1], [B(f"kh{st}_{j}"), B(f"eeb{j}")], [sn])
                    S.op("act", lambda e, pd=pd: e.activation(
                        out=Sbf[:].rearrange("p a v -> p (a v)"), in_=Tst[:, h, :, :].rearrange("p a v -> p (a v)"),
                        func=AF.Copy, scale=dec[:, pd, h:h + 1]), [B(f"T{h}"), B(f"dec{pd}")], [B("Sbf")])
                    S.op("dve", lambda e, pd=pd: e.tensor_scalar(
                        out=nbf[:, 2 * h:2 * h + 2], in0=Tn[:, 2 * h:2 * h + 2], scalar1=dec[:, pd, h:h + 1],
                        scalar2=None, op0=ALU.mult), [B(f"Tn{h}"), B(f"dec{pd}")], [B("nbf")])
                    yield
                    pN, pNb = bank()

                    def num_mm(e):
                        e.matmul(pN[:n, :], lhsT=smT[mk][:n, :n], rhs=vp[vk][:n, :], start=True, stop=False)
                        e.matmul(pN[:n, :], lhsT=qT[:, 0, off:off + n], rhs=Sbf[:, 0, :], start=False, stop=False)
                        return e.matmul(pN[:n, :], lhsT=qT[:, 1, off:off + n], rhs=Sbf[:, 1, :], start=False,
                                        stop=True)
                    S.op("pe", num_mm, [B(f"smT{mk}"), B(f"vp{vk}"), B(f"qT{st}"), B("Sbf")], [pNb])
                    sd = B("ps7")
                    dc = j % 2

                    def den_mm(e):
                        e.matmul(small2[:n, dc:dc + 1], lhsT=smT[mk][:n, :n], rhs=eeb[:n, j, h:h + 1], start=True,
                                 stop=False)
                        e.matmul(small2[:n, dc:dc + 1], lhsT=qT[:, 0, off:off + n], rhs=nbf[:, 2 * h:2 * h + 1],
                                 start=False, stop=False)
                        return e.matmul(small2[:n, dc:dc + 1], lhsT=qT[:, 1, off:off + n],
                                        rhs=nbf[:, 2 * h + 1:2 * h + 2], start=False, stop=True)
                    S.op("pe", den_mm, [B(f"smT{mk}"), B(f"eeb{j}"), B(f"qT{st}"), B("nbf")], [sd])
                    state_commit(j, h, ups, sn, pd, 5 * (ti % 2))
                    S.op("act", lambda e: e.activation(
                        out=bst[:n, j:j + 1], in_=small2[:n, dc:dc + 1], func=AF.Abs,
                        scale=av[:n, j, h:h + 1]), [sd, B(f"av{j}")], [B(f"bst_d{j}")])
                    S.op("act", lambda e: (e.activation(out=junk[:n, :DV], in_=pN[:n, :], func=AF.Square,
                                                        accum_out=bst[:n, 8 + j:9 + j]), e.drain())[-1],
                         [pNb], [B("junk"), B(f"bst_q{j}")])
                    S.op("act", lambda e: e.activation(out=numS[j][:n, :], in_=pN[:n, :], func=AF.Copy),
                         [pNb], [B(f"numS{j}")])
                    yield

                def stage_b_all():
                    subs_ = tile["subs"]
                    nc_ = len(subs_)
                    rd = [B(f"bst_d{j}") for j in range(nc_)] + [B(f"bst_q{j}") for j in range(nc_)]
                    bb = B("bst")
                    avh = av[:, 0:nc_, h]
                    S.op("dve", lambda e: e.tensor_scalar(out=bst[:, 16:16 + nc_], in0=bst[:, 0:nc_], scalar1=1.0,
                                                          scalar2=None, op0=ALU.max), rd, [bb])
                    S.op("dve", lambda e: e.reciprocal(out=bst[:, 16:16 + nc_], in_=bst[:, 16:16 + nc_]), [bb], [bb])
                    S.op("dve", lambda e: e.tensor_tensor(out=bst[:, 24:24 + nc_], in0=bst[:, 16:16 + nc_], in1=avh,
                                                          op=ALU.mult),
                         [bb] + [B(f"av{j}") for j in range(nc_)], [bb])
                    S.op("dve", lambda e: e.tensor_tensor(out=bst[:, 32:32 + nc_], in0=bst[:, 8:8 + nc_],
                                                          in1=bst[:, 24:24 + nc_], op=ALU.mult), [bb] + rd, [bb])
                    S.op("dve", lambda e: e.tensor_tensor(out=bst[:, 32:32 + nc_], in0=bst[:, 32:32 + nc_],
                                                          in1=bst[:, 24:24 + nc_], op=ALU.mult), [bb], [bb])
                    S.op("act", lambda e: e.activation(out=bst[:, 40:40 + nc_], in_=bst[:, 32:32 + nc_], func=AF.Sqrt,
                                                       scale=1.0 / DV, bias=EPS), [bb], [bb])
                    S.op("dve", lambda e: e.reciprocal(out=bst[:, 40:40 + nc_], in_=bst[:, 40:40 + nc_]), [bb], [bb])
                    S.op("dve", lambda e: e.tensor_tensor(out=bst[:, 48:48 + nc_], in0=bst[:, 40:40 + nc_],
                                                          in1=bst[:, 24:24 + nc_], op=ALU.mult), [bb], [bb])

                    def gate_stt(j, off, n):
                        hk = j % 2
                        S.op("dve", lambda e: e.scalar_tensor_tensor(
                            out=hg[hk][:n, :], in0=numS[j][:n, :], scalar=bst[:n, 48 + j:49 + j],
                            in1=wgh2[st][j][:n, :], op0=ALU.mult, op1=ALU.mult),
                            [B(f"numS{j}"), bb, B(f"wgh{st}_{j}")], [B(f"hg{hk}")])

                    def tr_evac(j, off, n):
                        hk = j % 2
                        pT, pTb = bank()
                        pTv = pT[:].bitcast(BF16).rearrange("p (c t) -> p c t", c=8)
                        S.op("pe", lambda e: [e.transpose(
                            out=pTv[:, c, :n], in_=hg[hk][:n, c * 128:(c + 1) * 128], identity=idb[:n, :n])
                            for c in range(4)][-1], [B(f"hg{hk}"), B("idb")], [pTb])
                        S.op("act", lambda e: e.activation(
                            out=gT[:, 4 * h:4 * h + 4, off:off + n], in_=pTv[:, 0:4, :n], func=AF.Copy),
                            [pTb], [B(f"gTh{h}_{j}")])
                    gate_stt(0, *subs_[0])
                    gate_stt(1, *subs_[1])
                    yield
                    for j in range(nc_):
                        tr_evac(j, *subs_[j])
                        if j + 2 < nc_:
                            gate_stt(j + 2, *subs_[j + 2])
                        yield

                subs = tile["subs"]
                yield from stage_a1(0, *subs[0])
                for j, (off, n) in enumerate(subs):
                    if j + 1 < len(subs):
                        yield from stage_a1(j + 1, *subs[j + 1])
                    yield from stage_a(j, off, n)
                yield from stage_b_all()

            for it in P_items(0, 0):
                it()
            for h in range(H):
                st = h % 2
                filler = P_items(h + 1, 1 - st) if h + 1 < H else []
                fi = 0
                nsub_ = len(tile["subs"])
                n_a = 3 * nsub_
                RES = min(5, len(filler))
                per = -(-(len(filler) - RES) // n_a) if filler else 0
                yi = 0
                for _ in S_gen(h, st):
                    yi += 1
                    if yi <= n_a:
                        lim, k = len(filler) - RES, per
                    elif yi == n_a + 1:
                        lim, k = len(filler), RES
                    else:
                        lim, k = len(filler), per
                    for _k in range(k):
                        if fi < lim:
                            filler[fi]()
                            fi += 1
                while fi < len(filler):
                    filler[fi]()
                    fi += 1
            opb = {}

            def op_first(j, off, n):
                for half in range(2):
                    pt, pb = bank()
                    opb[(j, half)] = (pt, pb)
                    S.op("pe", lambda e, half=half, pt=pt: [e.matmul(
                        pt[:n, :], lhsT=gT[:, ec, off:off + n], rhs=wout[:, ec, half * 512:(half + 1) * 512],
                        start=(ec == 0), stop=False) for ec in range(12)][-1],
                        [B(f"gTh{hh}_{j}") for hh in range(3)] + [B("wout")], [pb])

            def op_second(j, off, n):
                for half in range(2):
                    pt, pb = opb.pop((j, half))
                    S.op("pe", lambda e, half=half, pt=pt: [e.matmul(
                        pt[:n, :], lhsT=gT[:, ec, off:off + n], rhs=wout[:, ec, half * 512:(half + 1) * 512],
                        start=False, stop=(ec == 15)) for ec in range(12, 16)][-1],
                        [B(f"gTh3_{j}"), B("wout")], [pb])
                    S.op("dve", lambda e, half=half, pt=pt: e.tensor_tensor(
                        out=xt[j][:n, half * 512:(half + 1) * 512], in0=pt[:n, :],
                        in1=xt[j][:n, half * 512:(half + 1) * 512], op=ALU.add), [pb, B(f"xt{j}")], [B(f"xt{j}")])
                final_norm(j, off, n)

            def final_norm(j, off, n):
                if ti == 0 and j == 0:
                    return
                fs = B("fstat")
                S.op("act", lambda e, j=j, n=n: (e.activation(out=junk[:n, :], in_=xt[j][:n, :], func=AF.Square,
                                                               accum_out=sstat[:n, 8:9]), e.drain())[-1],
                     [B(f"xt{j}")], [B("junk"), fs])
                S.op("act", lambda e, n=n: e.activation(out=sstat[:n, 9:10], in_=sstat[:n, 8:9], func=AF.Sqrt,
                                                         scale=1.0 / D, bias=EPS), [fs], [fs])
                S.op("dve", lambda e, n=n: e.reciprocal(out=sstat[:n, 10:11], in_=sstat[:n, 9:10]), [fs], [fs])
                S.op("dve", lambda e, j=j, n=n: e.scalar_tensor_tensor(
                    out=yout[:n, :], in0=xt[j][:n, :], scalar=sstat[:n, 10:11], in1=fnw[:n, :], op0=ALU.mult,
                    op1=ALU.mult), [B(f"xt{j}"), fs, B("prm")], [B("yout")])
                o0 = t0 + off - NPRE
                dma("sp", out_d[o0:o0 + n, :], yout[:n, :], "yout", [B("yout")], [B(f"outd_{ti}_{j}")])
            psubs = tile["subs"]
            for j, (off, n) in enumerate(psubs):
                op_first(j, off, n)
                if j >= 1:
                    op_second(j - 1, *psubs[j - 1])
            op_second(len(psubs) - 1, *psubs[-1])

    fin = [b for k, b in bufs.items() if k.startswith("outd_") or k.startswith("h1d_") or k == "st_d"]
    S.op("sp", None, fin, [])

    dma_sems = {k: es.enter_context(nc.semaphore(f"d_{k}")) for k in sorted(dma_keys)}
    with nc.Block() as block:
        S.emit(block, eng_sems, dma_sems)
    es.close()
    return nc


_PROG = {}


def _prog(mode):
    if mode not in _PROG:
        _PROG[mode] = build_program(mode)
    return _PROG[mode]


def _core_inputs(x, meta_tokens):
    xs = []
    for c in range(8):
        b, half = divmod(c, 2)
        if half == 0:
            xs.append(np.ascontiguousarray(np.concatenate([meta_tokens, x[b, :4096]], axis=0)))
        else:
            xs.append(np.ascontiguousarray(x[b, 4096 - NPRE:]))
    return xs


def kernel(x, meta_tokens, norm_w, conv_in_w, conv_w, conv_out_w, mlstm_in_w, mlstm_gate_b,
           mlstm_head_norm_w, mlstm_out_w, final_norm_w):
    x = np.asarray(x, np.float32)
    f = lambda a: np.asarray(a, np.float32)
    wcin, wcout, wmin, wg, wmout = _layout_weights(f(conv_in_w), f(conv_out_w), f(mlstm_in_w), f(mlstm_out_w))
    xs = _core_inputs(x, f(meta_tokens))
    prm = [_layout_params(f(norm_w), f(conv_w), f(mlstm_gate_b), f(mlstm_head_norm_w), f(final_norm_w),
                          1.0 if c % 2 == 0 else 0.0, 0.0 if c % 2 == 0 else 1.0) for c in range(8)]
    cores = list(range(8))
    r2 = run_bass_kernel_spmd(_prog("fused"), [dict(params=prm[c], xin=xs[c], wcin=wcin, wcout=wcout, wmin=wmin,
                                                    wg=wg, wmout=wmout) for c in cores], core_ids=cores).results
    out = np.empty((4, 8192, D), np.float32)
    for c in cores:
        b, half = divmod(c, 2)
        out[b, half * 4096:(half + 1) * 4096] = r2[c]["out"]
    return out
```
